# Optimizing a Trainium2 kernel written in Bass

```python
import math
import jax, jax.numpy as jnp
from jax import lax
import numpy as np

D_MODEL = 1024
BATCH = 8
SEQ = 4096
DEPTH = 4

HEAD_DIM = 64
EPS = 1e-6
Q_BLOCK = 128
NEG_INF = -1e30

N_RWKV_HEADS = 4
RWKV_W = N_RWKV_HEADS * HEAD_DIM
RWKV_DECAY_RANK = 32
RWKV_A_RANK = 32
RWKV_GATE_RANK = 64
RWKV_GN_EPS = 64e-5

N_FOX_HEADS = 4
FOX_W = N_FOX_HEADS * HEAD_DIM

SSM_GROUP_CH = 16
N_SSM_GROUPS = 16
SSM_W = N_SSM_GROUPS * SSM_GROUP_CH
SSM_STATE = 64
SSM_DT_MIN = 1e-3
SSM_DT_MAX = 1e-1

N_MLA_HEADS = 4
MLA_NOPE_DIM = 64
MLA_ROPE_DIM = 32
MLA_QK_DIM = MLA_NOPE_DIM + MLA_ROPE_DIM
MLA_V_DIM = 64
MLA_W = N_MLA_HEADS * MLA_V_DIM
MLA_Q_RANK = 192
MLA_KV_RANK = 128
ROPE_BASE = 10000.0
POS_OFFSET_MAX = 4096

D_FF = 4 * D_MODEL

RWKV_COLS = (RWKV_W, RWKV_W, RWKV_W, RWKV_DECAY_RANK, RWKV_A_RANK, RWKV_GATE_RANK)
FOX_COLS = (FOX_W, FOX_W, FOX_W, N_FOX_HEADS)
SSM_COLS = (SSM_W,)
MLA_COLS = (MLA_Q_RANK, MLA_KV_RANK, MLA_ROPE_DIM)
GROUP_COLS = (sum(RWKV_COLS), sum(FOX_COLS), sum(SSM_COLS), sum(MLA_COLS))
IN_COLS = sum(GROUP_COLS)

kernel_name = "hybrid_parallel_heads_rwkv7_fox_s5_mla"


def _split(z, sizes):
    return jnp.split(z, np.cumsum(sizes)[:-1].tolist(), axis=-1)


def rmsnorm(x, g, eps=EPS):
    xf = x.astype(jnp.float32)
    y = xf * lax.rsqrt(jnp.mean(xf * xf, axis=-1, keepdims=True) + eps)
    return (y * g.astype(jnp.float32)).astype(x.dtype)


def _token_shift(z):
    return jnp.pad(z, ((0, 0), (1, 0), (0, 0)))[:, :-1]


def blocked_causal_attention(q, k, v, log_f_cum=None):
    B, H, T, dk = q.shape
    nb = T // Q_BLOCK
    scale = dk ** -0.5
    q_blocks = jnp.moveaxis(q.reshape(B, H, nb, Q_BLOCK, dk), 2, 0)
    k_pos = jnp.arange(T)

    def attend(i, q_i, c_i):
        s = jnp.einsum('bhqd,bhkd->bhqk', q_i, k, preferred_element_type=jnp.float32) * scale
        if c_i is not None:
            s = s + c_i[..., :, None] - log_f_cum[..., None, :]
        q_pos = i * Q_BLOCK + jnp.arange(Q_BLOCK)
        s = jnp.where(k_pos[None, :] <= q_pos[:, None], s, NEG_INF)
        p = jax.nn.softmax(s, axis=-1)
        return jnp.einsum('bhqk,bhkd->bhqd', p.astype(v.dtype), v)

    idx = jnp.arange(nb)
    if log_f_cum is None:
        out = lax.map(lambda a: attend(a[0], a[1], None), (idx, q_blocks))
    else:
        c_blocks = jnp.moveaxis(log_f_cum.reshape(B, H, nb, Q_BLOCK), 2, 0)
        out = lax.map(lambda a: attend(a[0], a[1], a[2]), (idx, q_blocks, c_blocks))
    return jnp.moveaxis(out, 0, 2).reshape(B, H, T, -1)


def rwkv7_mixer(z, mu, w0, w_up, a0, a_up, g_up, k_k, k_a, r_k, ln_g, ln_b):
    f32 = jnp.float32
    B, T, _ = z.shape
    H, N = N_RWKV_HEADS, HEAD_DIM
    z = z + (_token_shift(z) - z) * mu
    r, k, v, wd, ad, gd = _split(z, RWKV_COLS)
    w = -jax.nn.softplus(-(w0 + jnp.tanh(wd) @ w_up)) - 0.5
    decay = jnp.exp(-jnp.exp(w.astype(f32)))
    a = jax.nn.sigmoid(a0 + ad @ a_up)
    g = jax.nn.sigmoid(gd) @ g_up
    kk = (k * k_k).reshape(B, T, H, N).astype(f32)
    kk = kk / jnp.maximum(jnp.sqrt(jnp.sum(kk * kk, axis=-1, keepdims=True)), 1e-12)
    k = k * (1.0 + (a - 1.0) * k_a)

    def heads(t):
        return t.reshape(B, T, H, N).astype(f32)

    rh, kh, vh, ah, wh = heads(r), heads(k), heads(v), heads(a), heads(decay)

    def step(S, inp):
        r_t, w_t, k_t, v_t, kk_t, a_t = inp
        sa = jnp.einsum('bhij,bhj->bhi', S, kk_t)
        S = (S * w_t[:, :, None, :] - sa[..., None] * (kk_t * a_t)[:, :, None, :]
             + v_t[..., None] * k_t[:, :, None, :])
        return S, jnp.einsum('bhij,bhj->bhi', S, r_t)

    xs = tuple(jnp.moveaxis(t, 1, 0) for t in (rh, wh, kh, vh, kk, ah))
    _, y = lax.scan(step, jnp.zeros((B, H, N, N), f32), xs)
    y = jnp.moveaxis(y, 0, 1)
    mean = jnp.mean(y, axis=-1, keepdims=True)
    var = jnp.mean(jnp.square(y - mean), axis=-1, keepdims=True)
    y = ((y - mean) * lax.rsqrt(var + RWKV_GN_EPS)).reshape(B, T, RWKV_W)
    y = y * ln_g.astype(f32) + ln_b.astype(f32)
    bonus = jnp.sum(rh * kh * r_k.astype(f32), axis=-1, keepdims=True) * vh
    y = (y + bonus.reshape(B, T, RWKV_W)) * g.astype(f32)
    return y.astype(z.dtype)


def fox_mixer(z, f_b, q_g, k_g):
    B, T, _ = z.shape
    H = N_FOX_HEADS
    q, k, v, fz = _split(z, FOX_COLS)
    q = rmsnorm(q.reshape(B, T, H, HEAD_DIM), q_g).transpose(0, 2, 1, 3)
    k = rmsnorm(k.reshape(B, T, H, HEAD_DIM), k_g).transpose(0, 2, 1, 3)
    v = v.reshape(B, T, H, HEAD_DIM).transpose(0, 2, 1, 3)
    log_f = jax.nn.log_sigmoid(fz.astype(jnp.float32) + f_b.astype(jnp.float32))
    c = jnp.cumsum(log_f, axis=1).transpose(0, 2, 1)
    out = blocked_causal_attention(q, k, v, c)
    return out.transpose(0, 2, 1, 3).reshape(B, T, FOX_W)


def _complex_affine_combine(e1, e2):
    a1r, a1i, b1r, b1i = e1
    a2r, a2i, b2r, b2i = e2
    return (a2r * a1r - a2i * a1i, a2r * a1i + a2i * a1r,
            a2r * b1r - a2i * b1i + b2r, a2r * b1i + a2i * b1r + b2i)


def s5_mixer(u, lam_re, lam_im, log_dt, b_re, b_im, c_re, c_im, d, glu_w):
    f32 = jnp.float32
    B, T, _ = u.shape
    G, P = N_SSM_GROUPS, SSM_GROUP_CH
    uf = u.astype(f32).reshape(B, T, G, P)
    lr = jnp.minimum(lam_re.astype(f32), -1e-4)
    li = lam_im.astype(f32)
    dt = jnp.exp(log_dt.astype(f32))[:, None]
    mag = jnp.exp(lr * dt)
    ab_re, ab_im = mag * jnp.cos(li * dt), mag * jnp.sin(li * dt)
    den = lr * lr + li * li
    fac_re = ((ab_re - 1.0) * lr + ab_im * li) / den
    fac_im = (ab_im * lr - (ab_re - 1.0) * li) / den
    br, bi = b_re.astype(f32), b_im.astype(f32)
    bb_re = fac_re[..., None] * br - fac_im[..., None] * bi
    bb_im = fac_re[..., None] * bi + fac_im[..., None] * br
    bu_re = jnp.einsum('btgp,gnp->btgn', uf, bb_re)
    bu_im = jnp.einsum('btgp,gnp->btgn', uf, bb_im)
    a_re = jnp.broadcast_to(ab_re, bu_re.shape)
    a_im = jnp.broadcast_to(ab_im, bu_im.shape)
    _, _, x_re, x_im = lax.associative_scan(_complex_affine_combine, (a_re, a_im, bu_re, bu_im), axis=1)
    y = (jnp.einsum('btgn,gpn->btgp', x_re, c_re.astype(f32))
         - jnp.einsum('btgn,gpn->btgp', x_im, c_im.astype(f32)))
    y = y + d.astype(f32).reshape(G, P) * uf
    y = jax.nn.gelu(y)
    y = y * jax.nn.sigmoid(jnp.einsum('btgp,gpq->btgq', y, glu_w.astype(f32)))
    return y.reshape(B, T, SSM_W).astype(u.dtype)


def _rope_tables(positions):
    inv_freq = ROPE_BASE ** (-jnp.arange(0, MLA_ROPE_DIM, 2, dtype=jnp.float32) / MLA_ROPE_DIM)
    ang = positions.astype(jnp.float32)[..., None] * inv_freq
    return jnp.cos(ang)[:, :, None, :], jnp.sin(ang)[:, :, None, :]


def _apply_rope(x, cos, sin):
    xf = x.astype(jnp.float32)
    x1, x2 = jnp.split(xf, 2, axis=-1)
    return jnp.concatenate([x1 * cos - x2 * sin, x1 * sin + x2 * cos], axis=-1).astype(x.dtype)


def mla_mixer(z, positions, q_latent_g, w_q_up, kv_latent_g, w_kv_up, q_g, k_g):
    B, T, _ = z.shape
    H = N_MLA_HEADS
    cq, ckv, k_rope = _split(z, MLA_COLS)
    q = (rmsnorm(cq, q_latent_g) @ w_q_up).reshape(B, T, H, MLA_QK_DIM)
    kv = (rmsnorm(ckv, kv_latent_g) @ w_kv_up).reshape(B, T, H, MLA_NOPE_DIM + MLA_V_DIM)
    k_nope, v = jnp.split(kv, [MLA_NOPE_DIM], axis=-1)
    k = jnp.concatenate([k_nope, jnp.broadcast_to(k_rope[:, :, None, :], (B, T, H, MLA_ROPE_DIM))], axis=-1)
    q = rmsnorm(q, q_g)
    k = rmsnorm(k, k_g)
    cos, sin = _rope_tables(positions)
    q = jnp.concatenate([q[..., :MLA_NOPE_DIM], _apply_rope(q[..., MLA_NOPE_DIM:], cos, sin)], axis=-1)
    k = jnp.concatenate([k[..., :MLA_NOPE_DIM], _apply_rope(k[..., MLA_NOPE_DIM:], cos, sin)], axis=-1)
    out = blocked_causal_attention(q.transpose(0, 2, 1, 3), k.transpose(0, 2, 1, 3), v.transpose(0, 2, 1, 3))
    return out.transpose(0, 2, 1, 3).reshape(B, T, MLA_W)


def setup_inputs(seed: int = 0) -> dict:
    key = jax.random.key(seed)
    ks = iter(jax.random.split(key, 64))
    L = DEPTH

    def nrm(shape, scale):
        return scale * jax.random.normal(next(ks), shape, jnp.float32)

    def gain(shape):
        return 1.0 + nrm(shape, 0.02)

    def unif(shape, lo, hi):
        return jax.random.uniform(next(ks), shape, jnp.float32, lo, hi)

    G, P, N = N_SSM_GROUPS, SSM_GROUP_CH, SSM_STATE
    x = nrm((BATCH, SEQ, D_MODEL), 1.0)
    start = jax.random.randint(next(ks), (BATCH, 1), 0, POS_OFFSET_MAX, dtype=jnp.int32)
    positions = start + jnp.arange(SEQ, dtype=jnp.int32)[None, :]
    n = jnp.arange(N, dtype=jnp.float32)
    return {
        "x": x,
        "positions": positions,
        "mix_norm_g": gain((L, D_MODEL)),
        "w_in": nrm((L, D_MODEL, IN_COLS), D_MODEL ** -0.5),
        "rwkv_mu": unif((L, GROUP_COLS[0]), 0.0, 1.0),
        "rwkv_w0": unif((L, RWKV_W), -5.0, 0.0),
        "rwkv_w_up": nrm((L, RWKV_DECAY_RANK, RWKV_W), 0.1),
        "rwkv_a0": nrm((L, RWKV_W), 0.1),
        "rwkv_a_up": nrm((L, RWKV_A_RANK, RWKV_W), 0.1),
        "rwkv_g_up": nrm((L, RWKV_GATE_RANK, RWKV_W), RWKV_GATE_RANK ** -0.5),
        "rwkv_k_k": 0.85 + nrm((L, RWKV_W), 0.02),
        "rwkv_k_a": 1.0 + nrm((L, RWKV_W), 0.02),
        "rwkv_r_k": nrm((L, N_RWKV_HEADS, HEAD_DIM), 0.1),
        "rwkv_ln_g": gain((L, RWKV_W)),
        "rwkv_ln_b": nrm((L, RWKV_W), 0.02),
        "fox_f_b": 3.0 + nrm((L, N_FOX_HEADS), 0.1),
        "fox_q_g": gain((L, HEAD_DIM)),
        "fox_k_g": gain((L, HEAD_DIM)),
        "ssm_lambda_re": -0.5 + nrm((L, G, N), 0.01),
        "ssm_lambda_im": math.pi * n + nrm((L, G, N), 0.01),
        "ssm_log_dt": unif((L, G), math.log(SSM_DT_MIN), math.log(SSM_DT_MAX)),
        "ssm_b_re": nrm((L, G, N, P), 0.5),
        "ssm_b_im": nrm((L, G, N, P), 0.5),
        "ssm_c_re": nrm((L, G, P, N), N ** -0.5),
        "ssm_c_im": nrm((L, G, P, N), N ** -0.5),
        "ssm_d": nrm((L, SSM_W), 0.5),
        "ssm_glu_w": nrm((L, G, P, P), P ** -0.5),
        "mla_q_latent_g": gain((L, MLA_Q_RANK)),
        "mla_w_q_up": nrm((L, MLA_Q_RANK, N_MLA_HEADS * MLA_QK_DIM), MLA_Q_RANK ** -0.5),
        "mla_kv_latent_g": gain((L, MLA_KV_RANK)),
        "mla_w_kv_up": nrm((L, MLA_KV_RANK, N_MLA_HEADS * (MLA_NOPE_DIM + MLA_V_DIM)), MLA_KV_RANK ** -0.5),
        "mla_q_g": gain((L, MLA_QK_DIM)),
        "mla_k_g": gain((L, MLA_QK_DIM)),
        "out_norm_g": gain((L, 3, FOX_W)),
        "w_out": nrm((L, D_MODEL, D_MODEL), D_MODEL ** -0.5),
        "mlp_norm_g": gain((L, D_MODEL)),
        "w_ff1": nrm((L, D_MODEL, D_FF), D_MODEL ** -0.5),
        "w_ff2": nrm((L, D_FF, D_MODEL), D_FF ** -0.5),
    }


def reference(x, positions, mix_norm_g, w_in, rwkv_mu, rwkv_w0, rwkv_w_up, rwkv_a0, rwkv_a_up,
              rwkv_g_up, rwkv_k_k, rwkv_k_a, rwkv_r_k, rwkv_ln_g, rwkv_ln_b, fox_f_b, fox_q_g, fox_k_g,
              ssm_lambda_re, ssm_lambda_im, ssm_log_dt, ssm_b_re, ssm_b_im, ssm_c_re, ssm_c_im, ssm_d,
              ssm_glu_w, mla_q_latent_g, mla_w_q_up, mla_kv_latent_g, mla_w_kv_up, mla_q_g, mla_k_g,
              out_norm_g, w_out, mlp_norm_g, w_ff1, w_ff2):
    B, T, _ = x.shape
    for l in range(DEPTH):
        h = rmsnorm(x, mix_norm_g[l])
        z = h @ w_in[l]
        z_rwkv, z_fox, z_ssm, z_mla = _split(z, GROUP_COLS)
        y_rwkv = rwkv7_mixer(z_rwkv, rwkv_mu[l], rwkv_w0[l], rwkv_w_up[l], rwkv_a0[l], rwkv_a_up[l],
                             rwkv_g_up[l], rwkv_k_k[l], rwkv_k_a[l], rwkv_r_k[l], rwkv_ln_g[l], rwkv_ln_b[l])
        y_fox = fox_mixer(z_fox, fox_f_b[l], fox_q_g[l], fox_k_g[l])
        y_ssm = s5_mixer(z_ssm, ssm_lambda_re[l], ssm_lambda_im[l], ssm_log_dt[l], ssm_b_re[l], ssm_b_im[l],
                         ssm_c_re[l], ssm_c_im[l], ssm_d[l], ssm_glu_w[l])
        y_mla = mla_mixer(z_mla, positions, mla_q_latent_g[l], mla_w_q_up[l], mla_kv_latent_g[l],
                          mla_w_kv_up[l], mla_q_g[l], mla_k_g[l])
        others = rmsnorm(jnp.stack([y_fox, y_ssm, y_mla], axis=2), out_norm_g[l])
        mixed = jnp.concatenate([y_rwkv, others.reshape(B, T, -1)], axis=-1)
        x = x + mixed @ w_out[l]
        h = rmsnorm(x, mlp_norm_g[l])
        x = x + jnp.square(jax.nn.relu(h @ w_ff1[l])) @ w_ff2[l]
    return x
```

```python
import math
import contextlib
import numpy as np
import concourse.bass as bass
import concourse.mybir as mybir
from concourse.bass_utils import run_bass_kernel_spmd

F32 = mybir.dt.float32
BF16 = mybir.dt.bfloat16
I32 = mybir.dt.int32
AF = mybir.ActivationFunctionType
ALU = mybir.AluOpType

D = 1024
NZC = 19
ZW = NZC * 128
EPS = 1e-6
C0 = math.exp(-0.5)
NEG = -30000.0
RWKV_CD = F32
S5_CD = F32


class View:
    __slots__ = ("b", "ap")

    def __init__(self, b, ap):
        self.b = b
        self.ap = ap

    def __getitem__(self, idx):
        return View(self.b, self.ap[idx])

    def rearrange(self, *a, **k):
        return View(self.b, self.ap.rearrange(*a, **k))

    def bitcast(self, dt):
        return View(self.b, self.ap.bitcast(dt))


class Buf:
    __slots__ = ("ap", "w", "r", "name")

    def __init__(self, ap, name=""):
        self.ap = ap
        self.w = {}
        self.r = {}
        self.name = name

    def __getitem__(self, idx):
        return View(self, self.ap[idx])

    def v(self):
        return View(self, self.ap)


def _ap(x):
    return x.ap if isinstance(x, View) else x


def _bufs(*xs):
    out = []
    for x in xs:
        if isinstance(x, View):
            out.append(x.b)
        elif isinstance(x, Buf):
            out.append(x)
    return out


class Prog:
    ENGS = ("pe", "dve", "act", "pool", "sp")
    NSLOT = 14

    def __init__(self, nc):
        self.nc = nc
        self.q = {e: [] for e in self.ENGS}
        self.cnt = {e: 0 for e in self.ENGS}
        self.waited = {e: {} for e in self.ENGS}
        self.slot_val = {}
        self.slot_next = {"sp": 0, "pool": 0}
        self.final = []
        self.psb = []
        self.psi = 0

    def _need(self, eng, key, val, waits):
        if key == eng and eng == "pe":
            return
        if self.waited[eng].get(key, 0) >= val:
            return
        self.waited[eng][key] = val
        waits.append((key, val))

    def _deps(self, eng, reads, writes):
        waits = []
        for b in reads:
            for k, v in b.w.items():
                self._need(eng, k, v, waits)
        for b in writes:
            for k, v in b.w.items():
                self._need(eng, k, v, waits)
            for k, v in b.r.items():
                self._need(eng, k, v, waits)
        return waits

    def _mark(self, tok, reads, writes):
        k, v = tok
        for b in reads:
            if b.r.get(k, 0) < v:
                b.r[k] = v
        for b in writes:
            b.w = {k: v}
            b.r = {}

    def op(self, eng, fn, reads=(), writes=()):
        waits = self._deps(eng, reads, writes)
        self.cnt[eng] += 1
        tok = (eng, self.cnt[eng])
        self.q[eng].append((waits, fn, tok))
        self._mark(tok, reads, writes)

    def dma(self, queue, fn, reads=(), writes=(), final=False):
        s = self.slot_next[queue]
        self.slot_next[queue] = (s + 1) % self.NSLOT
        key = ("d", queue, s)
        prev = self.slot_val.get(key, 0)
        waits = self._deps(queue, reads, writes)
        if prev > 0:
            self._need(queue, key, prev, waits)
        val = prev + 16
        self.slot_val[key] = val
        tok = (key, val)
        self.q[queue].append((waits, fn, tok))
        self._mark(tok, reads, writes)
        if final:
            self.final.append(tok)

    def barrier(self):
        for e in self.ENGS:
            waits = []
            for f in ("pe", "dve", "act", "pool"):
                if f != e and self.cnt[f] > 0:
                    self._need(e, f, self.cnt[f], waits)
            for key, val in self.slot_val.items():
                self._need(e, key, val, waits)
            if waits:
                self.q[e].append((waits, None, None))

    def psum(self):
        b = self.psb[self.psi]
        self.psi = (self.psi + 1) % 6
        return b

    def psum_acc(self):
        self.pai = 1 - getattr(self, "pai", 0)
        return self.psb[6 + self.pai]

    def mm(self, out, lhsT, rhs, start=True, stop=True):
        o, l, r = _ap(out), _ap(lhsT), _ap(rhs)
        self.op("pe", lambda e: e.matmul(o, lhsT=l, rhs=r, start=start, stop=stop),
                reads=_bufs(lhsT, rhs), writes=_bufs(out))

    def act(self, out, in_, func, bias=0.0, scale=1.0):
        o, i, b = _ap(out), _ap(in_), _ap(bias)
        self.op("act", lambda e: e.activation(out=o, in_=i, func=func, bias=b, scale=scale),
                reads=_bufs(in_, bias), writes=_bufs(out))

    def tt(self, eng, out, in0, in1, op):
        o, a, b = _ap(out), _ap(in0), _ap(in1)
        self.op(eng, lambda e: e.tensor_tensor(out=o, in0=a, in1=b, op=op),
                reads=_bufs(in0, in1), writes=_bufs(out))

    def ts(self, eng, out, in0, s1, op0, s2=None, op1=None):
        o, a, x1, x2 = _ap(out), _ap(in0), _ap(s1), _ap(s2)
        if op1 is None:
            self.op(eng, lambda e: e.tensor_scalar(out=o, in0=a, scalar1=x1, scalar2=None, op0=op0),
                    reads=_bufs(in0, s1), writes=_bufs(out))
        else:
            self.op(eng, lambda e: e.tensor_scalar(out=o, in0=a, scalar1=x1, scalar2=x2, op0=op0, op1=op1),
                    reads=_bufs(in0, s1, s2), writes=_bufs(out))

    def stt(self, out, in0, scalar, in1, op0, op1):
        o, a, s, b = _ap(out), _ap(in0), _ap(scalar), _ap(in1)
        self.op("dve", lambda e: e.scalar_tensor_tensor(out=o, in0=a, scalar=s, in1=b, op0=op0, op1=op1),
                reads=_bufs(in0, scalar, in1), writes=_bufs(out))

    def copy(self, eng, out, in_):
        o, i = _ap(out), _ap(in_)
        if eng == "act":
            self.op("act", lambda e: e.activation(out=o, in_=i, func=AF.Copy), reads=_bufs(in_), writes=_bufs(out))
        else:
            self.op(eng, lambda e: e.tensor_copy(out=o, in_=i), reads=_bufs(in_), writes=_bufs(out))

    def scan(self, out, d0, d1, init, op0=ALU.mult, op1=ALU.add):
        o, a, b, i = _ap(out), _ap(d0), _ap(d1), _ap(init)
        self.op("dve", lambda e: e.tensor_tensor_scan(out=o, data0=a, data1=b, initial=i, op0=op0, op1=op1),
                reads=_bufs(d0, d1, init), writes=_bufs(out))

    def recip(self, out, in_):
        o, i = _ap(out), _ap(in_)
        self.op("dve", lambda e: e.reciprocal(out=o, in_=i), reads=_bufs(in_), writes=_bufs(out))

    def memset(self, eng, out, val):
        o = _ap(out)
        self.op(eng, lambda e: e.memset(o, val), writes=_bufs(out))

    def load(self, queue, out, src_ap, src_bufs=()):
        o = _ap(out)
        self.dma(queue, lambda e: e.dma_start(out=o, in_=src_ap), reads=list(src_bufs), writes=_bufs(out))

    def store(self, queue, dst_ap, in_, dst_bufs=(), final=False):
        i = _ap(in_)
        self.dma(queue, lambda e: e.dma_start(out=dst_ap, in_=i), reads=_bufs(in_), writes=list(dst_bufs), final=final)

    def emit(self):
        nc = self.nc
        with contextlib.ExitStack() as st:
            sems = {}
            for e in ("pe", "dve", "act", "pool"):
                sems[e] = st.enter_context(nc.semaphore("s_" + e))
            for key in self.slot_val:
                sems[key] = st.enter_context(nc.semaphore("d_%s_%d" % (key[1], key[2])))
            block = st.enter_context(nc.Block())
            endw = [(e, self.cnt[e]) for e in ("pe", "dve", "act", "pool") if self.cnt[e] > 0]
            endw += list(self.slot_val.items())

            def mk(ename):
                def body(eng):
                    for waits, fn, tok in self.q[ename]:
                        for k, v in waits:
                            eng.wait_ge(sems[k], v)
                        if fn is None:
                            continue
                        ins = fn(eng)
                        k, v = tok
                        ins.then_inc(sems[k], 16 if isinstance(k, tuple) else 1)
                    if ename == "sp":
                        for k, v in endw:
                            eng.wait_ge(sems[k], v)
                return body

            block.tensor(mk("pe"))
            block.vector(mk("dve"))
            block.scalar(mk("act"))
            block.gpsimd(mk("pool"))
            block.sync(mk("sp"))


class Arena:
    def __init__(self, buf_ap, words):
        self.ap = buf_ap
        self.words = words
        self.off = 0

    def mark(self):
        return self.off

    def release(self, m):
        self.off = m

    def alloc(self, shape, dt=F32, name=""):
        n = 1
        for s in shape[1:]:
            n *= s
        w = n if dt != BF16 else (n + 1) // 2
        w = (w + 7) // 8 * 8
        assert self.off + w <= self.words, ("arena overflow", name, self.off, w, self.words)
        ap = self.ap[:, self.off:self.off + w]
        self.off += w
        if dt != F32:
            ap = ap.bitcast(dt)
        ap = ap[:, 0:n]
        if len(shape) == 3:
            ap = ap.rearrange("p (a b) -> p a b", a=shape[1])
        elif len(shape) == 4:
            ap = ap.rearrange("p (a b c) -> p a b c", a=shape[1], b=shape[2])
        if shape[0] < 128:
            ap = ap[0:shape[0]]
        return Buf(ap, name)


COLS = {}
_o = 0
for _n, _c in [("mixg", 8), ("mlpg", 8), ("mu", 7), ("w0", 2), ("a0", 2), ("kk", 2), ("ka", 2), ("omka", 2),
               ("lng", 2), ("lnb", 2), ("rk", 2), ("foxqg", 1), ("foxkg", 1), ("foxfb", 1), ("ssmd", 2),
               ("og_fox", 4), ("og_ssm", 2), ("og_mla", 4), ("mqlg", 2), ("mkvlg", 1), ("mqg", 1), ("mkg", 1),
               ("slre", 8), ("slim", 8), ("sldt", 8)]:
    COLS[_n] = _o
    _o += _c
NC_COLS = _o


def build(T, NL, dbg=False):
    NB = T // 512
    NCH = T // 128
    nc = bass.Bass("TRN2", target_bir_lowering=False)
    P = Prog(nc)

    def din(name, shape, dt=F32):
        return nc.dram_tensor(name, shape, dt, kind="ExternalInput").ap()

    xT_in = din("xT", [D, T])
    pos_in = din("pos", [1, T], I32)
    w_in_d = din("w_in_p", [NL, D, ZW])
    w_out_d = din("w_out", [NL, D, D])
    w_ff1_d = din("w_ff1", [NL, D, 4 * D])
    w_ff2_d = din("w_ff2", [NL, 4 * D, D])
    cols_d = din("cols", [NL, 128, NC_COLS])
    lr_d = din("lr_w", [NL, 128, 256])
    ssmB_d = din("ssmB", [NL, 128, 16, 128])
    ssmC_d = din("ssmC", [NL, 128, 16, 128])
    glu_d = din("glu_bd", [NL, 128, 2, 128])
    wq_d = din("mla_wq", [NL, 128, 2, 384])
    wkvK_d = din("mla_wkvK", [NL, 128, 4, 96])
    wkvV_d = din("mla_wkvV", [NL, 128, 256])
    cst_d = din("cst", [128, 2048])
    amask_d = din("amask", [128, 4, 512])
    place_d = din("place", [4, 4, 128])
    auxm_d = din("auxm", [128, 2, 8])
    outT = nc.dram_tensor("outT", [D, T], F32, kind="ExternalOutput").ap()

    def dscr(name, shape):
        return nc.dram_tensor(name, shape, F32, kind="Internal").ap()

    zT_d = dscr("zT", [ZW, T])
    mixT_d = dscr("mixT", [D, T])
    xmid_d = dscr("xmid", [D, T])
    xres_d = dscr("xres", [D, T])
    cos_d = dscr("cosT", [96, T])
    sin_d = dscr("sinT", [96, T])
    dbg_out = {}
    if dbg:
        dbg_out["zT_o"] = nc.dram_tensor("zT_o", [ZW, T], F32, kind="ExternalOutput").ap()
        dbg_out["mixT_o"] = nc.dram_tensor("mixT_o", [D, T], F32, kind="ExternalOutput").ap()
        dbg_out["xmid_o"] = nc.dram_tensor("xmid_o", [D, T], F32, kind="ExternalOutput").ap()

    def regions(nchunk):
        return [[Buf(None, "r") for _ in range(NB)] for _ in range(nchunk)]

    zR = regions(NZC)
    mixR = regions(8)
    xmidR = regions(8)
    xresR = regions(8)
    outR = regions(8)

    st = contextlib.ExitStack()
    with st:
        AW = 52000
        arena_t = st.enter_context(nc.sbuf_tensor("arena", [128, AW], F32))
        A = Arena(arena_t[:, :], AW)
        for i in range(8):
            P.psb.append(Buf(st.enter_context(nc.psum_tensor("ps%d" % i, [128, 512], F32))[:, :], "ps%d" % i))

        cst = A.alloc([128, 2048], F32, "cst")
        P.load("sp", cst.v(), cst_d[:, :])
        ident = cst[:, 0:128]
        onesblk = cst[:, 128:256]
        ones = cst[:, 256:384]
        tri = cst[:, 384:576]
        scanmask = cst[:, 576:1088]
        iota = cst[:, 1088:1600]
        rotP = cst[0:96, 1600:1696]
        selkr = cst[0:64, 1696:1792]
        invfreq = cst[0:96, 1792:1793]
        onesrow = cst[:, 1800:1928]
        ones_bf = A.alloc([128, 128], BF16, "ones_bf")
        ident_bf = A.alloc([128, 128], BF16, "ident_bf")
        rotP_bf = A.alloc([96, 96], BF16, "rotP_bf")
        selkr_bf = A.alloc([64, 96], BF16, "selkr_bf")
        P.copy("dve", ones_bf.v(), ones)
        P.copy("dve", ident_bf.v(), ident)
        P.copy("dve", rotP_bf.v(), rotP)
        P.copy("dve", selkr_bf.v(), selkr)
        amask = A.alloc([128, 4, 512], BF16, "amask")
        P.load("pool", amask.v(), amask_d[:, :, :])
        place = A.alloc([4, 4, 128], F32, "place")
        P.load("sp", place.v(), place_d[:, :, :])
        auxm = A.alloc([128, 2, 8], F32, "auxm")
        P.load("sp", auxm.v(), auxm_d[:, :, :])
        ones512 = A.alloc([128, 512], F32, "ones512")
        P.memset("pool", ones512.v(), 1.0)
        cols = A.alloc([128, NC_COLS], F32, "cols")
        m_layer = A.mark()
        ropeR = [[Buf(None, "rope") for _ in range(NB)] for _ in range(2)]
        posi = A.alloc([96, 512], I32, "posi")
        ang, red, kf_, COSb, SINb = [A.alloc([96, 512], F32, n_) for n_ in ("ang", "red", "kf", "COSb", "SINb")]
        ki_ = A.alloc([96, 512], I32, "ki")
        for blk in range(NB):
            ts_ = slice(blk * 512, (blk + 1) * 512)
            P.load("sp", posi.v(), pos_in[0:1, ts_].broadcast_to([96, 512]))
            P.copy("dve", ang.v(), posi.v())
            P.ts("dve", ang.v(), ang.v(), invfreq, ALU.mult)
            range_reduce(P, red.v(), ang.v(), ki_.v(), kf_.v())
            P.act(SINb.v(), red.v(), AF.Sin)
            P.ts("dve", ang.v(), ang.v(), math.pi / 2, ALU.add)
            range_reduce(P, red.v(), ang.v(), ki_.v(), kf_.v())
            P.act(COSb.v(), red.v(), AF.Sin)
            P.store("sp", cos_d[:, ts_], COSb.v(), [ropeR[0][blk]])
            P.store("sp", sin_d[:, ts_], SINb.v(), [ropeR[1][blk]])

        def col(name, i=0, p0=0, p1=128):
            c = COLS[name] + i
            return cols[p0:p1, c:c + 1]

        def rsqrt_into(out, ps_in, scale, bias):
            P.act(out, ps_in, AF.Sqrt, bias=bias, scale=scale)
            P.recip(out, out)

        evac_rr = [0]

        def evac(out, in_):
            evac_rr[0] ^= 1
            P.copy("act" if evac_rr[0] else "dve", out, in_)

        def rmsnorm_block(xsrcR, src_ap, gname, blk, xt, xn, sq, rstd):
            t0 = blk * 512
            P.load("sp", xt, src_ap[:, t0:t0 + 512].rearrange("(c p) n -> p c n", p=128),
                   [xsrcR[c][blk] for c in range(8)])
            P.act(sq, xt, AF.Square)
            ps = P.psum()
            for c in range(8):
                P.mm(ps.v(), ones_bf.v(), sq[:, c, :], start=(c == 0), stop=(c == 7))
            rsqrt_into(rstd.v(), ps.v(), 1.0 / D, EPS)
            for c in range(8):
                P.stt(xn[:, c, :], xt[:, c, :], col(gname, c), rstd.v(), ALU.mult, ALU.mult)

        for l in range(NL):
            A.release(m_layer)
            P.barrier()
            P.load("sp", cols.v(), cols_d[l, :, :])
            P.ts("dve", cols[:, COLS["omka"]:COLS["omka"] + 2], cols[:, COLS["ka"]:COLS["ka"] + 2], -1.0, ALU.mult,
                 1.0, ALU.add)
            xsrc_ap, xsrcR = (xT_in, [[Buf(None) for _ in range(NB)] for _ in range(8)]) if l == 0 else (xres_d, xresR)
            Vfox = A.alloc([128, NCH, 4, 65], BF16, "Vfox")
            m_phase = A.mark()

            Wb = A.alloc([128, 8, ZW], BF16, "Win")
            wst = [A.alloc([128, ZW], F32, "wst%d" % i) for i in range(2)]
            for c in range(8):
                P.load("sp", wst[c % 2].v(), w_in_d[l, c * 128:(c + 1) * 128, :])
                evac(Wb[:, c, :], wst[c % 2].v())
            P.memset("pool", Vfox[:, :, :, 64:65], 1.0)
            xts = [A.alloc([128, 8, 512], F32, "xt%d" % i) for i in range(2)]
            xns = [A.alloc([128, 8, 512], BF16, "xn%d" % i) for i in range(2)]
            sqs = [A.alloc([128, 8, 512], BF16, "sq%d" % i) for i in range(2)]
            rstds = [A.alloc([128, 512], F32, "rstd%d" % i) for i in range(2)]
            zst = [A.alloc([128, 512], F32, "zst%d" % i) for i in range(4)]
            zi = 0
            FOXV = 11 * 128
            def normA(b_):
                rmsnorm_block(xsrcR, xsrc_ap, "mixg", b_, xts[b_ % 2].v(), xns[b_ % 2], sqs[b_ % 2].v(), rstds[b_ % 2])

            normA(0)
            for blk in range(NB):
                t0 = blk * 512
                xt, xn, sq, rstd = xts[blk % 2], xns[blk % 2], sqs[blk % 2], rstds[blk % 2]
                if blk + 1 < NB:
                    normA(blk + 1)
                for oc in range(NZC):
                    ps = P.psum()
                    for c in range(8):
                        P.mm(ps.v(), Wb[:, c, oc * 128:(oc + 1) * 128], xn[:, c, :], start=(c == 0), stop=(c == 7))
                    zs_ = zst[zi % 4]
                    zi += 1
                    evac(zs_.v(), ps.v())
                    P.store("sp", zT_d[oc * 128:(oc + 1) * 128, t0:t0 + 512], zs_.v(), [zR[oc][blk]])
                for tt in range(4):
                    ps = P.psum()
                    for c in range(8):
                        P.mm(ps[:, 0:256], xn[:, c, tt * 128:(tt + 1) * 128], Wb[:, c, FOXV:FOXV + 256],
                             start=(c == 0), stop=(c == 7))
                    evac(Vfox[:, blk * 4 + tt, :, 0:64], ps[:, 0:256].rearrange("p (h d) -> p h d", h=4))
            if dbg and l == 0:
                P.barrier()
                tmp = A.alloc([128, 512], F32, "dbgtmp")
                for oc in range(NZC):
                    for blk in range(NB):
                        P.load("sp", tmp.v(), zT_d[oc * 128:(oc + 1) * 128, blk * 512:(blk + 1) * 512], [zR[oc][blk]])
                        P.store("sp", dbg_out["zT_o"][oc * 128:(oc + 1) * 128, blk * 512:(blk + 1) * 512], tmp.v(), final=True)
            P.barrier()
            A.release(m_phase)

            phase_rwkv(P, A, l, NB, cols, col, lr_d, zT_d, zR, mixT_d, mixR, ident, ident_bf, onesblk, tri, scanmask, rsqrt_into, evac)
            P.barrier()
            A.release(m_phase)

            phase_fox(P, A, l, NB, T, col, zT_d, zR, mixT_d, mixR, Vfox, onesblk, ones, ident_bf, amask, place, auxm,
                      ones512, rsqrt_into, evac)
            P.barrier()
            A.release(m_layer)
            m_phase = A.mark()

            phase_s5(P, A, l, NB, col, cols, zT_d, zR, mixT_d, mixR, ssmB_d, ssmC_d, glu_d, ones, iota, rsqrt_into, evac)
            P.barrier()
            A.release(m_phase)

            phase_mla(P, A, l, NB, T, col, zT_d, zR, mixT_d, mixR, wq_d, wkvK_d, wkvV_d, (cos_d, sin_d, ropeR), ones,
                      ident_bf, amask, rotP_bf, selkr_bf, invfreq, rsqrt_into, evac)
            P.barrier()
            A.release(m_phase)

            if dbg and l == 0:
                tmp = A.alloc([128, 512], F32, "dbgtmp")
                for oc in range(8):
                    for blk in range(NB):
                        P.load("sp", tmp.v(), mixT_d[oc * 128:(oc + 1) * 128, blk * 512:(blk + 1) * 512], [mixR[oc][blk]])
                        P.store("sp", dbg_out["mixT_o"][oc * 128:(oc + 1) * 128, blk * 512:(blk + 1) * 512], tmp.v(), final=True)
                P.barrier()
                A.release(m_phase)

            Wo = A.alloc([128, 8, D], BF16, "Wout")
            wst = [A.alloc([128, D], F32, "wsto%d" % i) for i in range(2)]
            for c in range(8):
                P.load("sp", wst[c % 2].v(), w_out_d[l, c * 128:(c + 1) * 128, :])
                evac(Wo[:, c, :], wst[c % 2].v())
            mixb = [A.alloc([128, 8, 512], BF16, "mixb%d" % i) for i in range(2)]
            xts = [A.alloc([128, 8, 512], F32, "xt%d" % i) for i in range(2)]
            zst = [A.alloc([128, 512], F32, "zst%d" % i) for i in range(4)]
            zi = 0
            def loadF(b_):
                tb_ = b_ * 512
                P.load("pool", mixb[b_ % 2].v(), mixT_d[:, tb_:tb_ + 512].rearrange("(c p) n -> p c n", p=128),
                       [mixR[c][b_] for c in range(8)])
                P.load("sp", xts[b_ % 2].v(), xsrc_ap[:, tb_:tb_ + 512].rearrange("(c p) n -> p c n", p=128),
                       [xsrcR[c][b_] for c in range(8)])

            loadF(0)
            for blk in range(NB):
                t0 = blk * 512
                mb, xt = mixb[blk % 2], xts[blk % 2]
                if blk + 1 < NB:
                    loadF(blk + 1)
                for oc in range(8):
                    ps = P.psum()
                    for c in range(8):
                        P.mm(ps.v(), Wo[:, c, oc * 128:(oc + 1) * 128], mb[:, c, :], start=(c == 0), stop=(c == 7))
                    zs_ = zst[zi % 4]
                    zi += 1
                    P.tt("dve", zs_.v(), ps.v(), xt[:, oc, :], ALU.add)
                    P.store("sp", xmid_d[oc * 128:(oc + 1) * 128, t0:t0 + 512], zs_.v(), [xmidR[oc][blk]])
                    if dbg and l == 0:
                        P.store("sp", dbg_out["xmid_o"][oc * 128:(oc + 1) * 128, t0:t0 + 512], zs_.v(), final=True)
            P.barrier()
            A.release(m_phase)

            W1 = A.alloc([128, 8, 4 * D], BF16, "W1")
            W2 = A.alloc([128, 32, D], BF16, "W2")
            m_st = A.mark()
            wst = [A.alloc([128, 4 * D], F32, "wstg%d" % i) for i in range(2)]
            for c in range(8):
                P.load("sp", wst[c % 2].v(), w_ff1_d[l, c * 128:(c + 1) * 128, :])
                evac(W1[:, c, :], wst[c % 2].v())
            for c4 in range(8):
                P.load("sp", wst[c4 % 2].v().rearrange("p (c n) -> p c n", c=4),
                       w_ff2_d[l, c4 * 512:(c4 + 1) * 512, :].rearrange("(c p) n -> p c n", p=128))
                evac(W2[:, c4 * 4:(c4 + 1) * 4, :], wst[c4 % 2].v().rearrange("p (c n) -> p c n", c=4))
            P.barrier()
            A.release(m_st)
            xn = A.alloc([128, 8, 512], BF16, "xn")
            rstd = A.alloc([128, 512], F32, "rstd")
            h1 = A.alloc([128, 32 * 512], BF16, "h1")
            xt = h1[:, 0:16 * 512].bitcast(F32).rearrange("p (a b) -> p a b", a=8)
            sq = h1[:, 16 * 512:24 * 512].rearrange("p (a b) -> p a b", a=8)
            h1 = h1.v().rearrange("p (a b) -> p a b", a=32)
            xr = [A.alloc([128, 512], F32, "xr%d" % i) for i in range(2)]
            rl = [A.alloc([128, 512], F32, "rl%d" % i) for i in range(2)]
            zst = [A.alloc([128, 512], F32, "zst%d" % i) for i in range(2)]
            last = (l == NL - 1)
            dst_ap, dstR = (outT, outR) if last else (xres_d, xresR)
            for blk in range(NB):
                t0 = blk * 512
                rmsnorm_block(xmidR, xmid_d, "mlpg", blk, xt, xn, sq, rstd)
                for f in range(32):
                    ps = P.psum()
                    for c in range(8):
                        P.mm(ps.v(), W1[:, c, f * 128:(f + 1) * 128], xn[:, c, :], start=(c == 0), stop=(c == 7))
                    r_ = rl[f % 2]
                    P.act(r_.v(), ps.v(), AF.Relu)
                    P.tt("pool" if f % 2 else "dve", h1[:, f, :], r_.v(), r_.v(), ALU.mult)
                for oc in range(8):
                    ps = P.psum()
                    for f in range(32):
                        P.mm(ps.v(), W2[:, f, oc * 128:(oc + 1) * 128], h1[:, f, :], start=(f == 0), stop=(f == 31))
                    zs_ = zst[oc % 2]
                    xr_ = xr[oc % 2]
                    P.load("sp", xr_.v(), xmid_d[oc * 128:(oc + 1) * 128, t0:t0 + 512], [xmidR[oc][blk]])
                    P.tt("dve", zs_.v(), ps.v(), xr_.v(), ALU.add)
                    P.store("sp", dst_ap[oc * 128:(oc + 1) * 128, t0:t0 + 512], zs_.v(), [dstR[oc][blk]], final=last)
            P.barrier()
        P.emit()
    return nc


def phase_rwkv(P, A, l, NB, cols, col, lr_d, zT_d, zR, mixT_d, mixR, ident, ident_bf, onesblk, tri, scanmask, rsqrt_into, evac):
    LR = A.alloc([128, 256], F32, "LR")
    P.load("sp", LR.v(), lr_d[l, :, :])
    zbs = [A.alloc([128, 7, 513], F32, "zb%d" % i) for i in range(2)]
    dtmp = A.alloc([128, 7, 512], F32, "dtmp")
    zs = A.alloc([128, 7, 512], F32, "zs")
    lowr = A.alloc([128, 512], F32, "lowr")

    def t2(name):
        return A.alloc([128, 2, 512], F32, name)

    sig, a_, g_, kk, sqk, rn, t1, kmod, Lc, Ep, Em, Eq, yT, yc, tmpb = [
        t2(n) for n in ("sig", "a", "g", "kk", "sqk", "rn", "t1", "kmod", "Lc", "Ep", "Em", "Eq",
                        "yT", "yc", "tmpb")]
    b_, Lp = yc, tmpb
    CD = RWKV_CD
    identx = ident_bf if CD == BF16 else ident
    bett = A.alloc([128, 2, 512], CD, "bett")
    ktil = A.alloc([128, 2, 512], CD, "ktil")
    vb = A.alloc([128, 2, 512], CD, "vb")
    KR = A.alloc([128, 2, 8, 128], CD, "KR")
    ST = [A.alloc([128, 64], F32, "ST%d" % i) for i in range(2)]
    STb = [A.alloc([128, 64], CD, "STb%d" % i) for i in range(2)]
    for s_ in ST + STb:
        P.memset("pool", s_.v(), 0.0)
    NR = 4
    ABR = [A.alloc([128, 64], CD, "ABR%d" % i) for i in range(NR)]
    AK = [A.alloc([128, 128], CD, "AK%d" % i) for i in range(NR)]
    TK = [A.alloc([128, 192], CD, "TK%d" % i) for i in range(NR)]
    DG = [A.alloc([128, 128], CD, "DG%d" % i) for i in range(NR)]
    nWT = [A.alloc([128, 64], CD, "nWT%d" % i) for i in range(NR)]
    UT = [A.alloc([128, 64], CD, "UT%d" % i) for i in range(NR)]
    Qm = [A.alloc([128, 128], CD, "Qm%d" % i) for i in range(NR)]
    PW = [[A.alloc([128, 128], CD, "PW%d_%d" % (i, j)) for j in range(4)] for i in range(NR)]
    for i in range(NR):
        for j in range(4):
            P.memset("pool", PW[i][j].v(), 0.0)
    stage = [A.alloc([128, 512], F32, "stg%d" % i) for i in range(2)]
    it = 0
    for blk in range(NB):
        t0 = blk * 512
        zb = zbs[blk % 2]
        P.load("sp", zb[:, :, 1:513], zT_d[0:896, t0:t0 + 512].rearrange("(c p) n -> p c n", p=128),
               [zR[c][blk] for c in range(7)])
        if blk == 0:
            P.memset("pool", zb[:, :, 0:1], 0.0)
        else:
            P.copy("pool", zb[:, :, 0:1], zbs[(blk - 1) % 2][:, :, 512:513])
        P.tt("dve", dtmp.v(), zb[:, :, 0:512], zb[:, :, 1:513], ALU.subtract)
        for c in range(7):
            P.stt(zs[:, c, :], dtmp[:, c, :], col("mu", c), zb[:, c, 1:513], ALU.mult, ALU.add)
        r_, k_, v_ = zs[:, 0:2, :], zs[:, 2:4, :], zs[:, 4:6, :]
        P.act(lowr[0:32, :], zs[0:32, 6, :], AF.Tanh)
        P.act(lowr[64:128, :], zs[64:128, 6, :], AF.Sigmoid)
        for h in range(2):
            ps = P.psum()
            P.mm(ps.v(), LR[0:32, h * 128:(h + 1) * 128], lowr[0:32, :])
            P.act(sig[:, h, :], ps.v(), AF.Sigmoid, bias=col("w0", h))
            ps = P.psum()
            P.mm(ps.v(), LR[32:64, h * 128:(h + 1) * 128], zs[32:64, 6, :])
            P.act(a_[:, h, :], ps.v(), AF.Sigmoid, bias=col("a0", h))
            ps = P.psum()
            P.mm(ps.v(), LR[64:128, h * 128:(h + 1) * 128], lowr[64:128, :])
            P.copy("act", g_[:, h, :], ps.v())
            P.ts("dve", kk[:, h, :], zs[:, 2 + h, :], col("kk", h), ALU.mult)
        P.act(sqk.v(), kk.v(), AF.Square)
        for h in range(2):
            ps = P.psum()
            P.mm(ps.v(), onesblk, sqk[:, h, :])
            P.act(rn[:, h, :], ps.v(), AF.Sqrt)
        P.ts("dve", rn.v(), rn.v(), 1e-12, ALU.max)
        P.recip(rn.v(), rn.v())
        P.tt("dve", kk.v(), kk.v(), rn.v(), ALU.mult)
        P.tt("pool", b_.v(), kk.v(), a_.v(), ALU.mult)
        for h in range(2):
            P.ts("dve", t1[:, h, :], a_[:, h, :], col("ka", h), ALU.mult, col("omka", h), ALU.add)
        P.tt("pool", kmod.v(), k_, t1.v(), ALU.mult)
        for h in range(2):
            P.scan(Lc[:, h, :], scanmask, sig[:, h, :], 0.0)
        P.tt("pool", Lp.v(), Lc.v(), sig.v(), ALU.subtract)
        P.act(Ep.v(), Lc.v(), AF.Exp, scale=-C0)
        P.act(Em.v(), Lc.v(), AF.Exp, scale=C0)
        P.act(Eq.v(), Lp.v(), AF.Exp, scale=-C0)
        ch = "p h (c s) -> p h c s"
        P.tt("dve", KR[:, :, :, 0:64], kk.v().rearrange(ch, s=64), Eq.v().rearrange(ch, s=64), ALU.mult)
        P.tt("dve", KR[:, :, :, 64:128], r_.rearrange(ch, s=64), Ep.v().rearrange(ch, s=64), ALU.mult)
        P.tt("pool", bett.v(), b_.v(), Em.v(), ALU.mult)
        P.tt("pool", ktil.v(), kmod.v(), Em.v(), ALU.mult)
        P.copy("pool", vb.v(), zs[:, 4:6, :])
        def make_chunk(c):
            cs = slice(c * 64, (c + 1) * 64)
            ctxs = []

            def front():
                nonlocal it
                for hp in range(2):
                    i = it % NR
                    it += 1
                    gam = Ep[:, hp, c * 64 + 63:c * 64 + 64]
                    G1, G2, G3, G4 = P.psum(), P.psum(), P.psum(), P.psum()
                    for h2 in range(2):
                        pb = slice(64 * h2, 64 * h2 + 64)
                        P.mm(G1[pb, 0:128], bett[pb, hp, cs], KR[pb, hp, c, :])
                        P.mm(G2[pb, 0:128], ktil[pb, hp, cs], KR[pb, hp, c, :])
                        P.mm(G3[pb, 0:64], KR[pb, hp, c, 0:64], bett[pb, hp, cs])
                    P.ts("dve", DG[i].v(), ident, gam, ALU.mult)
                    for h2 in range(2):
                        pb = slice(64 * h2, 64 * h2 + 64)
                        P.mm(G4[pb, 0:64], vb[pb, hp, cs], identx[pb, pb])
                        P.mm(G4[pb, 64:128], bett[pb, hp, cs], DG[i][pb, pb])
                        P.mm(G4[pb, 128:192], ktil[pb, hp, cs], DG[i][pb, pb])
                    cur, curT, nxt, nxtT = PW[i]
                    for h2 in range(2):
                        pb = slice(64 * h2, 64 * h2 + 64)
                        P.tt("dve", cur[pb, pb], G1[pb, 0:64], tri[pb, 0:64], ALU.mult)
                        P.tt("dve", curT[pb, pb], G3[pb, 0:64], tri[pb, 128:192], ALU.mult)
                    P.tt("dve", Qm[i].v(), ident, cur.v(), ALU.subtract)
                    P.tt("dve", ABR[i].v(), G1[:, 64:128], tri[:, 64:128], ALU.mult)
                    P.tt("dve", AK[i].v(), G2[:, 0:128], tri[:, 0:128], ALU.mult)
                    P.copy("act", TK[i].v(), G4[:, 0:192])
                    ctxs.append(dict(i=i, hp=hp, gam=gam, pw=[cur, curT, nxt, nxtT]))

            def chain(k):
                def f():
                    for cx in ctxs:
                        cur, curT, nxt, nxtT = cx["pw"]
                        if k < 4:
                            pA = P.psum()
                            P.mm(pA[:, 0:128], curT.v(), cur.v())
                            P.copy("act", nxt.v(), pA[:, 0:128])
                        pB = P.psum()
                        P.mm(pB[:, 0:128], cur.v(), curT.v())
                        P.copy("act", nxtT.v(), pB[:, 0:128])
                    for cx in ctxs:
                        cur, curT, nxt, nxtT = cx["pw"]
                        i = cx["i"]
                        pQ = P.psum()
                        P.mm(pQ[:, 0:128], nxtT.v(), Qm[i].v())
                        P.tt("dve", Qm[i].v(), Qm[i].v(), pQ[:, 0:128], ALU.add)
                        cx["pw"] = [nxt, nxtT, cur, curT]
                return f

            def s1():
                for cx in ctxs:
                    i, hp = cx["i"], cx["hp"]
                    WT = P.psum()
                    for h2 in range(2):
                        pb = slice(64 * h2, 64 * h2 + 64)
                        P.mm(WT[pb, 0:64], KR[pb, hp, c, 0:64], STb[hp][pb, :], start=True, stop=False)
                        P.mm(WT[pb, 0:64], AK[i][pb, 0:64], TK[i][pb, 0:64], start=False, stop=True)
                    P.ts("dve", nWT[i].v(), WT[:, 0:64], -1.0, ALU.mult)

            def s2():
                for cx in ctxs:
                    i, hp = cx["i"], cx["hp"]
                    UTp = P.psum()
                    P.mm(UTp[:, 0:64], Qm[i].v(), nWT[i].v())
                    P.copy("act", UT[i].v(), UTp[:, 0:64])

            def s3():
                for cx in ctxs:
                    i, hp, gam = cx["i"], cx["hp"], cx["gam"]
                    Yp, SN = P.psum(), P.psum()
                    for h2 in range(2):
                        pb = slice(64 * h2, 64 * h2 + 64)
                        P.mm(SN[pb, 0:64], TK[i][pb, 64:128], UT[i][pb, :], start=True, stop=False)
                        P.mm(SN[pb, 0:64], TK[i][pb, 128:192], TK[i][pb, 0:64], start=False, stop=True)
                        P.mm(Yp[pb, 0:64], STb[hp][pb, :], KR[pb, hp, c, 64:128], start=True, stop=False)
                        P.mm(Yp[pb, 0:64], UT[i][pb, :], ABR[i][pb, :], start=False, stop=False)
                        P.mm(Yp[pb, 0:64], TK[i][pb, 0:64], AK[i][pb, 64:128], start=False, stop=True)
                    P.stt(ST[hp].v(), ST[hp].v(), gam, SN[:, 0:64], ALU.mult, ALU.add)
                    P.copy("pool", STb[hp].v(), ST[hp].v())
                    P.copy("act", yT[:, hp, cs], Yp[:, 0:64])

            return [front] + [chain(k) for k in range(5)], [s1, s2, s3]

        chunks = [make_chunk(c) for c in range(8)]
        for f in chunks[0][0]:
            f()
        for c in range(8):
            Aq = list(chunks[c + 1][0]) if c + 1 < 8 else []
            Bq = list(chunks[c][1])
            order = ["A", "B", "A", "A", "B", "A", "A", "B", "A"]
            for o in order:
                if o == "A" and Aq:
                    Aq.pop(0)()
                elif o == "B" and Bq:
                    Bq.pop(0)()
            for f in Aq + Bq:
                f()
        for hp in range(2):
            ps = P.psum()
            P.mm(ps.v(), onesblk, yT[:, hp, :])
            P.stt(yc[:, hp, :], ps.v(), -1.0 / 64, yT[:, hp, :], ALU.mult, ALU.add)
        P.act(sqk.v(), yc.v(), AF.Square)
        for hp in range(2):
            ps = P.psum()
            P.mm(ps.v(), onesblk, sqk[:, hp, :])
            rsqrt_into(rn[:, hp, :], ps.v(), 1.0 / 64, 64e-5)
        P.tt("dve", yc.v(), yc.v(), rn.v(), ALU.mult)
        P.tt("pool", tmpb.v(), r_, kmod.v(), ALU.mult)
        for hp in range(2):
            P.ts("dve", yc[:, hp, :], yc[:, hp, :], col("lng", hp), ALU.mult, col("lnb", hp), ALU.add)
            P.ts("pool", tmpb[:, hp, :], tmpb[:, hp, :], col("rk", hp), ALU.mult)
            ps = P.psum()
            P.mm(ps.v(), onesblk, tmpb[:, hp, :])
            P.tt("dve", t1[:, hp, :], ps.v(), zs[:, 4 + hp, :], ALU.mult)
        P.tt("dve", yc.v(), yc.v(), t1.v(), ALU.add)
        for hp in range(2):
            sg_ = stage[hp]
            P.tt("dve", sg_.v(), yc[:, hp, :], g_[:, hp, :], ALU.mult)
            P.store("sp", mixT_d[hp * 128:(hp + 1) * 128, t0:t0 + 512], sg_.v(), [mixR[hp][blk]])


def attention(P, A, NB, Kt, Qt, krows, Vt, ident_bf, amask, ones, ycol, og_name, col, mix_row0, mixT_d, mixR,
              rsqrt_into, evac, hooks=None):
    PT = [A.alloc([128, 512], BF16, "PT%d" % i) for i in range(4)]
    Osb = [A.alloc([65, 512], F32, "Osb%d" % i) for i in range(2)]
    YF = A.alloc([64, 4, 512], F32, "YF")
    sqy = A.alloc([64, 4, 512], F32, "sqy")
    rst = A.alloc([64, 512], F32, "rst")
    stg = [A.alloc([64, 512], F32, "astg%d" % i) for i in range(2)]
    items = [(qb, h, kc) for qb in range(NB) for h in range(4) for kc in range(4 * (qb + 1))]
    LOOK = 2
    state = {"pi": 0, "O": None}

    def issue(item):
        qb, h, kc = item
        qs = slice(qb * 512, (qb + 1) * 512)
        S = P.psum()
        diag = kc >= 4 * qb
        P.mm(S.v(), Kt[kc // 4][0:krows, h, (kc % 4) * 128:(kc % 4 + 1) * 128], Qt[qb][0:krows, h, :],
             start=True, stop=not diag)
        if diag:
            P.mm(S.v(), ident_bf.v(), amask[:, kc - 4 * qb, :], start=False, stop=True)
        return S

    def finish(item, S):
        qb, h, kc = item
        qs = slice(qb * 512, (qb + 1) * 512)
        nk = 4 * (qb + 1)
        if kc == 0:
            state["O"] = P.psum_acc()
        O = state["O"]
        pt = PT[state["pi"] % 4]
        state["pi"] += 1
        P.act(pt.v(), S.v(), AF.Exp)
        P.mm(O[0:65, :], Vt[kc // 4][:, kc % 4, h, :], pt.v(), start=(kc == 0), stop=(kc == nk - 1))
        if kc != nk - 1:
            return
        ob = Osb[h % 2]
        P.copy("act", ob.v(), O[0:65, :])
        P.recip(ob[64:65, :], ob[64:65, :])
        bc = P.psum()
        P.mm(bc[0:64, :], ones[64:65, 0:64], ob[64:65, :])
        P.tt("dve", YF[:, h, :], ob[0:64, :], bc[0:64, :], ALU.mult)
        if h != 3:
            return
        P.act(sqy.v(), YF.v(), AF.Square)
        ps = P.psum()
        for hh in range(4):
            P.mm(ps[0:64, :], ones[0:64, 0:64], sqy[:, hh, :], start=(hh == 0), stop=(hh == 3))
        rsqrt_into(rst.v(), ps[0:64, :], 1.0 / 256, EPS)
        for hh in range(4):
            sg_ = stg[hh % 2]
            P.stt(sg_.v(), YF[:, hh, :], col(og_name, hh, 0, 64), rst.v(), ALU.mult, ALU.mult)
            r0 = mix_row0 + 64 * hh
            P.store("sp", mixT_d[r0:r0 + 64, qs], sg_.v(), [mixR[r0 // 128][qb]])

    pend = []
    for item in items:
        if hooks is not None and item[2] == 0 and hooks.get((item[0], item[1])):
            while pend:
                finish(*pend.pop(0))
            for f_ in hooks[(item[0], item[1])]:
                f_()
        pend.append((item, issue(item)))
        if len(pend) > LOOK:
            finish(*pend.pop(0))
    while pend:
        finish(*pend.pop(0))


def phase_fox(P, A, l, NB, T, col, zT_d, zR, mixT_d, mixR, Vfox, onesblk, ones, ident_bf, amask, place, auxm,
              ones512, rsqrt_into, evac):
    Qa = A.alloc([128, 4, T], BF16, "Qp")
    Ka = A.alloc([128, 4, T], BF16, "Kp")
    Qb_ = [Buf(Qa.ap[:, :, b_ * 512:(b_ + 1) * 512], "Qp%d" % b_) for b_ in range(NB)]
    Kb_ = [Buf(Ka.ap[:, :, b_ * 512:(b_ + 1) * 512], "Kp%d" % b_) for b_ in range(NB)]
    Vb_ = [Buf(Vfox.ap[:, b_ * 4:(b_ + 1) * 4, :, :], "Vf%d" % b_) for b_ in range(NB)]
    for b_ in range(NB):
        P.memset("pool", Qb_[b_].v(), 0.0)
        P.memset("pool", Kb_[b_].v(), 0.0)
    zq = A.alloc([128, 4, 512], F32, "zq0")
    fz = A.alloc([4, 512], F32, "fz0")
    sq = A.alloc([128, 4, 512], F32, "fsq")
    rb = A.alloc([128, 4, 512], F32, "frb")
    lf = A.alloc([4, 512], F32, "lf")
    C6 = [[A.alloc([128, 512], F32, "C6_%d_%d" % (h, i)) for i in range(2)] for h in range(2)]
    P6 = [A.alloc([128, 512], F32, "P6s%d" % h) for h in range(2)]
    Hb, Mb, Lb = [[A.alloc([128, 512], BF16, "%s%d" % (n, h)) for h in range(2)] for n in ("Hb", "Mb", "Lb")]
    r1, r2, tq = [[A.alloc([128, 512], F32, "%s%d" % (n, h)) for h in range(2)] for n in ("r1", "r2", "tq")]
    ars = [slice(64 if h % 2 == 0 else 0, (64 if h % 2 == 0 else 0) + 32) for h in range(4)]

    def prep(blk):
        ts_ = slice(blk * 512, (blk + 1) * 512)
        Qp, Kp = Qb_[blk], Kb_[blk]
        z, f_ = zq, fz
        st = []

        def s_load():
            P.load("sp", z.v(), zT_d[7 * 128:11 * 128, ts_].rearrange("(c p) n -> p c n", p=128),
                   [zR[c][blk] for c in range(7, 11)])
            P.load("sp", f_.v(), zT_d[13 * 128:13 * 128 + 4, ts_], [zR[13][blk]])
            P.act(sq.v(), z.v(), AF.Square)
            P.act(lf.v(), f_.v(), AF.Sigmoid, bias=col("foxfb", 0, 0, 4))
            P.act(lf.v(), lf.v(), AF.Ln)
        st.append(s_load)

        def s_norm():
            for c4 in range(4):
                ps = P.psum()
                P.mm(ps.v(), onesblk, sq[:, c4, :])
                if c4 < 2:
                    rsqrt_into(rb[:, c4, :], ps.v(), 1.0, 64 * EPS)
                else:
                    rsqrt_into(rb[:, c4, :], ps.v(), 1.0 / 64, EPS)
        st.append(s_norm)

        def s_place():
            for h in range(4):
                ar = ars[h]
                ps = P.psum()
                P.mm(ps.v(), place[0:4, h, :], lf.v())
                P.copy("act", P6[h // 2][ar, :], ps[ar, :])
            for h in range(4):
                ar = ars[h]
                init = 0.0 if blk == 0 else C6[h // 2][(blk - 1) % 2][ar, 511:512]
                P.scan(C6[h // 2][blk % 2][ar, :], ones512[ar, :], P6[h // 2][ar, :], init)
        st.append(s_place)

        def s_qk(c4):
            def f():
                for h2 in range(2):
                    pb = slice(64 * h2, 64 * h2 + 64)
                    h = 2 * (c4 % 2) + h2
                    dst = (Qp if c4 < 2 else Kp)[pb, h, :]
                    g = col("foxqg" if c4 < 2 else "foxkg", 0, 64 * h2, 64 * h2 + 64)
                    P.stt(dst, z[pb, c4, :], g, rb[pb, c4, :], ALU.mult, ALU.mult)
            return f
        st.append(s_qk(0))
        st.append(s_qk(1))

        def s_split():
            for h in range(4):
                P.copy("act", Hb[h // 2][ars[h], :], C6[h // 2][blk % 2][ars[h], :])
            for h in range(4):
                P.tt("pool", r1[h // 2][ars[h], :], C6[h // 2][blk % 2][ars[h], :], Hb[h // 2][ars[h], :], ALU.subtract)
            for h in range(4):
                P.copy("act", Mb[h // 2][ars[h], :], r1[h // 2][ars[h], :])
            for h in range(4):
                P.tt("pool", r2[h // 2][ars[h], :], r1[h // 2][ars[h], :], Mb[h // 2][ars[h], :], ALU.subtract)
            for h in range(4):
                P.copy("act", Lb[h // 2][ars[h], :], r2[h // 2][ars[h], :])
        st.append(s_split)
        st.append(s_qk(2))
        st.append(s_qk(3))

        def s_aux():
            for h in range(4):
                ar, par = ars[h], h % 2
                P.ts("dve", tq[h // 2][ar, :], Hb[h // 2][ar, :], auxm[ar, par, 0:1], ALU.mult)
                P.stt(tq[h // 2][ar, :], Mb[h // 2][ar, :], auxm[ar, par, 1:2], tq[h // 2][ar, :], ALU.mult, ALU.add)
            for h in range(4):
                ar, par = ars[h], h % 2
                P.stt(tq[h // 2][ar, :], Lb[h // 2][ar, :], auxm[ar, par, 2:3], tq[h // 2][ar, :], ALU.mult, ALU.add)
            for h in range(4):
                ar, par = ars[h], h % 2
                P.ts("dve", Qp[ar, h, :], tq[h // 2][ar, :], auxm[ar, par, 3:4], ALU.mult, auxm[ar, par, 4:5], ALU.add)
                P.ts("pool", Kp[ar, h, :], tq[h // 2][ar, :], auxm[ar, par, 5:6], ALU.mult, auxm[ar, par, 6:7], ALU.add)
        st.append(s_aux)
        return st

    for f_ in prep(0):
        f_()
    hooks = {}
    for qb in range(NB - 1):
        st = prep(qb + 1)
        n_ = len(st)
        for h in range(4):
            hooks[(qb, h)] = st[(h * n_) // 4:((h + 1) * n_) // 4]
    attention(P, A, NB, Kb_, Qb_, 128, Vb_, ident_bf, amask, ones, None, "og_fox", col, 256, mixT_d, mixR,
              rsqrt_into, evac, hooks)


def range_reduce(P, out, ang, ki, kf):
    P.ts("dve", ki, ang, 1.0 / (2 * math.pi), ALU.mult)
    P.copy("dve", kf, ki)
    P.stt(out, kf, -6.28125, ang, ALU.mult, ALU.add)
    P.stt(out, kf, -(2 * math.pi - 6.28125), out, ALU.mult, ALU.add)
    P.ts("dve", kf, out, math.pi, ALU.is_gt)
    P.stt(out, kf, -2 * math.pi, out, ALU.mult, ALU.add)
    P.ts("dve", kf, out, -math.pi, ALU.is_lt)
    P.stt(out, kf, 2 * math.pi, out, ALU.mult, ALU.add)
    P.ts("dve", out, out, math.pi, ALU.min, -math.pi, ALU.max)


def phase_s5(P, A, l, NB, col, cols, zT_d, zR, mixT_d, mixR, ssmB_d, ssmC_d, glu_d, ones, iota, rsqrt_into, evac):
    SB = A.alloc([128, 16, 128], S5_CD, "ssmB")
    SC = A.alloc([128, 16, 128], S5_CD, "ssmC")
    GL = A.alloc([128, 2, 128], S5_CD, "glu")
    P.load("pool", SB.v(), ssmB_d[l, :, :, :])
    P.load("pool", SC.v(), ssmC_d[l, :, :, :])
    P.load("pool", GL.v(), glu_d[l, :, :, :])
    c_re, c_im, c_dt = COLS["slre"], COLS["slim"], COLS["sldt"]

    def s8(name):
        return A.alloc([128, 8], F32, name)

    dt, lr, lrdt, th, rho, sn, cs, abr1, abi, den, fr, fi, t8, u8 = [s8(n) for n in (
        "dt", "lr", "lrdt", "th", "rho", "sn", "cs", "abr1", "abi", "den", "fr", "fi", "t8", "u8")]
    ki8 = A.alloc([128, 8], I32, "ki8")
    kf8 = s8("kf8")
    li = cols[:, c_im:c_im + 8]
    P.act(dt.v(), cols[:, c_dt:c_dt + 8], AF.Exp)
    P.ts("dve", lr.v(), cols[:, c_re:c_re + 8], -1e-4, ALU.min)
    P.tt("dve", lrdt.v(), lr.v(), dt.v(), ALU.mult)
    P.tt("dve", th.v(), li, dt.v(), ALU.mult)
    P.act(rho.v(), lrdt.v(), AF.Exp)
    range_reduce(P, t8.v(), th.v(), ki8.v(), kf8.v())
    P.act(sn.v(), t8.v(), AF.Sin)
    P.ts("dve", u8.v(), th.v(), math.pi / 2, ALU.add)
    range_reduce(P, t8.v(), u8.v(), ki8.v(), kf8.v())
    P.act(cs.v(), t8.v(), AF.Sin)
    P.tt("dve", abr1.v(), rho.v(), cs.v(), ALU.mult)
    P.ts("dve", abr1.v(), abr1.v(), -1.0, ALU.add)
    P.tt("dve", abi.v(), rho.v(), sn.v(), ALU.mult)
    P.tt("dve", den.v(), lr.v(), lr.v(), ALU.mult)
    P.tt("dve", t8.v(), li, li, ALU.mult)
    P.tt("dve", den.v(), den.v(), t8.v(), ALU.add)
    P.recip(den.v(), den.v())
    P.tt("dve", fr.v(), abr1.v(), lr.v(), ALU.mult)
    P.tt("dve", t8.v(), abi.v(), li, ALU.mult)
    P.tt("dve", fr.v(), fr.v(), t8.v(), ALU.add)
    P.tt("dve", fr.v(), fr.v(), den.v(), ALU.mult)
    P.tt("dve", fi.v(), abi.v(), lr.v(), ALU.mult)
    P.tt("dve", t8.v(), abr1.v(), li, ALU.mult)
    P.tt("dve", fi.v(), fi.v(), t8.v(), ALU.subtract)
    P.tt("dve", fi.v(), fi.v(), den.v(), ALU.mult)
    ANG = A.alloc([128, 8, 512], F32, "ANG")
    RED = A.alloc([128, 8, 512], F32, "RED")
    KF = A.alloc([128, 8, 512], F32, "KF")
    SINT = A.alloc([128, 8, 512], F32, "SINT")
    COST = A.alloc([128, 8, 512], F32, "COST")
    m_ki = A.mark()
    KI = A.alloc([128, 8, 512], I32, "KI")
    for i in range(8):
        P.ts("dve", ANG[:, i, :], iota, th[:, i:i + 1], ALU.mult)
    range_reduce(P, RED.v(), ANG.v(), KI.v(), KF.v())
    P.act(SINT.v(), RED.v(), AF.Sin)
    P.ts("dve", ANG.v(), ANG.v(), math.pi / 2, ALU.add)
    range_reduce(P, RED.v(), ANG.v(), KI.v(), KF.v())
    P.act(COST.v(), RED.v(), AF.Sin)
    TinR, TinI, RHO = ANG, RED, KF
    for i in range(8):
        P.ts("dve", TinR[:, i, :], COST[:, i, :], fr[:, i:i + 1], ALU.mult)
        P.stt(TinR[:, i, :], SINT[:, i, :], fi[:, i:i + 1], TinR[:, i, :], ALU.mult, ALU.add)
        P.ts("dve", TinI[:, i, :], SINT[:, i, :], fr[:, i:i + 1], ALU.mult)
        P.stt(TinI[:, i, :], COST[:, i, :], fi[:, i:i + 1], TinI[:, i, :], ALU.mult, ALU.subtract)
        P.ts("dve", RHO[:, i, :], iota, 0.0, ALU.mult, rho[:, i:i + 1], ALU.add)
    P.barrier()
    A.release(m_ki)
    car_r = [s8("car_r0"), s8("car_r1")]
    car_i = [s8("car_i0"), s8("car_i1")]
    ub = [A.alloc([128, 2, 512], S5_CD, "ub%d" % i) for i in range(2)]

    def w5(name, dt=F32):
        return A.alloc([128, 512], dt, name)

    NS = 3
    ta, tb, tc, td, wr, wi, zr, zi_ = [[w5("%s%d" % (n, i)) for i in range(NS)] for n in (
        "ta", "tb", "tc", "td", "wr", "wi", "zr", "zi")]
    xr, xin = [[w5("%s%d" % (n, i), S5_CD) for i in range(NS)] for n in ("xr", "xin")]
    YS = A.alloc([128, 2, 512], F32, "YS")
    yp, sqs = w5("yp"), A.alloc([128, 2, 512], F32, "sqs")
    yg = w5("yg", S5_CD)
    rst = w5("rst")
    stg = [w5("sstg0"), w5("sstg1")]
    n = 0
    for blk in range(NB):
        t0 = blk * 512
        ts_ = slice(t0, t0 + 512)
        u = ub[blk % 2]
        P.load("pool", u.v(), zT_d[14 * 128:16 * 128, ts_].rearrange("(c p) n -> p c n", p=128),
               [zR[14][blk], zR[15][blk]])
        cr_o, ci_o = car_r[(blk + 1) % 2], car_i[(blk + 1) % 2]
        cr_n, ci_n = car_r[blk % 2], car_i[blk % 2]
        Ys = [P.psum_acc(), P.psum_acc()]

        def stA(i, j):
            chn = i // 4
            PR, PI = P.psum(), P.psum()
            P.mm(PR.v(), SB[:, i, :], u[:, chn, :])
            P.mm(PI.v(), SB[:, 8 + i, :], u[:, chn, :])
            P.tt("dve", ta[j].v(), PR.v(), TinR[:, i, :], ALU.mult)
            P.tt("dve", tb[j].v(), PI.v(), TinI[:, i, :], ALU.mult)
            P.tt("pool", wr[j].v(), ta[j].v(), tb[j].v(), ALU.subtract)
            P.tt("dve", tc[j].v(), PR.v(), TinI[:, i, :], ALU.mult)
            P.tt("dve", td[j].v(), PI.v(), TinR[:, i, :], ALU.mult)
            P.tt("pool", wi[j].v(), tc[j].v(), td[j].v(), ALU.add)

        def stB(i, j):
            ir = 0.0 if blk == 0 else cr_o[:, i:i + 1]
            ii = 0.0 if blk == 0 else ci_o[:, i:i + 1]
            P.scan(zr[j].v(), RHO[:, i, :], wr[j].v(), ir)
            P.scan(zi_[j].v(), RHO[:, i, :], wi[j].v(), ii)
            P.tt("pool", ta[j].v(), zr[j].v(), COST[:, i, :], ALU.mult)
            P.tt("pool", tb[j].v(), zi_[j].v(), SINT[:, i, :], ALU.mult)
            P.tt("pool", tc[j].v(), zr[j].v(), SINT[:, i, :], ALU.mult)
            P.tt("dve", td[j].v(), zi_[j].v(), COST[:, i, :], ALU.mult)

        def stC(i, j):
            chn, i4 = i // 4, i % 4
            Y = Ys[chn]
            P.tt("pool", xr[j].v(), ta[j].v(), tb[j].v(), ALU.subtract)
            P.tt("pool", cr_n[:, i:i + 1], ta[j][:, 511:512], tb[j][:, 511:512], ALU.subtract)
            P.stt(xin[j].v(), tc[j].v(), -1.0, td[j].v(), ALU.mult, ALU.subtract)
            P.tt("pool", ci_n[:, i:i + 1], tc[j][:, 511:512], td[j][:, 511:512], ALU.add)
            P.mm(Y.v(), SC[:, i, :], xr[j].v(), start=(i4 == 0), stop=False)
            P.mm(Y.v(), SC[:, 8 + i, :], xin[j].v(), start=False, stop=(i4 == 3))
            if i4 == 3:
                P.stt(yp.v(), u[:, chn, :], col("ssmd", chn), Y.v(), ALU.mult, ALU.add)
                P.act(yg.v(), yp.v(), AF.Gelu_apprx_tanh)
                ps = P.psum()
                P.mm(ps.v(), GL[:, chn, :], yg.v())
                P.act(yp.v(), ps.v(), AF.Sigmoid)
                P.tt("dve", YS[:, chn, :], yg.v(), yp.v(), ALU.mult)

        js = [(n + i) % NS for i in range(8)]
        n += 8
        for step in range(8 + 2):
            if step < 8:
                stA(step, js[step])
            if 0 <= step - 1 < 8:
                stB(step - 1, js[step - 1])
            if 0 <= step - 2 < 8:
                stC(step - 2, js[step - 2])
        P.act(sqs.v(), YS.v(), AF.Square)
        ps = P.psum()
        P.mm(ps.v(), ones, sqs[:, 0, :], start=True, stop=False)
        P.mm(ps.v(), ones, sqs[:, 1, :], start=False, stop=True)
        rsqrt_into(rst.v(), ps.v(), 1.0 / 256, EPS)
        for chn in range(2):
            P.stt(stg[chn].v(), YS[:, chn, :], col("og_ssm", chn), rst.v(), ALU.mult, ALU.mult)
            P.store("sp", mixT_d[(4 + chn) * 128:(5 + chn) * 128, ts_], stg[chn].v(), [mixR[4 + chn][blk]])


def phase_mla(P, A, l, NB, T, col, zT_d, zR, mixT_d, mixR, wq_d, wkvK_d, wkvV_d, pos_in, ones, ident_bf, amask,
              rotP_bf, selkr_bf, invfreq, rsqrt_into, evac):
    NCH = T // 128
    Qa = A.alloc([128, 4, T], BF16, "Qmla")
    Ka = A.alloc([128, 4, T], BF16, "Kmla")
    Va = A.alloc([128, NCH, 4, 65], BF16, "Vmla")
    Qb_ = [Buf(Qa.ap[:, :, b_ * 512:(b_ + 1) * 512], "Qm%d" % b_) for b_ in range(NB)]
    Kb_ = [Buf(Ka.ap[:, :, b_ * 512:(b_ + 1) * 512], "Km%d" % b_) for b_ in range(NB)]
    Vb_ = [Buf(Va.ap[:, b_ * 4:(b_ + 1) * 4, :, :], "Vm%d" % b_) for b_ in range(NB)]
    for b_ in range(NB):
        P.memset("pool", Vb_[b_][:, :, :, 64:65], 1.0)
    Wq = A.alloc([128, 2, 384], BF16, "Wq")
    WkK = A.alloc([128, 4, 96], BF16, "WkK")
    WkV = A.alloc([128, 256], BF16, "WkV")
    P.load("pool", Wq.v(), wq_d[l, :, :, :])
    P.load("pool", WkK.v(), wkvK_d[l, :, :, :])
    P.load("pool", WkV.v(), wkvV_d[l, :, :])

    def w5(name, dt=F32, p=128):
        return A.alloc([p, 512], dt, name)

    z16, z17, z18 = [w5(n) for n in ("z16", "z17", "z18")]
    z13b = w5("z13b", BF16, 64)
    cos_d, sin_d, ropeR = pos_in
    COS, SIN = w5("COS", F32, 96), w5("SIN", F32, 96)
    sq16, sq17, sq18, rs192, rs128 = [w5(n) for n in ("sq16", "sq17", "sq18", "rs192", "rs128")]
    cqn = A.alloc([128, 2, 512], BF16, "cqn")
    ckvn = w5("ckvn", BF16)
    raw, sqr, rbq, qn, tc_ = [[w5("%s%d" % (n, i), F32, 96) for i in range(4)] for n in ("raw", "sqr", "rbq", "qn", "tc")]
    qnb = [w5("qnb%d" % i, BF16, 96) for i in range(4)]

    def prep(blk):
        ts_ = slice(blk * 512, (blk + 1) * 512)
        Qm, Km, Vm = Qb_[blk], Kb_[blk], Vb_[blk]
        st = []

        def s_load():
            P.load("sp", z16.v(), zT_d[16 * 128:17 * 128, ts_], [zR[16][blk]])
            P.load("sp", z17[0:64, :], zT_d[17 * 128:17 * 128 + 64, ts_], [zR[17][blk]])
            P.load("sp", z18.v(), zT_d[18 * 128:19 * 128, ts_], [zR[18][blk]])
            P.load("pool", z13b.v(), zT_d[13 * 128:13 * 128 + 64, ts_], [zR[13][blk]])
            P.load("sp", COS.v(), cos_d[:, ts_], [ropeR[0][blk]])
            P.load("sp", SIN.v(), sin_d[:, ts_], [ropeR[1][blk]])
            P.act(sq16.v(), z16.v(), AF.Square)
            P.act(sq17[0:64, :], z17[0:64, :], AF.Square)
            P.act(sq18.v(), z18.v(), AF.Square)
        st.append(s_load)

        def s_lat():
            ps = P.psum()
            P.mm(ps.v(), ones, sq16.v(), start=True, stop=False)
            P.mm(ps.v(), ones[0:64, :], sq17[0:64, :], start=False, stop=True)
            rsqrt_into(rs192.v(), ps.v(), 1.0 / 192, EPS)
            ps = P.psum()
            P.mm(ps.v(), ones, sq18.v())
            rsqrt_into(rs128.v(), ps.v(), 1.0 / 128, EPS)
            P.stt(cqn[:, 0, :], z16.v(), col("mqlg", 0), rs192.v(), ALU.mult, ALU.mult)
            P.stt(cqn[0:64, 1, :], z17[0:64, :], col("mqlg", 1, 0, 64), rs192[0:64, :], ALU.mult, ALU.mult)
            P.stt(ckvn.v(), z18.v(), col("mkvlg", 0), rs128.v(), ALU.mult, ALU.mult)
        st.append(s_lat)

        def s_v():
            for tt in range(4):
                ps = P.psum()
                P.mm(ps[:, 0:256], ckvn[:, tt * 128:(tt + 1) * 128], WkV.v())
                evac(Vm[:, tt, :, 0:64], ps[:, 0:256].rearrange("p (h d) -> p h d", h=4))
        st.append(s_v)
        items = [(h, isk) for h in range(4) for isk in range(2)]
        for g0 in range(0, 8, 4):
            grp = items[g0:g0 + 4]

            def g1(grp=grp):
                for j, (h, isk) in enumerate(grp):
                    ps = P.psum()
                    if not isk:
                        P.mm(ps[0:96, :], Wq[:, 0, 96 * h:96 * h + 96], cqn[:, 0, :], start=True, stop=False)
                        P.mm(ps[0:96, :], Wq[0:64, 1, 96 * h:96 * h + 96], cqn[0:64, 1, :], start=False, stop=True)
                    else:
                        P.mm(ps[0:96, :], WkK[:, h, :], ckvn.v(), start=True, stop=False)
                        P.mm(ps[0:96, :], selkr_bf.v(), z13b.v(), start=False, stop=True)
                    P.copy("act", raw[j].v(), ps[0:96, :])
                    P.act(sqr[j].v(), ps[0:96, :], AF.Square)

            def g2(grp=grp):
                for j, (h, isk) in enumerate(grp):
                    p2 = P.psum()
                    P.mm(p2[0:96, :], ones[0:96, 0:96], sqr[j].v())
                    if not isk:
                        P.act(rbq[j].v(), p2[0:96, :], AF.Sqrt, bias=96 * EPS, scale=1.0)
                    else:
                        P.act(rbq[j].v(), p2[0:96, :], AF.Sqrt, bias=EPS, scale=1.0 / 96)
                for j, (h, isk) in enumerate(grp):
                    P.recip(rbq[j].v(), rbq[j].v())

            def g3(grp=grp):
                for j, (h, isk) in enumerate(grp):
                    P.stt(qn[j].v(), raw[j].v(), col("mkg" if isk else "mqg", 0, 0, 96), rbq[j].v(), ALU.mult, ALU.mult)
                    P.copy("pool", qnb[j].v(), qn[j].v())

            def g4(grp=grp):
                for j, (h, isk) in enumerate(grp):
                    p3 = P.psum()
                    P.mm(p3[0:96, :], rotP_bf.v(), qnb[j].v())
                    P.tt("pool", tc_[j].v(), qn[j].v(), COS.v(), ALU.mult)
                    P.tt("dve", raw[j].v(), p3[0:96, :], SIN.v(), ALU.mult)
                for j, (h, isk) in enumerate(grp):
                    P.tt("pool", (Km if isk else Qm)[0:96, h, :], tc_[j].v(), raw[j].v(), ALU.add)

            st += [g1, g2, g3, g4]
        return st

    for f_ in prep(0):
        f_()
    hooks = {}
    for qb in range(NB - 1):
        st = prep(qb + 1)
        n_ = len(st)
        for h in range(4):
            hooks[(qb, h)] = st[(h * n_) // 4:((h + 1) * n_) // 4]
    attention(P, A, NB, Kb_, Qb_, 96, Vb_, ident_bf, amask, ones, None, "og_mla", col, 768, mixT_d, mixR,
              rsqrt_into, evac, hooks)


def host_prep(inputs, T):
    L = inputs["w_in"].shape[0]
    f = np.float32
    g = lambda k: np.asarray(inputs[k], dtype=f)
    w_in = g("w_in")
    wp = np.zeros((L, D, ZW), f)
    wp[:, :, 0:896] = w_in[:, :, 0:896]
    fo = 896
    wp[:, :, 7 * 128:13 * 128] = w_in[:, :, fo:fo + 768]
    wp[:, :, 13 * 128:13 * 128 + 4] = w_in[:, :, fo + 768:fo + 772]
    so = fo + 772
    wp[:, :, 14 * 128:16 * 128] = w_in[:, :, so:so + 256]
    mo = so + 256
    wp[:, :, 16 * 128:16 * 128 + 192] = w_in[:, :, mo:mo + 192]
    wp[:, :, 18 * 128:19 * 128] = w_in[:, :, mo + 192:mo + 320]
    wp[:, :, 13 * 128 + 32:13 * 128 + 64] = w_in[:, :, mo + 320:mo + 352]
    cols = np.zeros((L, 128, NC_COLS), f)

    def put(name, arr, width):
        a = arr.reshape(L, width, 128).transpose(0, 2, 1)
        cols[:, :, COLS[name]:COLS[name] + width] = a

    put("mixg", g("mix_norm_g"), 8)
    put("mlpg", g("mlp_norm_g"), 8)
    put("mu", g("rwkv_mu"), 7)
    put("w0", g("rwkv_w0"), 2)
    put("a0", g("rwkv_a0"), 2)
    put("kk", g("rwkv_k_k"), 2)
    put("ka", g("rwkv_k_a"), 2)
    put("lng", g("rwkv_ln_g"), 2)
    put("lnb", g("rwkv_ln_b"), 2)
    put("rk", g("rwkv_r_k").reshape(L, 256), 2)
    cols[:, :, COLS["foxqg"]] = np.tile(g("fox_q_g"), (1, 2))
    cols[:, :, COLS["foxkg"]] = np.tile(g("fox_k_g"), (1, 2))
    cols[:, 0:4, COLS["foxfb"]] = g("fox_f_b")
    put("ssmd", g("ssm_d"), 2)
    og = g("out_norm_g")
    cols[:, 0:64, COLS["og_fox"]:COLS["og_fox"] + 4] = og[:, 0].reshape(L, 4, 64).transpose(0, 2, 1)
    put("og_ssm", og[:, 1], 2)
    cols[:, 0:64, COLS["og_mla"]:COLS["og_mla"] + 4] = og[:, 2].reshape(L, 4, 64).transpose(0, 2, 1)
    ql = g("mla_q_latent_g")
    cols[:, :, COLS["mqlg"]] = ql[:, 0:128]
    cols[:, 0:64, COLS["mqlg"] + 1] = ql[:, 128:192]
    cols[:, :, COLS["mkvlg"]] = g("mla_kv_latent_g")
    cols[:, 0:96, COLS["mqg"]] = g("mla_q_g")
    cols[:, 0:96, COLS["mkg"]] = g("mla_k_g")
    for nm, arr in (("slre", g("ssm_lambda_re")), ("slim", g("ssm_lambda_im")),
                    ("sldt", np.repeat(g("ssm_log_dt")[:, :, None], 64, axis=2))):
        cols[:, :, COLS[nm]:COLS[nm] + 8] = arr.reshape(L, 8, 128).transpose(0, 2, 1)
    lr_w = np.concatenate([g("rwkv_w_up"), g("rwkv_a_up"), g("rwkv_g_up")], axis=1)
    bre, bim, cre, cim = g("ssm_b_re"), g("ssm_b_im"), g("ssm_c_re"), g("ssm_c_im")
    ssmB = np.zeros((L, 128, 16, 128), f)
    ssmC = np.zeros((L, 128, 16, 128), f)
    for gi in range(16):
        i, gg = gi // 2, gi % 2
        chn, i4 = i // 4, i % 4
        rows = slice(32 * i4 + 16 * gg, 32 * i4 + 16 * gg + 16)
        stc = slice(64 * gg, 64 * gg + 64)
        ssmB[:, rows, i, stc] = bre[:, gi].transpose(0, 2, 1)
        ssmB[:, rows, 8 + i, stc] = bim[:, gi].transpose(0, 2, 1)
        oc = slice((gi * 16) % 128, (gi * 16) % 128 + 16)
        ssmC[:, stc, i, oc] = cre[:, gi].transpose(0, 2, 1)
        ssmC[:, stc, 8 + i, oc] = cim[:, gi].transpose(0, 2, 1)
    glu = g("ssm_glu_w")
    glu_bd = np.zeros((L, 128, 2, 128), f)
    for gi in range(16):
        chn, o = gi // 8, (gi % 8) * 16
        glu_bd[:, o:o + 16, chn, o:o + 16] = glu[:, gi]
    wq = g("mla_w_q_up")
    mla_wq = np.zeros((L, 128, 2, 384), f)
    mla_wq[:, :, 0, :] = wq[:, 0:128]
    mla_wq[:, 0:64, 1, :] = wq[:, 128:192]
    wkv = g("mla_w_kv_up").reshape(L, 128, 4, 128)
    mla_wkvK = np.zeros((L, 128, 4, 96), f)
    mla_wkvK[:, :, :, 0:64] = wkv[:, :, :, 0:64]
    mla_wkvV = np.ascontiguousarray(wkv[:, :, :, 64:128]).reshape(L, 128, 256)
    cst = np.zeros((128, 2048), f)
    cst[:, 0:128] = np.eye(128)
    cst[0:64, 128:192] = 1
    cst[64:128, 192:256] = 1
    cst[:, 256:384] = 1
    s = (np.arange(128) % 64)[:, None]
    t = np.arange(64)[None, :]
    cst[:, 384:448] = (s < t)
    cst[:, 448:512] = (s <= t)
    cst[:, 512:576] = (s > t)
    sm = np.ones((128, 512), f)
    sm[:, 0::64] = 0
    cst[:, 576:1088] = sm
    cst[:, 1088:1600] = np.arange(1, 513)[None, :]
    rot = np.zeros((96, 96), f)
    for i in range(16):
        rot[80 + i, 64 + i] = -1
        rot[64 + i, 80 + i] = 1
    cst[0:96, 1600:1696] = rot
    sel = np.zeros((64, 96), f)
    for i in range(32):
        sel[32 + i, 64 + i] = 1
    cst[0:64, 1696:1792] = sel
    inv = (10000.0 ** (-np.arange(0, 32, 2, dtype=np.float32) / 32)).astype(f)
    cst[64:80, 1792] = inv
    cst[80:96, 1792] = inv
    cst[:, 1800:1928] = 1
    k = np.arange(128)[:, None]
    q = np.arange(512)[None, :]
    amask = np.zeros((128, 4, 512), f)
    for j in range(4):
        amask[:, j, :] = np.where(q >= k + 128 * j, 0.0, NEG)
    place = np.zeros((4, 4, 128), f)
    auxm = np.zeros((128, 2, 8), f)
    for h in range(4):
        a0 = 64 if h % 2 == 0 else 0
        place[h, h, a0:a0 + 6] = 1
    for par in range(2):
        a0 = 64 if par == 0 else 0
        for r in range(3):
            auxm[a0 + r, par, r] = 1
            auxm[a0 + 3 + r, par, r] = 1
            auxm[a0 + r, par, 3] = 1
            auxm[a0 + 3 + r, par, 4] = 1
            auxm[a0 + 3 + r, par, 5] = -1
            auxm[a0 + r, par, 6] = 1
    shared = dict(w_in_p=wp, w_out=g("w_out"), w_ff1=g("w_ff1"), w_ff2=g("w_ff2"), cols=cols, lr_w=lr_w, ssmB=ssmB,
                  ssmC=ssmC, glu_bd=glu_bd, mla_wq=mla_wq, mla_wkvK=mla_wkvK, mla_wkvV=mla_wkvV, cst=cst, amask=amask,
                  place=place, auxm=auxm)
    return shared


_NC_CACHE = {}


def kernel(**inputs):
    x = np.asarray(inputs["x"], dtype=np.float32)
    B, T, _ = x.shape
    L = inputs["w_in"].shape[0]
    shared = host_prep(inputs, T)
    key = (T, L)
    if key not in _NC_CACHE:
        _NC_CACHE[key] = build(T, L)
    nc = _NC_CACHE[key]
    pos = np.asarray(inputs["positions"], dtype=np.int32)
    in_maps = []
    for b in range(B):
        m = dict(shared)
        m["xT"] = np.ascontiguousarray(x[b].T)
        m["pos"] = np.ascontiguousarray(pos[b:b + 1])
        in_maps.append(m)
    res = run_bass_kernel_spmd(nc, in_maps, core_ids=list(range(B)))
    out = np.stack([np.asarray(r["outT"]).T for r in res.results], axis=0)
    return np.ascontiguousarray(out.astype(np.float32))
```

```python
import math
import contextlib
import numpy as np
import concourse.bass as bass
import concourse.mybir as mybir
from concourse.bass_utils import run_bass_kernel_spmd

F32 = mybir.dt.float32
BF16 = mybir.dt.bfloat16
I32 = mybir.dt.int32
AF = mybir.ActivationFunctionType
ALU = mybir.AluOpType

D = 1024
NZC = 19
ZW = NZC * 128
EPS = 1e-6
C0 = math.exp(-0.5)
NEG = -30000.0
RWKV_CD = BF16
RWKV_CHD = F32
S5_CD = F32


class View:
    __slots__ = ("b", "ap")

    def __init__(self, b, ap):
        self.b = b
        self.ap = ap

    def __getitem__(self, idx):
        return View(self.b, self.ap[idx])

    def rearrange(self, *a, **k):
        return View(self.b, self.ap.rearrange(*a, **k))

    def bitcast(self, dt):
        return View(self.b, self.ap.bitcast(dt))


class Buf:
    __slots__ = ("ap", "w", "r", "name")

    def __init__(self, ap, name=""):
        self.ap = ap
        self.w = {}
        self.r = {}
        self.name = name

    def __getitem__(self, idx):
        return View(self, self.ap[idx])

    def v(self):
        return View(self, self.ap)


def _ap(x):
    return x.ap if isinstance(x, View) else x


def _bufs(*xs):
    out = []
    for x in xs:
        if isinstance(x, View):
            out.append(x.b)
        elif isinstance(x, Buf):
            out.append(x)
    return out


class Prog:
    ENGS = ("pe", "dve", "act", "pool", "sp")
    NSLOT = 14

    def __init__(self, nc):
        self.nc = nc
        self.q = {e: [] for e in self.ENGS}
        self.cnt = {e: 0 for e in self.ENGS}
        self.waited = {e: {} for e in self.ENGS}
        self.slot_val = {}
        self.slot_next = {"sp": 0, "pool": 0}
        self.final = []
        self.psb = []
        self.psi = 0

    def _need(self, eng, key, val, waits):
        if key == eng and eng == "pe":
            return
        if self.waited[eng].get(key, 0) >= val:
            return
        self.waited[eng][key] = val
        waits.append((key, val))

    def _deps(self, eng, reads, writes):
        waits = []
        for b in reads:
            for k, v in b.w.items():
                self._need(eng, k, v, waits)
        for b in writes:
            for k, v in b.w.items():
                self._need(eng, k, v, waits)
            for k, v in b.r.items():
                self._need(eng, k, v, waits)
        return waits

    def _mark(self, tok, reads, writes):
        k, v = tok
        for b in reads:
            if b.r.get(k, 0) < v:
                b.r[k] = v
        for b in writes:
            b.w = {k: v}
            b.r = {}

    def op(self, eng, fn, reads=(), writes=()):
        waits = self._deps(eng, reads, writes)
        self.cnt[eng] += 1
        tok = (eng, self.cnt[eng])
        self.q[eng].append((waits, fn, tok))
        self._mark(tok, reads, writes)

    def dma(self, queue, fn, reads=(), writes=(), final=False):
        s = self.slot_next[queue]
        self.slot_next[queue] = (s + 1) % self.NSLOT
        key = ("d", queue, s)
        prev = self.slot_val.get(key, 0)
        waits = self._deps(queue, reads, writes)
        if prev > 0:
            self._need(queue, key, prev, waits)
        val = prev + 16
        self.slot_val[key] = val
        tok = (key, val)
        self.q[queue].append((waits, fn, tok))
        self._mark(tok, reads, writes)
        if final:
            self.final.append(tok)

    def barrier(self):
        for e in self.ENGS:
            waits = []
            for f in ("pe", "dve", "act", "pool"):
                if f != e and self.cnt[f] > 0:
                    self._need(e, f, self.cnt[f], waits)
            for key, val in self.slot_val.items():
                self._need(e, key, val, waits)
            if waits:
                self.q[e].append((waits, None, None))

    def psum(self):
        b = self.psb[self.psi]
        self.psi = (self.psi + 1) % 6
        return b

    def psum_acc(self):
        self.pai = 1 - getattr(self, "pai", 0)
        return self.psb[6 + self.pai]

    def mm(self, out, lhsT, rhs, start=True, stop=True):
        o, l, r = _ap(out), _ap(lhsT), _ap(rhs)
        self.op("pe", lambda e: e.matmul(o, lhsT=l, rhs=r, start=start, stop=stop),
                reads=_bufs(lhsT, rhs), writes=_bufs(out))

    def act(self, out, in_, func, bias=0.0, scale=1.0):
        o, i, b = _ap(out), _ap(in_), _ap(bias)
        self.op("act", lambda e: e.activation(out=o, in_=i, func=func, bias=b, scale=scale),
                reads=_bufs(in_, bias), writes=_bufs(out))

    def tt(self, eng, out, in0, in1, op):
        o, a, b = _ap(out), _ap(in0), _ap(in1)
        self.op(eng, lambda e: e.tensor_tensor(out=o, in0=a, in1=b, op=op),
                reads=_bufs(in0, in1), writes=_bufs(out))

    def ts(self, eng, out, in0, s1, op0, s2=None, op1=None):
        o, a, x1, x2 = _ap(out), _ap(in0), _ap(s1), _ap(s2)
        if op1 is None:
            self.op(eng, lambda e: e.tensor_scalar(out=o, in0=a, scalar1=x1, scalar2=None, op0=op0),
                    reads=_bufs(in0, s1), writes=_bufs(out))
        else:
            self.op(eng, lambda e: e.tensor_scalar(out=o, in0=a, scalar1=x1, scalar2=x2, op0=op0, op1=op1),
                    reads=_bufs(in0, s1, s2), writes=_bufs(out))

    def stt(self, out, in0, scalar, in1, op0, op1):
        o, a, s, b = _ap(out), _ap(in0), _ap(scalar), _ap(in1)
        self.op("dve", lambda e: e.scalar_tensor_tensor(out=o, in0=a, scalar=s, in1=b, op0=op0, op1=op1),
                reads=_bufs(in0, scalar, in1), writes=_bufs(out))

    def copy(self, eng, out, in_):
        o, i = _ap(out), _ap(in_)
        if eng == "act":
            self.op("act", lambda e: e.activation(out=o, in_=i, func=AF.Copy), reads=_bufs(in_), writes=_bufs(out))
        else:
            self.op(eng, lambda e: e.tensor_copy(out=o, in_=i), reads=_bufs(in_), writes=_bufs(out))

    def scan(self, out, d0, d1, init, op0=ALU.mult, op1=ALU.add):
        o, a, b, i = _ap(out), _ap(d0), _ap(d1), _ap(init)
        self.op("dve", lambda e: e.tensor_tensor_scan(out=o, data0=a, data1=b, initial=i, op0=op0, op1=op1),
                reads=_bufs(d0, d1, init), writes=_bufs(out))

    def recip(self, out, in_):
        o, i = _ap(out), _ap(in_)
        self.op("dve", lambda e: e.reciprocal(out=o, in_=i), reads=_bufs(in_), writes=_bufs(out))

    def memset(self, eng, out, val):
        o = _ap(out)
        self.op(eng, lambda e: e.memset(o, val), writes=_bufs(out))

    def load(self, queue, out, src_ap, src_bufs=()):
        o = _ap(out)
        self.dma(queue, lambda e: e.dma_start(out=o, in_=src_ap), reads=list(src_bufs), writes=_bufs(out))

    def store(self, queue, dst_ap, in_, dst_bufs=(), final=False):
        i = _ap(in_)
        self.dma(queue, lambda e: e.dma_start(out=dst_ap, in_=i), reads=_bufs(in_), writes=list(dst_bufs), final=final)

    def emit(self):
        nc = self.nc
        with contextlib.ExitStack() as st:
            sems = {}
            for e in ("pe", "dve", "act", "pool"):
                sems[e] = st.enter_context(nc.semaphore("s_" + e))
            for key in self.slot_val:
                sems[key] = st.enter_context(nc.semaphore("d_%s_%d" % (key[1], key[2])))
            block = st.enter_context(nc.Block())
            endw = [(e, self.cnt[e]) for e in ("pe", "dve", "act", "pool") if self.cnt[e] > 0]
            endw += list(self.slot_val.items())

            def mk(ename):
                def body(eng):
                    for waits, fn, tok in self.q[ename]:
                        for k, v in waits:
                            eng.wait_ge(sems[k], v)
                        if fn is None:
                            continue
                        ins = fn(eng)
                        k, v = tok
                        ins.then_inc(sems[k], 16 if isinstance(k, tuple) else 1)
                    if ename == "sp":
                        for k, v in endw:
                            eng.wait_ge(sems[k], v)
                return body

            block.tensor(mk("pe"))
            block.vector(mk("dve"))
            block.scalar(mk("act"))
            block.gpsimd(mk("pool"))
            block.sync(mk("sp"))


class Arena:
    def __init__(self, buf_ap, words):
        self.ap = buf_ap
        self.words = words
        self.off = 0

    def mark(self):
        return self.off

    def release(self, m):
        self.off = m

    def alloc(self, shape, dt=F32, name=""):
        n = 1
        for s in shape[1:]:
            n *= s
        w = n if dt != BF16 else (n + 1) // 2
        w = (w + 7) // 8 * 8
        assert self.off + w <= self.words, ("arena overflow", name, self.off, w, self.words)
        ap = self.ap[:, self.off:self.off + w]
        self.off += w
        if dt != F32:
            ap = ap.bitcast(dt)
        ap = ap[:, 0:n]
        if len(shape) == 3:
            ap = ap.rearrange("p (a b) -> p a b", a=shape[1])
        elif len(shape) == 4:
            ap = ap.rearrange("p (a b c) -> p a b c", a=shape[1], b=shape[2])
        if shape[0] < 128:
            ap = ap[0:shape[0]]
        return Buf(ap, name)


COLS = {}
_o = 0
for _n, _c in [("mixg", 8), ("mlpg", 8), ("mu", 7), ("w0", 2), ("a0", 2), ("kk", 2), ("ka", 2), ("omka", 2),
               ("lng", 2), ("lnb", 2), ("rk", 2), ("foxqg", 1), ("foxkg", 1), ("foxfb", 1), ("ssmd", 2),
               ("og_fox", 4), ("og_ssm", 2), ("og_mla", 4), ("mqlg", 2), ("mkvlg", 1), ("mqg", 1), ("mkg", 1),
               ("slre", 8), ("slim", 8), ("sldt", 8)]:
    COLS[_n] = _o
    _o += _c
NC_COLS = _o


def build(T, NL, dbg=False):
    NB = T // 512
    NCH = T // 128
    nc = bass.Bass("TRN2", target_bir_lowering=False)
    P = Prog(nc)

    def din(name, shape, dt=F32):
        return nc.dram_tensor(name, shape, dt, kind="ExternalInput").ap()

    xT_in = din("xT", [D, T])
    pos_in = din("pos", [1, T], I32)
    w_in_d = din("w_in_p", [NL, D, ZW])
    w_out_d = din("w_out", [NL, D, D])
    w_ff1_d = din("w_ff1", [NL, D, 4 * D])
    w_ff2_d = din("w_ff2", [NL, 4 * D, D])
    cols_d = din("cols", [NL, 128, NC_COLS])
    lr_d = din("lr_w", [NL, 128, 256])
    ssmB_d = din("ssmB", [NL, 128, 16, 128])
    ssmC_d = din("ssmC", [NL, 128, 16, 128])
    glu_d = din("glu_bd", [NL, 128, 2, 128])
    wq_d = din("mla_wq", [NL, 128, 2, 384])
    wkvK_d = din("mla_wkvK", [NL, 128, 4, 96])
    wkvV_d = din("mla_wkvV", [NL, 128, 256])
    cst_d = din("cst", [128, 2048])
    amask_d = din("amask", [128, 4, 512])
    place_d = din("place", [4, 4, 128])
    auxm_d = din("auxm", [128, 2, 8])
    outT = nc.dram_tensor("outT", [D, T], F32, kind="ExternalOutput").ap()

    def dscr(name, shape):
        return nc.dram_tensor(name, shape, F32, kind="Internal").ap()

    zT_d = dscr("zT", [ZW, T])
    mixT_d = dscr("mixT", [D, T])
    xmid_d = dscr("xmid", [D, T])
    xres_d = dscr("xres", [D, T])
    cos_d = dscr("cosT", [96, T])
    sin_d = dscr("sinT", [96, T])
    dbg_out = {}
    if dbg:
        dbg_out["zT_o"] = nc.dram_tensor("zT_o", [ZW, T], F32, kind="ExternalOutput").ap()
        dbg_out["mixT_o"] = nc.dram_tensor("mixT_o", [D, T], F32, kind="ExternalOutput").ap()
        dbg_out["xmid_o"] = nc.dram_tensor("xmid_o", [D, T], F32, kind="ExternalOutput").ap()

    def regions(nchunk):
        return [[Buf(None, "r") for _ in range(NB)] for _ in range(nchunk)]

    zR = regions(NZC)
    mixR = regions(8)
    xmidR = regions(8)
    xresR = regions(8)
    outR = regions(8)

    st = contextlib.ExitStack()
    with st:
        AW = 52000
        arena_t = st.enter_context(nc.sbuf_tensor("arena", [128, AW], F32))
        A = Arena(arena_t[:, :], AW)
        for i in range(8):
            P.psb.append(Buf(st.enter_context(nc.psum_tensor("ps%d" % i, [128, 512], F32))[:, :], "ps%d" % i))

        cst = A.alloc([128, 2048], F32, "cst")
        P.load("sp", cst.v(), cst_d[:, :])
        ident = cst[:, 0:128]
        onesblk = cst[:, 128:256]
        ones = cst[:, 256:384]
        tri = cst[:, 384:576]
        scanmask = cst[:, 576:1088]
        iota = cst[:, 1088:1600]
        rotP = cst[0:96, 1600:1696]
        selkr = cst[0:64, 1696:1792]
        invfreq = cst[0:96, 1792:1793]
        onesrow = cst[:, 1800:1928]
        ones_bf = A.alloc([128, 128], BF16, "ones_bf")
        ident_bf = A.alloc([128, 128], BF16, "ident_bf")
        rotP_bf = A.alloc([96, 96], BF16, "rotP_bf")
        selkr_bf = A.alloc([64, 96], BF16, "selkr_bf")
        P.copy("dve", ones_bf.v(), ones)
        P.copy("dve", ident_bf.v(), ident)
        P.copy("dve", rotP_bf.v(), rotP)
        P.copy("dve", selkr_bf.v(), selkr)
        amask = A.alloc([128, 4, 512], BF16, "amask")
        P.load("pool", amask.v(), amask_d[:, :, :])
        place = A.alloc([4, 4, 128], F32, "place")
        P.load("sp", place.v(), place_d[:, :, :])
        auxm = A.alloc([128, 2, 8], F32, "auxm")
        P.load("sp", auxm.v(), auxm_d[:, :, :])
        ones512 = A.alloc([128, 512], F32, "ones512")
        P.memset("pool", ones512.v(), 1.0)
        cols = A.alloc([128, NC_COLS], F32, "cols")
        m_layer = A.mark()
        ropeR = [[Buf(None, "rope") for _ in range(NB)] for _ in range(2)]
        posi = A.alloc([96, 512], I32, "posi")
        ang, red, kf_, COSb, SINb = [A.alloc([96, 512], F32, n_) for n_ in ("ang", "red", "kf", "COSb", "SINb")]
        ki_ = A.alloc([96, 512], I32, "ki")
        for blk in range(NB):
            ts_ = slice(blk * 512, (blk + 1) * 512)
            P.load("sp", posi.v(), pos_in[0:1, ts_].broadcast_to([96, 512]))
            P.copy("dve", ang.v(), posi.v())
            P.ts("dve", ang.v(), ang.v(), invfreq, ALU.mult)
            range_reduce(P, red.v(), ang.v(), ki_.v(), kf_.v())
            P.act(SINb.v(), red.v(), AF.Sin)
            P.ts("dve", ang.v(), ang.v(), math.pi / 2, ALU.add)
            range_reduce(P, red.v(), ang.v(), ki_.v(), kf_.v())
            P.act(COSb.v(), red.v(), AF.Sin)
            P.store("sp", cos_d[:, ts_], COSb.v(), [ropeR[0][blk]])
            P.store("sp", sin_d[:, ts_], SINb.v(), [ropeR[1][blk]])

        def col(name, i=0, p0=0, p1=128):
            c = COLS[name] + i
            return cols[p0:p1, c:c + 1]

        def rsqrt_into(out, ps_in, scale, bias):
            P.act(out, ps_in, AF.Sqrt, bias=bias, scale=scale)
            P.recip(out, out)

        evac_rr = [0]

        def evac(out, in_):
            evac_rr[0] ^= 1
            P.copy("act" if evac_rr[0] else "dve", out, in_)

        def rmsnorm_block(xsrcR, src_ap, gname, blk, xt, xn, sq, rstd):
            t0 = blk * 512
            P.load("sp", xt, src_ap[:, t0:t0 + 512].rearrange("(c p) n -> p c n", p=128),
                   [xsrcR[c][blk] for c in range(8)])
            P.act(sq, xt, AF.Square)
            ps = P.psum()
            for c in range(8):
                P.mm(ps.v(), ones_bf.v(), sq[:, c, :], start=(c == 0), stop=(c == 7))
            rsqrt_into(rstd.v(), ps.v(), 1.0 / D, EPS)
            for c in range(8):
                P.stt(xn[:, c, :], xt[:, c, :], col(gname, c), rstd.v(), ALU.mult, ALU.mult)

        for l in range(NL):
            A.release(m_layer)
            P.barrier()
            P.load("sp", cols.v(), cols_d[l, :, :])
            P.ts("dve", cols[:, COLS["omka"]:COLS["omka"] + 2], cols[:, COLS["ka"]:COLS["ka"] + 2], -1.0, ALU.mult,
                 1.0, ALU.add)
            xsrc_ap, xsrcR = (xT_in, [[Buf(None) for _ in range(NB)] for _ in range(8)]) if l == 0 else (xres_d, xresR)
            Vfox = A.alloc([128, NCH, 4, 65], BF16, "Vfox")
            m_phase = A.mark()

            Wb = A.alloc([128, 8, ZW], BF16, "Win")
            wst = [A.alloc([128, ZW], F32, "wst%d" % i) for i in range(2)]
            for c in range(8):
                P.load("sp", wst[c % 2].v(), w_in_d[l, c * 128:(c + 1) * 128, :])
                evac(Wb[:, c, :], wst[c % 2].v())
            P.memset("pool", Vfox[:, :, :, 64:65], 1.0)
            xts = [A.alloc([128, 8, 512], F32, "xt%d" % i) for i in range(2)]
            xns = [A.alloc([128, 8, 512], BF16, "xn%d" % i) for i in range(2)]
            sqs = [A.alloc([128, 8, 512], BF16, "sq%d" % i) for i in range(2)]
            rstds = [A.alloc([128, 512], F32, "rstd%d" % i) for i in range(2)]
            zst = [A.alloc([128, 512], F32, "zst%d" % i) for i in range(4)]
            zi = 0
            FOXV = 11 * 128
            def normA(b_):
                rmsnorm_block(xsrcR, xsrc_ap, "mixg", b_, xts[b_ % 2].v(), xns[b_ % 2], sqs[b_ % 2].v(), rstds[b_ % 2])

            normA(0)
            for blk in range(NB):
                t0 = blk * 512
                xt, xn, sq, rstd = xts[blk % 2], xns[blk % 2], sqs[blk % 2], rstds[blk % 2]
                if blk + 1 < NB:
                    normA(blk + 1)
                for oc in range(NZC):
                    ps = P.psum()
                    for c in range(8):
                        P.mm(ps.v(), Wb[:, c, oc * 128:(oc + 1) * 128], xn[:, c, :], start=(c == 0), stop=(c == 7))
                    zs_ = zst[zi % 4]
                    zi += 1
                    evac(zs_.v(), ps.v())
                    P.store("sp", zT_d[oc * 128:(oc + 1) * 128, t0:t0 + 512], zs_.v(), [zR[oc][blk]])
                for tt in range(4):
                    ps = P.psum()
                    for c in range(8):
                        P.mm(ps[:, 0:256], xn[:, c, tt * 128:(tt + 1) * 128], Wb[:, c, FOXV:FOXV + 256],
                             start=(c == 0), stop=(c == 7))
                    evac(Vfox[:, blk * 4 + tt, :, 0:64], ps[:, 0:256].rearrange("p (h d) -> p h d", h=4))
            if dbg and l == 0:
                P.barrier()
                tmp = A.alloc([128, 512], F32, "dbgtmp")
                for oc in range(NZC):
                    for blk in range(NB):
                        P.load("sp", tmp.v(), zT_d[oc * 128:(oc + 1) * 128, blk * 512:(blk + 1) * 512], [zR[oc][blk]])
                        P.store("sp", dbg_out["zT_o"][oc * 128:(oc + 1) * 128, blk * 512:(blk + 1) * 512], tmp.v(), final=True)
            P.barrier()
            A.release(m_phase)

            phase_rwkv(P, A, l, NB, cols, col, lr_d, zT_d, zR, mixT_d, mixR, ident, ident_bf, onesblk, tri, scanmask, rsqrt_into, evac)
            P.barrier()
            A.release(m_phase)

            phase_fox(P, A, l, NB, T, col, zT_d, zR, mixT_d, mixR, Vfox, onesblk, ones, ident_bf, amask, place, auxm,
                      ones512, rsqrt_into, evac)
            P.barrier()
            A.release(m_layer)
            m_phase = A.mark()

            phase_s5(P, A, l, NB, col, cols, zT_d, zR, mixT_d, mixR, ssmB_d, ssmC_d, glu_d, ones, iota, rsqrt_into, evac)
            P.barrier()
            A.release(m_phase)

            phase_mla(P, A, l, NB, T, col, zT_d, zR, mixT_d, mixR, wq_d, wkvK_d, wkvV_d, (cos_d, sin_d, ropeR), ones,
                      ident_bf, amask, rotP_bf, selkr_bf, invfreq, rsqrt_into, evac)
            P.barrier()
            A.release(m_phase)

            if dbg and l == 0:
                tmp = A.alloc([128, 512], F32, "dbgtmp")
                for oc in range(8):
                    for blk in range(NB):
                        P.load("sp", tmp.v(), mixT_d[oc * 128:(oc + 1) * 128, blk * 512:(blk + 1) * 512], [mixR[oc][blk]])
                        P.store("sp", dbg_out["mixT_o"][oc * 128:(oc + 1) * 128, blk * 512:(blk + 1) * 512], tmp.v(), final=True)
                P.barrier()
                A.release(m_phase)

            Wo = A.alloc([128, 8, D], BF16, "Wout")
            wst = [A.alloc([128, D], F32, "wsto%d" % i) for i in range(2)]
            for c in range(8):
                P.load("sp", wst[c % 2].v(), w_out_d[l, c * 128:(c + 1) * 128, :])
                evac(Wo[:, c, :], wst[c % 2].v())
            mixb = [A.alloc([128, 8, 512], BF16, "mixb%d" % i) for i in range(2)]
            xts = [A.alloc([128, 8, 512], F32, "xt%d" % i) for i in range(2)]
            zst = [A.alloc([128, 512], F32, "zst%d" % i) for i in range(4)]
            zi = 0
            def loadF(b_):
                tb_ = b_ * 512
                P.load("pool", mixb[b_ % 2].v(), mixT_d[:, tb_:tb_ + 512].rearrange("(c p) n -> p c n", p=128),
                       [mixR[c][b_] for c in range(8)])
                P.load("sp", xts[b_ % 2].v(), xsrc_ap[:, tb_:tb_ + 512].rearrange("(c p) n -> p c n", p=128),
                       [xsrcR[c][b_] for c in range(8)])

            loadF(0)
            for blk in range(NB):
                t0 = blk * 512
                mb, xt = mixb[blk % 2], xts[blk % 2]
                if blk + 1 < NB:
                    loadF(blk + 1)
                for oc in range(8):
                    ps = P.psum()
                    for c in range(8):
                        P.mm(ps.v(), Wo[:, c, oc * 128:(oc + 1) * 128], mb[:, c, :], start=(c == 0), stop=(c == 7))
                    zs_ = zst[zi % 4]
                    zi += 1
                    P.tt("dve", zs_.v(), ps.v(), xt[:, oc, :], ALU.add)
                    P.store("sp", xmid_d[oc * 128:(oc + 1) * 128, t0:t0 + 512], zs_.v(), [xmidR[oc][blk]])
                    if dbg and l == 0:
                        P.store("sp", dbg_out["xmid_o"][oc * 128:(oc + 1) * 128, t0:t0 + 512], zs_.v(), final=True)
            P.barrier()
            A.release(m_phase)

            W1 = A.alloc([128, 8, 4 * D], BF16, "W1")
            W2 = A.alloc([128, 32, D], BF16, "W2")
            m_st = A.mark()
            wst = [A.alloc([128, 4 * D], F32, "wstg%d" % i) for i in range(2)]
            for c in range(8):
                P.load("sp", wst[c % 2].v(), w_ff1_d[l, c * 128:(c + 1) * 128, :])
                evac(W1[:, c, :], wst[c % 2].v())
            for c4 in range(8):
                P.load("sp", wst[c4 % 2].v().rearrange("p (c n) -> p c n", c=4),
                       w_ff2_d[l, c4 * 512:(c4 + 1) * 512, :].rearrange("(c p) n -> p c n", p=128))
                evac(W2[:, c4 * 4:(c4 + 1) * 4, :], wst[c4 % 2].v().rearrange("p (c n) -> p c n", c=4))
            P.barrier()
            A.release(m_st)
            xn = A.alloc([128, 8, 512], BF16, "xn")
            rstd = A.alloc([128, 512], F32, "rstd")
            h1 = A.alloc([128, 32 * 512], BF16, "h1")
            xt = h1[:, 0:16 * 512].bitcast(F32).rearrange("p (a b) -> p a b", a=8)
            sq = h1[:, 16 * 512:24 * 512].rearrange("p (a b) -> p a b", a=8)
            h1 = h1.v().rearrange("p (a b) -> p a b", a=32)
            xr = [A.alloc([128, 512], F32, "xr%d" % i) for i in range(2)]
            rl = [A.alloc([128, 512], F32, "rl%d" % i) for i in range(2)]
            zst = [A.alloc([128, 512], F32, "zst%d" % i) for i in range(2)]
            last = (l == NL - 1)
            dst_ap, dstR = (outT, outR) if last else (xres_d, xresR)
            for blk in range(NB):
                t0 = blk * 512
                rmsnorm_block(xmidR, xmid_d, "mlpg", blk, xt, xn, sq, rstd)
                for f in range(32):
                    ps = P.psum()
                    for c in range(8):
                        P.mm(ps.v(), W1[:, c, f * 128:(f + 1) * 128], xn[:, c, :], start=(c == 0), stop=(c == 7))
                    r_ = rl[f % 2]
                    P.act(r_.v(), ps.v(), AF.Relu)
                    P.tt("pool" if f % 2 else "dve", h1[:, f, :], r_.v(), r_.v(), ALU.mult)
                for oc in range(8):
                    ps = P.psum()
                    for f in range(32):
                        P.mm(ps.v(), W2[:, f, oc * 128:(oc + 1) * 128], h1[:, f, :], start=(f == 0), stop=(f == 31))
                    zs_ = zst[oc % 2]
                    xr_ = xr[oc % 2]
                    P.load("sp", xr_.v(), xmid_d[oc * 128:(oc + 1) * 128, t0:t0 + 512], [xmidR[oc][blk]])
                    P.tt("dve", zs_.v(), ps.v(), xr_.v(), ALU.add)
                    P.store("sp", dst_ap[oc * 128:(oc + 1) * 128, t0:t0 + 512], zs_.v(), [dstR[oc][blk]], final=last)
            P.barrier()
        P.emit()
    return nc


def phase_rwkv(P, A, l, NB, cols, col, lr_d, zT_d, zR, mixT_d, mixR, ident, ident_bf, onesblk, tri, scanmask, rsqrt_into, evac):
    LR = A.alloc([128, 256], F32, "LR")
    P.load("sp", LR.v(), lr_d[l, :, :])
    zbs = [A.alloc([128, 7, 513], F32, "zb%d" % i) for i in range(2)]
    dtmp = A.alloc([128, 7, 512], F32, "dtmp")
    zs = A.alloc([128, 7, 512], F32, "zs")
    lowr = A.alloc([128, 512], F32, "lowr")

    def t2(name):
        return A.alloc([128, 2, 512], F32, name)

    sig, a_, g_, kk, sqk, rn, t1, kmod, Lc, Ep, Em, Eq, yT, yc, tmpb = [
        t2(n) for n in ("sig", "a", "g", "kk", "sqk", "rn", "t1", "kmod", "Lc", "Ep", "Em", "Eq",
                        "yT", "yc", "tmpb")]
    b_, Lp = yc, tmpb
    CD = RWKV_CD
    identx = ident_bf if CD == BF16 else ident
    bett = A.alloc([128, 2, 512], CD, "bett")
    ktil = A.alloc([128, 2, 512], CD, "ktil")
    vb = A.alloc([128, 2, 512], CD, "vb")
    KR = A.alloc([128, 2, 8, 128], CD, "KR")
    ST = [A.alloc([128, 64], F32, "ST%d" % i) for i in range(2)]
    STb = [A.alloc([128, 64], CD, "STb%d" % i) for i in range(2)]
    for s_ in ST + STb:
        P.memset("pool", s_.v(), 0.0)
    NR = 4
    ABR = [A.alloc([128, 64], CD, "ABR%d" % i) for i in range(NR)]
    AK = [A.alloc([128, 128], CD, "AK%d" % i) for i in range(NR)]
    TK = [A.alloc([128, 192], CD, "TK%d" % i) for i in range(NR)]
    DG = [A.alloc([128, 128], CD, "DG%d" % i) for i in range(NR)]
    nWT = [A.alloc([128, 64], RWKV_CHD, "nWT%d" % i) for i in range(NR)]
    UT = [A.alloc([128, 64], CD, "UT%d" % i) for i in range(NR)]
    Qm = [A.alloc([128, 128], RWKV_CHD, "Qm%d" % i) for i in range(NR)]
    PW = [[A.alloc([128, 128], RWKV_CHD, "PW%d_%d" % (i, j)) for j in range(4)] for i in range(NR)]
    for i in range(NR):
        for j in range(4):
            P.memset("pool", PW[i][j].v(), 0.0)
    stage = [A.alloc([128, 512], F32, "stg%d" % i) for i in range(2)]
    it = 0
    for blk in range(NB):
        t0 = blk * 512
        zb = zbs[blk % 2]
        P.load("sp", zb[:, :, 1:513], zT_d[0:896, t0:t0 + 512].rearrange("(c p) n -> p c n", p=128),
               [zR[c][blk] for c in range(7)])
        if blk == 0:
            P.memset("pool", zb[:, :, 0:1], 0.0)
        else:
            P.copy("pool", zb[:, :, 0:1], zbs[(blk - 1) % 2][:, :, 512:513])
        P.tt("dve", dtmp.v(), zb[:, :, 0:512], zb[:, :, 1:513], ALU.subtract)
        for c in range(7):
            P.stt(zs[:, c, :], dtmp[:, c, :], col("mu", c), zb[:, c, 1:513], ALU.mult, ALU.add)
        r_, k_, v_ = zs[:, 0:2, :], zs[:, 2:4, :], zs[:, 4:6, :]
        P.act(lowr[0:32, :], zs[0:32, 6, :], AF.Tanh)
        P.act(lowr[64:128, :], zs[64:128, 6, :], AF.Sigmoid)
        for h in range(2):
            ps = P.psum()
            P.mm(ps.v(), LR[0:32, h * 128:(h + 1) * 128], lowr[0:32, :])
            P.act(sig[:, h, :], ps.v(), AF.Sigmoid, bias=col("w0", h))
            ps = P.psum()
            P.mm(ps.v(), LR[32:64, h * 128:(h + 1) * 128], zs[32:64, 6, :])
            P.act(a_[:, h, :], ps.v(), AF.Sigmoid, bias=col("a0", h))
            ps = P.psum()
            P.mm(ps.v(), LR[64:128, h * 128:(h + 1) * 128], lowr[64:128, :])
            P.copy("act", g_[:, h, :], ps.v())
            P.ts("dve", kk[:, h, :], zs[:, 2 + h, :], col("kk", h), ALU.mult)
        P.act(sqk.v(), kk.v(), AF.Square)
        for h in range(2):
            ps = P.psum()
            P.mm(ps.v(), onesblk, sqk[:, h, :])
            P.act(rn[:, h, :], ps.v(), AF.Sqrt)
        P.ts("dve", rn.v(), rn.v(), 1e-12, ALU.max)
        P.recip(rn.v(), rn.v())
        P.tt("dve", kk.v(), kk.v(), rn.v(), ALU.mult)
        P.tt("pool", b_.v(), kk.v(), a_.v(), ALU.mult)
        for h in range(2):
            P.ts("dve", t1[:, h, :], a_[:, h, :], col("ka", h), ALU.mult, col("omka", h), ALU.add)
        P.tt("pool", kmod.v(), k_, t1.v(), ALU.mult)
        for h in range(2):
            P.scan(Lc[:, h, :], scanmask, sig[:, h, :], 0.0)
        P.tt("pool", Lp.v(), Lc.v(), sig.v(), ALU.subtract)
        P.act(Ep.v(), Lc.v(), AF.Exp, scale=-C0)
        P.act(Em.v(), Lc.v(), AF.Exp, scale=C0)
        P.act(Eq.v(), Lp.v(), AF.Exp, scale=-C0)
        ch = "p h (c s) -> p h c s"
        P.tt("dve", KR[:, :, :, 0:64], kk.v().rearrange(ch, s=64), Eq.v().rearrange(ch, s=64), ALU.mult)
        P.tt("dve", KR[:, :, :, 64:128], r_.rearrange(ch, s=64), Ep.v().rearrange(ch, s=64), ALU.mult)
        P.tt("pool", bett.v(), b_.v(), Em.v(), ALU.mult)
        P.tt("pool", ktil.v(), kmod.v(), Em.v(), ALU.mult)
        P.copy("pool", vb.v(), zs[:, 4:6, :])
        def make_chunk(c):
            cs = slice(c * 64, (c + 1) * 64)
            ctxs = []

            def front():
                nonlocal it
                for hp in range(2):
                    i = it % NR
                    it += 1
                    gam = Ep[:, hp, c * 64 + 63:c * 64 + 64]
                    G1, G2, G3, G4 = P.psum(), P.psum(), P.psum(), P.psum()
                    for h2 in range(2):
                        pb = slice(64 * h2, 64 * h2 + 64)
                        P.mm(G1[pb, 0:128], bett[pb, hp, cs], KR[pb, hp, c, :])
                        P.mm(G2[pb, 0:128], ktil[pb, hp, cs], KR[pb, hp, c, :])
                        P.mm(G3[pb, 0:64], KR[pb, hp, c, 0:64], bett[pb, hp, cs])
                    P.ts("dve", DG[i].v(), ident, gam, ALU.mult)
                    for h2 in range(2):
                        pb = slice(64 * h2, 64 * h2 + 64)
                        P.mm(G4[pb, 0:64], vb[pb, hp, cs], identx[pb, pb])
                        P.mm(G4[pb, 64:128], bett[pb, hp, cs], DG[i][pb, pb])
                        P.mm(G4[pb, 128:192], ktil[pb, hp, cs], DG[i][pb, pb])
                    cur, curT, nxt, nxtT = PW[i]
                    for h2 in range(2):
                        pb = slice(64 * h2, 64 * h2 + 64)
                        P.tt("dve", cur[pb, pb], G1[pb, 0:64], tri[pb, 0:64], ALU.mult)
                        P.tt("dve", curT[pb, pb], G3[pb, 0:64], tri[pb, 128:192], ALU.mult)
                    P.tt("dve", Qm[i].v(), ident, cur.v(), ALU.subtract)
                    P.tt("dve", ABR[i].v(), G1[:, 64:128], tri[:, 64:128], ALU.mult)
                    P.tt("dve", AK[i].v(), G2[:, 0:128], tri[:, 0:128], ALU.mult)
                    P.copy("act", TK[i].v(), G4[:, 0:192])
                    ctxs.append(dict(i=i, hp=hp, gam=gam, pw=[cur, curT, nxt, nxtT]))

            def chain(k):
                def f():
                    for cx in ctxs:
                        cur, curT, nxt, nxtT = cx["pw"]
                        if k < 4:
                            pA = P.psum()
                            P.mm(pA[:, 0:128], curT.v(), cur.v())
                            P.copy("act", nxt.v(), pA[:, 0:128])
                        pB = P.psum()
                        P.mm(pB[:, 0:128], cur.v(), curT.v())
                        P.copy("act", nxtT.v(), pB[:, 0:128])
                    for cx in ctxs:
                        cur, curT, nxt, nxtT = cx["pw"]
                        i = cx["i"]
                        pQ = P.psum()
                        P.mm(pQ[:, 0:128], nxtT.v(), Qm[i].v())
                        P.tt("dve", Qm[i].v(), Qm[i].v(), pQ[:, 0:128], ALU.add)
                        cx["pw"] = [nxt, nxtT, cur, curT]
                return f

            def s1():
                for cx in ctxs:
                    i, hp = cx["i"], cx["hp"]
                    WT = P.psum()
                    for h2 in range(2):
                        pb = slice(64 * h2, 64 * h2 + 64)
                        P.mm(WT[pb, 0:64], KR[pb, hp, c, 0:64], STb[hp][pb, :], start=True, stop=False)
                        P.mm(WT[pb, 0:64], AK[i][pb, 0:64], TK[i][pb, 0:64], start=False, stop=True)
                    P.ts("dve", nWT[i].v(), WT[:, 0:64], -1.0, ALU.mult)

            def s2():
                for cx in ctxs:
                    i, hp = cx["i"], cx["hp"]
                    UTp = P.psum()
                    P.mm(UTp[:, 0:64], Qm[i].v(), nWT[i].v())
                    P.copy("act", UT[i].v(), UTp[:, 0:64])

            def s3():
                for cx in ctxs:
                    i, hp, gam = cx["i"], cx["hp"], cx["gam"]
                    Yp, SN = P.psum(), P.psum()
                    for h2 in range(2):
                        pb = slice(64 * h2, 64 * h2 + 64)
                        P.mm(SN[pb, 0:64], TK[i][pb, 64:128], UT[i][pb, :], start=True, stop=False)
                        P.mm(SN[pb, 0:64], TK[i][pb, 128:192], TK[i][pb, 0:64], start=False, stop=True)
                        P.mm(Yp[pb, 0:64], STb[hp][pb, :], KR[pb, hp, c, 64:128], start=True, stop=False)
                        P.mm(Yp[pb, 0:64], UT[i][pb, :], ABR[i][pb, :], start=False, stop=False)
                        P.mm(Yp[pb, 0:64], TK[i][pb, 0:64], AK[i][pb, 64:128], start=False, stop=True)
                    P.stt(ST[hp].v(), ST[hp].v(), gam, SN[:, 0:64], ALU.mult, ALU.add)
                    P.copy("pool", STb[hp].v(), ST[hp].v())
                    P.copy("act", yT[:, hp, cs], Yp[:, 0:64])

            return [front] + [chain(k) for k in range(5)], [s1, s2, s3]

        chunks = [make_chunk(c) for c in range(8)]
        for f in chunks[0][0]:
            f()
        for c in range(8):
            Aq = list(chunks[c + 1][0]) if c + 1 < 8 else []
            Bq = list(chunks[c][1])
            order = ["A", "B", "A", "A", "B", "A", "A", "B", "A"]
            for o in order:
                if o == "A" and Aq:
                    Aq.pop(0)()
                elif o == "B" and Bq:
                    Bq.pop(0)()
            for f in Aq + Bq:
                f()
        for hp in range(2):
            ps = P.psum()
            P.mm(ps.v(), onesblk, yT[:, hp, :])
            P.stt(yc[:, hp, :], ps.v(), -1.0 / 64, yT[:, hp, :], ALU.mult, ALU.add)
        P.act(sqk.v(), yc.v(), AF.Square)
        for hp in range(2):
            ps = P.psum()
            P.mm(ps.v(), onesblk, sqk[:, hp, :])
            rsqrt_into(rn[:, hp, :], ps.v(), 1.0 / 64, 64e-5)
        P.tt("dve", yc.v(), yc.v(), rn.v(), ALU.mult)
        P.tt("pool", tmpb.v(), r_, kmod.v(), ALU.mult)
        for hp in range(2):
            P.ts("dve", yc[:, hp, :], yc[:, hp, :], col("lng", hp), ALU.mult, col("lnb", hp), ALU.add)
            P.ts("pool", tmpb[:, hp, :], tmpb[:, hp, :], col("rk", hp), ALU.mult)
            ps = P.psum()
            P.mm(ps.v(), onesblk, tmpb[:, hp, :])
            P.tt("dve", t1[:, hp, :], ps.v(), zs[:, 4 + hp, :], ALU.mult)
        P.tt("dve", yc.v(), yc.v(), t1.v(), ALU.add)
        for hp in range(2):
            sg_ = stage[hp]
            P.tt("dve", sg_.v(), yc[:, hp, :], g_[:, hp, :], ALU.mult)
            P.store("sp", mixT_d[hp * 128:(hp + 1) * 128, t0:t0 + 512], sg_.v(), [mixR[hp][blk]])


def attention(P, A, NB, Kt, Qt, krows, Vt, ident_bf, amask, ones, ycol, og_name, col, mix_row0, mixT_d, mixR,
              rsqrt_into, evac):
    PT = [A.alloc([128, 512], BF16, "PT%d" % i) for i in range(4)]
    Osb = [A.alloc([65, 512], F32, "Osb%d" % i) for i in range(2)]
    YF = A.alloc([64, 4, 512], F32, "YF")
    sqy = A.alloc([64, 4, 512], F32, "sqy")
    rst = A.alloc([64, 512], F32, "rst")
    stg = [A.alloc([64, 512], F32, "astg%d" % i) for i in range(2)]
    items = [(qb, h, kc) for qb in range(NB) for h in range(4) for kc in range(4 * (qb + 1))]
    LOOK = 2
    state = {"pi": 0, "O": None}

    def issue(item):
        qb, h, kc = item
        qs = slice(qb * 512, (qb + 1) * 512)
        S = P.psum()
        diag = kc >= 4 * qb
        P.mm(S.v(), Kt[0:krows, h, kc * 128:(kc + 1) * 128], Qt[0:krows, h, qs], start=True, stop=not diag)
        if diag:
            P.mm(S.v(), ident_bf.v(), amask[:, kc - 4 * qb, :], start=False, stop=True)
        return S

    def finish(item, S):
        qb, h, kc = item
        qs = slice(qb * 512, (qb + 1) * 512)
        nk = 4 * (qb + 1)
        if kc == 0:
            state["O"] = P.psum_acc()
        O = state["O"]
        pt = PT[state["pi"] % 4]
        state["pi"] += 1
        P.act(pt.v(), S.v(), AF.Exp)
        P.mm(O[0:65, :], Vt[:, kc, h, :], pt.v(), start=(kc == 0), stop=(kc == nk - 1))
        if kc != nk - 1:
            return
        ob = Osb[h % 2]
        P.copy("act", ob.v(), O[0:65, :])
        P.recip(ob[64:65, :], ob[64:65, :])
        bc = P.psum()
        P.mm(bc[0:64, :], ones[64:65, 0:64], ob[64:65, :])
        P.tt("dve", YF[:, h, :], ob[0:64, :], bc[0:64, :], ALU.mult)
        if h != 3:
            return
        P.act(sqy.v(), YF.v(), AF.Square)
        ps = P.psum()
        for hh in range(4):
            P.mm(ps[0:64, :], ones[0:64, 0:64], sqy[:, hh, :], start=(hh == 0), stop=(hh == 3))
        rsqrt_into(rst.v(), ps[0:64, :], 1.0 / 256, EPS)
        for hh in range(4):
            sg_ = stg[hh % 2]
            P.stt(sg_.v(), YF[:, hh, :], col(og_name, hh, 0, 64), rst.v(), ALU.mult, ALU.mult)
            r0 = mix_row0 + 64 * hh
            P.store("sp", mixT_d[r0:r0 + 64, qs], sg_.v(), [mixR[r0 // 128][qb]])

    pend = []
    for item in items:
        pend.append((item, issue(item)))
        if len(pend) > LOOK:
            finish(*pend.pop(0))
    while pend:
        finish(*pend.pop(0))


def phase_fox(P, A, l, NB, T, col, zT_d, zR, mixT_d, mixR, Vfox, onesblk, ones, ident_bf, amask, place, auxm,
              ones512, rsqrt_into, evac):
    Qp = A.alloc([128, 4, T], BF16, "Qp")
    Kp = A.alloc([128, 4, T], BF16, "Kp")
    P.memset("pool", Qp.v(), 0.0)
    P.memset("pool", Kp.v(), 0.0)
    m = A.mark()
    zq = [A.alloc([128, 4, 512], F32, "zq0")] * 2
    fz = [A.alloc([4, 512], F32, "fz%d" % i) for i in range(2)]
    sq = A.alloc([128, 4, 512], F32, "fsq")
    rb = A.alloc([128, 4, 512], F32, "frb")
    lf = A.alloc([4, 512], F32, "lf")
    C6 = [[A.alloc([128, 512], F32, "C6_%d_%d" % (h, i)) for i in range(2)] for h in range(4)]
    P6 = [A.alloc([128, 512], F32, "P6s%d" % h) for h in range(4)]
    Hb, Mb, Lb = [[A.alloc([128, 512], BF16, "%s%d" % (n, h)) for h in range(4)] for n in ("Hb", "Mb", "Lb")]
    r1, r2, tq = [[A.alloc([128, 512], F32, "%s%d" % (n, h)) for h in range(4)] for n in ("r1", "r2", "tq")]
    for blk in range(NB):
        t0 = blk * 512
        ts_ = slice(t0, t0 + 512)
        z = zq[blk % 2]
        f_ = fz[blk % 2]
        P.load("sp", z.v(), zT_d[7 * 128:11 * 128, ts_].rearrange("(c p) n -> p c n", p=128),
               [zR[c][blk] for c in range(7, 11)])
        P.load("sp", f_.v(), zT_d[13 * 128:13 * 128 + 4, ts_], [zR[13][blk]])
        P.act(sq.v(), z.v(), AF.Square)
        for c4 in range(4):
            ps = P.psum()
            P.mm(ps.v(), onesblk, sq[:, c4, :])
            if c4 < 2:
                rsqrt_into(rb[:, c4, :], ps.v(), 1.0, 64 * EPS)
            else:
                rsqrt_into(rb[:, c4, :], ps.v(), 1.0 / 64, EPS)
        for c4 in range(4):
            for h2 in range(2):
                pb = slice(64 * h2, 64 * h2 + 64)
                h = 2 * (c4 % 2) + h2
                dst = (Qp if c4 < 2 else Kp)[pb, h, ts_]
                g = col("foxqg" if c4 < 2 else "foxkg", 0, 64 * h2, 64 * h2 + 64)
                P.stt(dst, z[pb, c4, :], g, rb[pb, c4, :], ALU.mult, ALU.mult)
        P.act(lf.v(), f_.v(), AF.Sigmoid, bias=col("foxfb", 0, 0, 4))
        P.act(lf.v(), lf.v(), AF.Ln)
        ars = [slice(64 if h % 2 == 0 else 0, (64 if h % 2 == 0 else 0) + 32) for h in range(4)]
        for h in range(4):
            ar = ars[h]
            ps = P.psum()
            P.mm(ps.v(), place[0:4, h, :], lf.v())
            P.copy("act", P6[h][ar, :], ps[ar, :])
        for h in range(4):
            ar = ars[h]
            init = 0.0 if blk == 0 else C6[h][(blk - 1) % 2][ar, 511:512]
            P.scan(C6[h][blk % 2][ar, :], ones512[ar, :], P6[h][ar, :], init)
        for h in range(4):
            P.copy("act", Hb[h][ars[h], :], C6[h][blk % 2][ars[h], :])
        for h in range(4):
            P.tt("pool", r1[h][ars[h], :], C6[h][blk % 2][ars[h], :], Hb[h][ars[h], :], ALU.subtract)
        for h in range(4):
            P.copy("act", Mb[h][ars[h], :], r1[h][ars[h], :])
        for h in range(4):
            P.tt("pool", r2[h][ars[h], :], r1[h][ars[h], :], Mb[h][ars[h], :], ALU.subtract)
        for h in range(4):
            P.copy("act", Lb[h][ars[h], :], r2[h][ars[h], :])
        for h in range(4):
            ar, par = ars[h], h % 2
            P.ts("dve", tq[h][ar, :], Hb[h][ar, :], auxm[ar, par, 0:1], ALU.mult)
            P.stt(tq[h][ar, :], Mb[h][ar, :], auxm[ar, par, 1:2], tq[h][ar, :], ALU.mult, ALU.add)
        for h in range(4):
            ar, par = ars[h], h % 2
            P.stt(tq[h][ar, :], Lb[h][ar, :], auxm[ar, par, 2:3], tq[h][ar, :], ALU.mult, ALU.add)
        for h in range(4):
            ar, par = ars[h], h % 2
            P.ts("dve", Qp[ar, h, ts_], tq[h][ar, :], auxm[ar, par, 3:4], ALU.mult, auxm[ar, par, 4:5], ALU.add)
            P.ts("pool", Kp[ar, h, ts_], tq[h][ar, :], auxm[ar, par, 5:6], ALU.mult, auxm[ar, par, 6:7], ALU.add)
    A.release(m)
    attention(P, A, NB, Kp, Qp, 128, Vfox, ident_bf, amask, ones, None, "og_fox", col, 256, mixT_d, mixR,
              rsqrt_into, evac)


def range_reduce(P, out, ang, ki, kf):
    P.ts("dve", ki, ang, 1.0 / (2 * math.pi), ALU.mult)
    P.copy("dve", kf, ki)
    P.stt(out, kf, -6.28125, ang, ALU.mult, ALU.add)
    P.stt(out, kf, -(2 * math.pi - 6.28125), out, ALU.mult, ALU.add)
    P.ts("dve", kf, out, math.pi, ALU.is_gt)
    P.stt(out, kf, -2 * math.pi, out, ALU.mult, ALU.add)
    P.ts("dve", kf, out, -math.pi, ALU.is_lt)
    P.stt(out, kf, 2 * math.pi, out, ALU.mult, ALU.add)
    P.ts("dve", out, out, math.pi, ALU.min, -math.pi, ALU.max)


def phase_s5(P, A, l, NB, col, cols, zT_d, zR, mixT_d, mixR, ssmB_d, ssmC_d, glu_d, ones, iota, rsqrt_into, evac):
    SB = A.alloc([128, 16, 128], S5_CD, "ssmB")
    SC = A.alloc([128, 16, 128], S5_CD, "ssmC")
    GL = A.alloc([128, 2, 128], S5_CD, "glu")
    P.load("pool", SB.v(), ssmB_d[l, :, :, :])
    P.load("pool", SC.v(), ssmC_d[l, :, :, :])
    P.load("pool", GL.v(), glu_d[l, :, :, :])
    c_re, c_im, c_dt = COLS["slre"], COLS["slim"], COLS["sldt"]

    def s8(name):
        return A.alloc([128, 8], F32, name)

    dt, lr, lrdt, th, rho, sn, cs, abr1, abi, den, fr, fi, t8, u8 = [s8(n) for n in (
        "dt", "lr", "lrdt", "th", "rho", "sn", "cs", "abr1", "abi", "den", "fr", "fi", "t8", "u8")]
    ki8 = A.alloc([128, 8], I32, "ki8")
    kf8 = s8("kf8")
    li = cols[:, c_im:c_im + 8]
    P.act(dt.v(), cols[:, c_dt:c_dt + 8], AF.Exp)
    P.ts("dve", lr.v(), cols[:, c_re:c_re + 8], -1e-4, ALU.min)
    P.tt("dve", lrdt.v(), lr.v(), dt.v(), ALU.mult)
    P.tt("dve", th.v(), li, dt.v(), ALU.mult)
    P.act(rho.v(), lrdt.v(), AF.Exp)
    range_reduce(P, t8.v(), th.v(), ki8.v(), kf8.v())
    P.act(sn.v(), t8.v(), AF.Sin)
    P.ts("dve", u8.v(), th.v(), math.pi / 2, ALU.add)
    range_reduce(P, t8.v(), u8.v(), ki8.v(), kf8.v())
    P.act(cs.v(), t8.v(), AF.Sin)
    P.tt("dve", abr1.v(), rho.v(), cs.v(), ALU.mult)
    P.ts("dve", abr1.v(), abr1.v(), -1.0, ALU.add)
    P.tt("dve", abi.v(), rho.v(), sn.v(), ALU.mult)
    P.tt("dve", den.v(), lr.v(), lr.v(), ALU.mult)
    P.tt("dve", t8.v(), li, li, ALU.mult)
    P.tt("dve", den.v(), den.v(), t8.v(), ALU.add)
    P.recip(den.v(), den.v())
    P.tt("dve", fr.v(), abr1.v(), lr.v(), ALU.mult)
    P.tt("dve", t8.v(), abi.v(), li, ALU.mult)
    P.tt("dve", fr.v(), fr.v(), t8.v(), ALU.add)
    P.tt("dve", fr.v(), fr.v(), den.v(), ALU.mult)
    P.tt("dve", fi.v(), abi.v(), lr.v(), ALU.mult)
    P.tt("dve", t8.v(), abr1.v(), li, ALU.mult)
    P.tt("dve", fi.v(), fi.v(), t8.v(), ALU.subtract)
    P.tt("dve", fi.v(), fi.v(), den.v(), ALU.mult)
    ANG = A.alloc([128, 8, 512], F32, "ANG")
    RED = A.alloc([128, 8, 512], F32, "RED")
    KF = A.alloc([128, 8, 512], F32, "KF")
    SINT = A.alloc([128, 8, 512], F32, "SINT")
    COST = A.alloc([128, 8, 512], F32, "COST")
    m_ki = A.mark()
    KI = A.alloc([128, 8, 512], I32, "KI")
    for i in range(8):
        P.ts("dve", ANG[:, i, :], iota, th[:, i:i + 1], ALU.mult)
    range_reduce(P, RED.v(), ANG.v(), KI.v(), KF.v())
    P.act(SINT.v(), RED.v(), AF.Sin)
    P.ts("dve", ANG.v(), ANG.v(), math.pi / 2, ALU.add)
    range_reduce(P, RED.v(), ANG.v(), KI.v(), KF.v())
    P.act(COST.v(), RED.v(), AF.Sin)
    TinR, TinI, RHO = ANG, RED, KF
    for i in range(8):
        P.ts("dve", TinR[:, i, :], COST[:, i, :], fr[:, i:i + 1], ALU.mult)
        P.stt(TinR[:, i, :], SINT[:, i, :], fi[:, i:i + 1], TinR[:, i, :], ALU.mult, ALU.add)
        P.ts("dve", TinI[:, i, :], SINT[:, i, :], fr[:, i:i + 1], ALU.mult)
        P.stt(TinI[:, i, :], COST[:, i, :], fi[:, i:i + 1], TinI[:, i, :], ALU.mult, ALU.subtract)
        P.ts("dve", RHO[:, i, :], iota, 0.0, ALU.mult, rho[:, i:i + 1], ALU.add)
    P.barrier()
    A.release(m_ki)
    car_r = [s8("car_r0"), s8("car_r1")]
    car_i = [s8("car_i0"), s8("car_i1")]
    ub = [A.alloc([128, 2, 512], S5_CD, "ub%d" % i) for i in range(2)]

    def w5(name, dt=F32):
        return A.alloc([128, 512], dt, name)

    NS = 3
    ta, tb, tc, td, wr, wi, zr, zi_ = [[w5("%s%d" % (n, i)) for i in range(NS)] for n in (
        "ta", "tb", "tc", "td", "wr", "wi", "zr", "zi")]
    xr, xin = [[w5("%s%d" % (n, i), S5_CD) for i in range(NS)] for n in ("xr", "xin")]
    YS = A.alloc([128, 2, 512], F32, "YS")
    yp, sqs = w5("yp"), A.alloc([128, 2, 512], F32, "sqs")
    yg = w5("yg", S5_CD)
    rst = w5("rst")
    stg = [w5("sstg0"), w5("sstg1")]
    n = 0
    for blk in range(NB):
        t0 = blk * 512
        ts_ = slice(t0, t0 + 512)
        u = ub[blk % 2]
        P.load("pool", u.v(), zT_d[14 * 128:16 * 128, ts_].rearrange("(c p) n -> p c n", p=128),
               [zR[14][blk], zR[15][blk]])
        cr_o, ci_o = car_r[(blk + 1) % 2], car_i[(blk + 1) % 2]
        cr_n, ci_n = car_r[blk % 2], car_i[blk % 2]
        Ys = [P.psum_acc(), P.psum_acc()]

        def stA(i, j):
            chn = i // 4
            PR, PI = P.psum(), P.psum()
            P.mm(PR.v(), SB[:, i, :], u[:, chn, :])
            P.mm(PI.v(), SB[:, 8 + i, :], u[:, chn, :])
            P.tt("dve", ta[j].v(), PR.v(), TinR[:, i, :], ALU.mult)
            P.tt("dve", tb[j].v(), PI.v(), TinI[:, i, :], ALU.mult)
            P.tt("pool", wr[j].v(), ta[j].v(), tb[j].v(), ALU.subtract)
            P.tt("dve", tc[j].v(), PR.v(), TinI[:, i, :], ALU.mult)
            P.tt("dve", td[j].v(), PI.v(), TinR[:, i, :], ALU.mult)
            P.tt("pool", wi[j].v(), tc[j].v(), td[j].v(), ALU.add)

        def stB(i, j):
            ir = 0.0 if blk == 0 else cr_o[:, i:i + 1]
            ii = 0.0 if blk == 0 else ci_o[:, i:i + 1]
            P.scan(zr[j].v(), RHO[:, i, :], wr[j].v(), ir)
            P.scan(zi_[j].v(), RHO[:, i, :], wi[j].v(), ii)
            P.tt("pool", ta[j].v(), zr[j].v(), COST[:, i, :], ALU.mult)
            P.tt("pool", tb[j].v(), zi_[j].v(), SINT[:, i, :], ALU.mult)
            P.tt("pool", tc[j].v(), zr[j].v(), SINT[:, i, :], ALU.mult)
            P.tt("dve", td[j].v(), zi_[j].v(), COST[:, i, :], ALU.mult)

        def stC(i, j):
            chn, i4 = i // 4, i % 4
            Y = Ys[chn]
            P.tt("pool", xr[j].v(), ta[j].v(), tb[j].v(), ALU.subtract)
            P.tt("pool", cr_n[:, i:i + 1], ta[j][:, 511:512], tb[j][:, 511:512], ALU.subtract)
            P.stt(xin[j].v(), tc[j].v(), -1.0, td[j].v(), ALU.mult, ALU.subtract)
            P.tt("pool", ci_n[:, i:i + 1], tc[j][:, 511:512], td[j][:, 511:512], ALU.add)
            P.mm(Y.v(), SC[:, i, :], xr[j].v(), start=(i4 == 0), stop=False)
            P.mm(Y.v(), SC[:, 8 + i, :], xin[j].v(), start=False, stop=(i4 == 3))
            if i4 == 3:
                P.stt(yp.v(), u[:, chn, :], col("ssmd", chn), Y.v(), ALU.mult, ALU.add)
                P.act(yg.v(), yp.v(), AF.Gelu_apprx_tanh)
                ps = P.psum()
                P.mm(ps.v(), GL[:, chn, :], yg.v())
                P.act(yp.v(), ps.v(), AF.Sigmoid)
                P.tt("dve", YS[:, chn, :], yg.v(), yp.v(), ALU.mult)

        js = [(n + i) % NS for i in range(8)]
        n += 8
        for step in range(8 + 2):
            if step < 8:
                stA(step, js[step])
            if 0 <= step - 1 < 8:
                stB(step - 1, js[step - 1])
            if 0 <= step - 2 < 8:
                stC(step - 2, js[step - 2])
        P.act(sqs.v(), YS.v(), AF.Square)
        ps = P.psum()
        P.mm(ps.v(), ones, sqs[:, 0, :], start=True, stop=False)
        P.mm(ps.v(), ones, sqs[:, 1, :], start=False, stop=True)
        rsqrt_into(rst.v(), ps.v(), 1.0 / 256, EPS)
        for chn in range(2):
            P.stt(stg[chn].v(), YS[:, chn, :], col("og_ssm", chn), rst.v(), ALU.mult, ALU.mult)
            P.store("sp", mixT_d[(4 + chn) * 128:(5 + chn) * 128, ts_], stg[chn].v(), [mixR[4 + chn][blk]])


def phase_mla(P, A, l, NB, T, col, zT_d, zR, mixT_d, mixR, wq_d, wkvK_d, wkvV_d, pos_in, ones, ident_bf, amask,
              rotP_bf, selkr_bf, invfreq, rsqrt_into, evac):
    NCH = T // 128
    Qm = A.alloc([128, 4, T], BF16, "Qmla")
    Km = A.alloc([128, 4, T], BF16, "Kmla")
    Vm = A.alloc([128, NCH, 4, 65], BF16, "Vmla")
    P.memset("pool", Vm[:, :, :, 64:65], 1.0)
    m = A.mark()
    Wq = A.alloc([128, 2, 384], BF16, "Wq")
    WkK = A.alloc([128, 4, 96], BF16, "WkK")
    WkV = A.alloc([128, 256], BF16, "WkV")
    P.load("pool", Wq.v(), wq_d[l, :, :, :])
    P.load("pool", WkK.v(), wkvK_d[l, :, :, :])
    P.load("pool", WkV.v(), wkvV_d[l, :, :])

    def w5(name, dt=F32, p=128):
        return A.alloc([p, 512], dt, name)

    z16, z17, z18 = [[w5("%s_%d" % (n, i)) for i in range(2)] for n in ("z16", "z17", "z18")]
    z13b = [w5("z13b%d" % i, BF16, 64) for i in range(2)]
    cos_d, sin_d, ropeR = pos_in
    COSs = [w5("COS%d" % i, F32, 96) for i in range(2)]
    SINs = [w5("SIN%d" % i, F32, 96) for i in range(2)]
    sq16, sq17, sq18, rs192, rs128 = [w5(n) for n in ("sq16", "sq17", "sq18", "rs192", "rs128")]
    cqn = A.alloc([128, 2, 512], BF16, "cqn")
    ckvn = w5("ckvn", BF16)
    raw, sqr, rbq, qn, tc_ = [[w5("%s%d" % (n, i), F32, 96) for i in range(4)] for n in ("raw", "sqr", "rbq", "qn", "tc")]
    qnb = [w5("qnb%d" % i, BF16, 96) for i in range(4)]
    n = 0
    for blk in range(NB):
        t0 = blk * 512
        ts_ = slice(t0, t0 + 512)
        j2 = blk % 2
        P.load("sp", z16[j2].v(), zT_d[16 * 128:17 * 128, ts_], [zR[16][blk]])
        P.load("sp", z17[j2][0:64, :], zT_d[17 * 128:17 * 128 + 64, ts_], [zR[17][blk]])
        P.load("sp", z18[j2].v(), zT_d[18 * 128:19 * 128, ts_], [zR[18][blk]])
        P.load("pool", z13b[j2].v(), zT_d[13 * 128:13 * 128 + 64, ts_], [zR[13][blk]])
        COS, SIN = COSs[j2], SINs[j2]
        P.load("sp", COS.v(), cos_d[:, ts_], [ropeR[0][blk]])
        P.load("sp", SIN.v(), sin_d[:, ts_], [ropeR[1][blk]])
        P.act(sq16.v(), z16[j2].v(), AF.Square)
        P.act(sq17[0:64, :], z17[j2][0:64, :], AF.Square)
        P.act(sq18.v(), z18[j2].v(), AF.Square)
        ps = P.psum()
        P.mm(ps.v(), ones, sq16.v(), start=True, stop=False)
        P.mm(ps.v(), ones[0:64, :], sq17[0:64, :], start=False, stop=True)
        rsqrt_into(rs192.v(), ps.v(), 1.0 / 192, EPS)
        ps = P.psum()
        P.mm(ps.v(), ones, sq18.v())
        rsqrt_into(rs128.v(), ps.v(), 1.0 / 128, EPS)
        P.stt(cqn[:, 0, :], z16[j2].v(), col("mqlg", 0), rs192.v(), ALU.mult, ALU.mult)
        P.stt(cqn[0:64, 1, :], z17[j2][0:64, :], col("mqlg", 1, 0, 64), rs192[0:64, :], ALU.mult, ALU.mult)
        P.stt(ckvn.v(), z18[j2].v(), col("mkvlg", 0), rs128.v(), ALU.mult, ALU.mult)
        items = [(h, isk) for h in range(4) for isk in range(2)]
        for g0 in range(0, 8, 4):
            grp = items[g0:g0 + 4]
            for j, (h, isk) in enumerate(grp):
                ps = P.psum()
                if not isk:
                    P.mm(ps[0:96, :], Wq[:, 0, 96 * h:96 * h + 96], cqn[:, 0, :], start=True, stop=False)
                    P.mm(ps[0:96, :], Wq[0:64, 1, 96 * h:96 * h + 96], cqn[0:64, 1, :], start=False, stop=True)
                else:
                    P.mm(ps[0:96, :], WkK[:, h, :], ckvn.v(), start=True, stop=False)
                    P.mm(ps[0:96, :], selkr_bf.v(), z13b[j2].v(), start=False, stop=True)
                P.copy("act", raw[j].v(), ps[0:96, :])
                P.act(sqr[j].v(), ps[0:96, :], AF.Square)
            for j, (h, isk) in enumerate(grp):
                p2 = P.psum()
                P.mm(p2[0:96, :], ones[0:96, 0:96], sqr[j].v())
                if not isk:
                    P.act(rbq[j].v(), p2[0:96, :], AF.Sqrt, bias=96 * EPS, scale=1.0)
                else:
                    P.act(rbq[j].v(), p2[0:96, :], AF.Sqrt, bias=EPS, scale=1.0 / 96)
            for j, (h, isk) in enumerate(grp):
                P.recip(rbq[j].v(), rbq[j].v())
            for j, (h, isk) in enumerate(grp):
                P.stt(qn[j].v(), raw[j].v(), col("mkg" if isk else "mqg", 0, 0, 96), rbq[j].v(), ALU.mult, ALU.mult)
                P.copy("pool", qnb[j].v(), qn[j].v())
            for j, (h, isk) in enumerate(grp):
                p3 = P.psum()
                P.mm(p3[0:96, :], rotP_bf.v(), qnb[j].v())
                P.tt("pool", tc_[j].v(), qn[j].v(), COS.v(), ALU.mult)
                P.tt("dve", raw[j].v(), p3[0:96, :], SIN.v(), ALU.mult)
            for j, (h, isk) in enumerate(grp):
                P.tt("pool", (Km if isk else Qm)[0:96, h, ts_], tc_[j].v(), raw[j].v(), ALU.add)
        for tt in range(4):
            ps = P.psum()
            P.mm(ps[:, 0:256], ckvn[:, tt * 128:(tt + 1) * 128], WkV.v())
            evac(Vm[:, blk * 4 + tt, :, 0:64], ps[:, 0:256].rearrange("p (h d) -> p h d", h=4))
    A.release(m)
    attention(P, A, NB, Km, Qm, 96, Vm, ident_bf, amask, ones, None, "og_mla", col, 768, mixT_d, mixR,
              rsqrt_into, evac)


def host_prep(inputs, T):
    L = inputs["w_in"].shape[0]
    f = np.float32
    g = lambda k: np.asarray(inputs[k], dtype=f)
    w_in = g("w_in")
    wp = np.zeros((L, D, ZW), f)
    wp[:, :, 0:896] = w_in[:, :, 0:896]
    fo = 896
    wp[:, :, 7 * 128:13 * 128] = w_in[:, :, fo:fo + 768]
    wp[:, :, 13 * 128:13 * 128 + 4] = w_in[:, :, fo + 768:fo + 772]
    so = fo + 772
    wp[:, :, 14 * 128:16 * 128] = w_in[:, :, so:so + 256]
    mo = so + 256
    wp[:, :, 16 * 128:16 * 128 + 192] = w_in[:, :, mo:mo + 192]
    wp[:, :, 18 * 128:19 * 128] = w_in[:, :, mo + 192:mo + 320]
    wp[:, :, 13 * 128 + 32:13 * 128 + 64] = w_in[:, :, mo + 320:mo + 352]
    cols = np.zeros((L, 128, NC_COLS), f)

    def put(name, arr, width):
        a = arr.reshape(L, width, 128).transpose(0, 2, 1)
        cols[:, :, COLS[name]:COLS[name] + width] = a

    put("mixg", g("mix_norm_g"), 8)
    put("mlpg", g("mlp_norm_g"), 8)
    put("mu", g("rwkv_mu"), 7)
    put("w0", g("rwkv_w0"), 2)
    put("a0", g("rwkv_a0"), 2)
    put("kk", g("rwkv_k_k"), 2)
    put("ka", g("rwkv_k_a"), 2)
    put("lng", g("rwkv_ln_g"), 2)
    put("lnb", g("rwkv_ln_b"), 2)
    put("rk", g("rwkv_r_k").reshape(L, 256), 2)
    cols[:, :, COLS["foxqg"]] = np.tile(g("fox_q_g"), (1, 2))
    cols[:, :, COLS["foxkg"]] = np.tile(g("fox_k_g"), (1, 2))
    cols[:, 0:4, COLS["foxfb"]] = g("fox_f_b")
    put("ssmd", g("ssm_d"), 2)
    og = g("out_norm_g")
    cols[:, 0:64, COLS["og_fox"]:COLS["og_fox"] + 4] = og[:, 0].reshape(L, 4, 64).transpose(0, 2, 1)
    put("og_ssm", og[:, 1], 2)
    cols[:, 0:64, COLS["og_mla"]:COLS["og_mla"] + 4] = og[:, 2].reshape(L, 4, 64).transpose(0, 2, 1)
    ql = g("mla_q_latent_g")
    cols[:, :, COLS["mqlg"]] = ql[:, 0:128]
    cols[:, 0:64, COLS["mqlg"] + 1] = ql[:, 128:192]
    cols[:, :, COLS["mkvlg"]] = g("mla_kv_latent_g")
    cols[:, 0:96, COLS["mqg"]] = g("mla_q_g")
    cols[:, 0:96, COLS["mkg"]] = g("mla_k_g")
    for nm, arr in (("slre", g("ssm_lambda_re")), ("slim", g("ssm_lambda_im")),
                    ("sldt", np.repeat(g("ssm_log_dt")[:, :, None], 64, axis=2))):
        cols[:, :, COLS[nm]:COLS[nm] + 8] = arr.reshape(L, 8, 128).transpose(0, 2, 1)
    lr_w = np.concatenate([g("rwkv_w_up"), g("rwkv_a_up"), g("rwkv_g_up")], axis=1)
    bre, bim, cre, cim = g("ssm_b_re"), g("ssm_b_im"), g("ssm_c_re"), g("ssm_c_im")
    ssmB = np.zeros((L, 128, 16, 128), f)
    ssmC = np.zeros((L, 128, 16, 128), f)
    for gi in range(16):
        i, gg = gi // 2, gi % 2
        chn, i4 = i // 4, i % 4
        rows = slice(32 * i4 + 16 * gg, 32 * i4 + 16 * gg + 16)
        stc = slice(64 * gg, 64 * gg + 64)
        ssmB[:, rows, i, stc] = bre[:, gi].transpose(0, 2, 1)
        ssmB[:, rows, 8 + i, stc] = bim[:, gi].transpose(0, 2, 1)
        oc = slice((gi * 16) % 128, (gi * 16) % 128 + 16)
        ssmC[:, stc, i, oc] = cre[:, gi].transpose(0, 2, 1)
        ssmC[:, stc, 8 + i, oc] = cim[:, gi].transpose(0, 2, 1)
    glu = g("ssm_glu_w")
    glu_bd = np.zeros((L, 128, 2, 128), f)
    for gi in range(16):
        chn, o = gi // 8, (gi % 8) * 16
        glu_bd[:, o:o + 16, chn, o:o + 16] = glu[:, gi]
    wq = g("mla_w_q_up")
    mla_wq = np.zeros((L, 128, 2, 384), f)
    mla_wq[:, :, 0, :] = wq[:, 0:128]
    mla_wq[:, 0:64, 1, :] = wq[:, 128:192]
    wkv = g("mla_w_kv_up").reshape(L, 128, 4, 128)
    mla_wkvK = np.zeros((L, 128, 4, 96), f)
    mla_wkvK[:, :, :, 0:64] = wkv[:, :, :, 0:64]
    mla_wkvV = np.ascontiguousarray(wkv[:, :, :, 64:128]).reshape(L, 128, 256)
    cst = np.zeros((128, 2048), f)
    cst[:, 0:128] = np.eye(128)
    cst[0:64, 128:192] = 1
    cst[64:128, 192:256] = 1
    cst[:, 256:384] = 1
    s = (np.arange(128) % 64)[:, None]
    t = np.arange(64)[None, :]
    cst[:, 384:448] = (s < t)
    cst[:, 448:512] = (s <= t)
    cst[:, 512:576] = (s > t)
    sm = np.ones((128, 512), f)
    sm[:, 0::64] = 0
    cst[:, 576:1088] = sm
    cst[:, 1088:1600] = np.arange(1, 513)[None, :]
    rot = np.zeros((96, 96), f)
    for i in range(16):
        rot[80 + i, 64 + i] = -1
        rot[64 + i, 80 + i] = 1
    cst[0:96, 1600:1696] = rot
    sel = np.zeros((64, 96), f)
    for i in range(32):
        sel[32 + i, 64 + i] = 1
    cst[0:64, 1696:1792] = sel
    inv = (10000.0 ** (-np.arange(0, 32, 2, dtype=np.float32) / 32)).astype(f)
    cst[64:80, 1792] = inv
    cst[80:96, 1792] = inv
    cst[:, 1800:1928] = 1
    k = np.arange(128)[:, None]
    q = np.arange(512)[None, :]
    amask = np.zeros((128, 4, 512), f)
    for j in range(4):
        amask[:, j, :] = np.where(q >= k + 128 * j, 0.0, NEG)
    place = np.zeros((4, 4, 128), f)
    auxm = np.zeros((128, 2, 8), f)
    for h in range(4):
        a0 = 64 if h % 2 == 0 else 0
        place[h, h, a0:a0 + 6] = 1
    for par in range(2):
        a0 = 64 if par == 0 else 0
        for r in range(3):
            auxm[a0 + r, par, r] = 1
            auxm[a0 + 3 + r, par, r] = 1
            auxm[a0 + r, par, 3] = 1
            auxm[a0 + 3 + r, par, 4] = 1
            auxm[a0 + 3 + r, par, 5] = -1
            auxm[a0 + r, par, 6] = 1
    shared = dict(w_in_p=wp, w_out=g("w_out"), w_ff1=g("w_ff1"), w_ff2=g("w_ff2"), cols=cols, lr_w=lr_w, ssmB=ssmB,
                  ssmC=ssmC, glu_bd=glu_bd, mla_wq=mla_wq, mla_wkvK=mla_wkvK, mla_wkvV=mla_wkvV, cst=cst, amask=amask,
                  place=place, auxm=auxm)
    return shared


_NC_CACHE = {}


def kernel(**inputs):
    x = np.asarray(inputs["x"], dtype=np.float32)
    B, T, _ = x.shape
    L = inputs["w_in"].shape[0]
    shared = host_prep(inputs, T)
    key = (T, L)
    if key not in _NC_CACHE:
        _NC_CACHE[key] = build(T, L)
    nc = _NC_CACHE[key]
    pos = np.asarray(inputs["positions"], dtype=np.int32)
    in_maps = []
    for b in range(B):
        m = dict(shared)
        m["xT"] = np.ascontiguousarray(x[b].T)
        m["pos"] = np.ascontiguousarray(pos[b:b + 1])
        in_maps.append(m)
    res = run_bass_kernel_spmd(nc, in_maps, core_ids=list(range(B)))
    out = np.stack([np.asarray(r["outT"]).T for r in res.results], axis=0)
    return np.ascontiguousarray(out.astype(np.float32))
```

```python
import math
import contextlib
import numpy as np
import concourse.bass as bass
import concourse.mybir as mybir
from concourse.bass_utils import run_bass_kernel_spmd

F32 = mybir.dt.float32
BF16 = mybir.dt.bfloat16
I32 = mybir.dt.int32
AF = mybir.ActivationFunctionType
ALU = mybir.AluOpType

D = 1024
NZC = 19
ZW = NZC * 128
EPS = 1e-6
C0 = math.exp(-0.5)
NEG = -30000.0
RWKV_CD = BF16
RWKV_CHD = F32
S5_CD = BF16


class View:
    __slots__ = ("b", "ap")

    def __init__(self, b, ap):
        self.b = b
        self.ap = ap

    def __getitem__(self, idx):
        return View(self.b, self.ap[idx])

    def rearrange(self, *a, **k):
        return View(self.b, self.ap.rearrange(*a, **k))

    def bitcast(self, dt):
        return View(self.b, self.ap.bitcast(dt))


class Buf:
    __slots__ = ("ap", "w", "r", "name")

    def __init__(self, ap, name=""):
        self.ap = ap
        self.w = {}
        self.r = {}
        self.name = name

    def __getitem__(self, idx):
        return View(self, self.ap[idx])

    def v(self):
        return View(self, self.ap)


def _ap(x):
    return x.ap if isinstance(x, View) else x


def _bufs(*xs):
    out = []
    for x in xs:
        if isinstance(x, View):
            out.append(x.b)
        elif isinstance(x, Buf):
            out.append(x)
    return out


class Prog:
    ENGS = ("pe", "dve", "act", "pool", "sp")
    NSLOT = 14

    def __init__(self, nc):
        self.nc = nc
        self.q = {e: [] for e in self.ENGS}
        self.cnt = {e: 0 for e in self.ENGS}
        self.waited = {e: {} for e in self.ENGS}
        self.slot_val = {}
        self.slot_next = {"sp": 0, "pool": 0}
        self.final = []
        self.psb = []
        self.psi = 0

    def _need(self, eng, key, val, waits):
        if key == eng and eng == "pe":
            return
        if self.waited[eng].get(key, 0) >= val:
            return
        self.waited[eng][key] = val
        waits.append((key, val))

    def _deps(self, eng, reads, writes):
        waits = []
        for b in reads:
            for k, v in b.w.items():
                self._need(eng, k, v, waits)
        for b in writes:
            for k, v in b.w.items():
                self._need(eng, k, v, waits)
            for k, v in b.r.items():
                self._need(eng, k, v, waits)
        return waits

    def _mark(self, tok, reads, writes):
        k, v = tok
        for b in reads:
            if b.r.get(k, 0) < v:
                b.r[k] = v
        for b in writes:
            b.w = {k: v}
            b.r = {}

    def op(self, eng, fn, reads=(), writes=()):
        waits = self._deps(eng, reads, writes)
        self.cnt[eng] += 1
        tok = (eng, self.cnt[eng])
        self.q[eng].append((waits, fn, tok))
        self._mark(tok, reads, writes)

    def dma(self, queue, fn, reads=(), writes=(), final=False):
        s = self.slot_next[queue]
        self.slot_next[queue] = (s + 1) % self.NSLOT
        key = ("d", queue, s)
        prev = self.slot_val.get(key, 0)
        waits = self._deps(queue, reads, writes)
        if prev > 0:
            self._need(queue, key, prev, waits)
        val = prev + 16
        self.slot_val[key] = val
        tok = (key, val)
        self.q[queue].append((waits, fn, tok))
        self._mark(tok, reads, writes)
        if final:
            self.final.append(tok)

    def barrier(self):
        for e in self.ENGS:
            waits = []
            for f in ("pe", "dve", "act", "pool"):
                if f != e and self.cnt[f] > 0:
                    self._need(e, f, self.cnt[f], waits)
            for key, val in self.slot_val.items():
                self._need(e, key, val, waits)
            if waits:
                self.q[e].append((waits, None, None))

    def psum(self):
        b = self.psb[self.psi]
        self.psi = (self.psi + 1) % 6
        return b

    def psum_acc(self):
        self.pai = 1 - getattr(self, "pai", 0)
        return self.psb[6 + self.pai]

    def mm(self, out, lhsT, rhs, start=True, stop=True):
        o, l, r = _ap(out), _ap(lhsT), _ap(rhs)
        self.op("pe", lambda e: e.matmul(o, lhsT=l, rhs=r, start=start, stop=stop),
                reads=_bufs(lhsT, rhs), writes=_bufs(out))

    def act(self, out, in_, func, bias=0.0, scale=1.0):
        o, i, b = _ap(out), _ap(in_), _ap(bias)
        self.op("act", lambda e: e.activation(out=o, in_=i, func=func, bias=b, scale=scale),
                reads=_bufs(in_, bias), writes=_bufs(out))

    def tt(self, eng, out, in0, in1, op):
        o, a, b = _ap(out), _ap(in0), _ap(in1)
        self.op(eng, lambda e: e.tensor_tensor(out=o, in0=a, in1=b, op=op),
                reads=_bufs(in0, in1), writes=_bufs(out))

    def ts(self, eng, out, in0, s1, op0, s2=None, op1=None):
        o, a, x1, x2 = _ap(out), _ap(in0), _ap(s1), _ap(s2)
        if op1 is None:
            self.op(eng, lambda e: e.tensor_scalar(out=o, in0=a, scalar1=x1, scalar2=None, op0=op0),
                    reads=_bufs(in0, s1), writes=_bufs(out))
        else:
            self.op(eng, lambda e: e.tensor_scalar(out=o, in0=a, scalar1=x1, scalar2=x2, op0=op0, op1=op1),
                    reads=_bufs(in0, s1, s2), writes=_bufs(out))

    def stt(self, out, in0, scalar, in1, op0, op1):
        o, a, s, b = _ap(out), _ap(in0), _ap(scalar), _ap(in1)
        self.op("dve", lambda e: e.scalar_tensor_tensor(out=o, in0=a, scalar=s, in1=b, op0=op0, op1=op1),
                reads=_bufs(in0, scalar, in1), writes=_bufs(out))

    def copy(self, eng, out, in_):
        o, i = _ap(out), _ap(in_)
        if eng == "act":
            self.op("act", lambda e: e.activation(out=o, in_=i, func=AF.Copy), reads=_bufs(in_), writes=_bufs(out))
        else:
            self.op(eng, lambda e: e.tensor_copy(out=o, in_=i), reads=_bufs(in_), writes=_bufs(out))

    def scan(self, out, d0, d1, init, op0=ALU.mult, op1=ALU.add):
        o, a, b, i = _ap(out), _ap(d0), _ap(d1), _ap(init)
        self.op("dve", lambda e: e.tensor_tensor_scan(out=o, data0=a, data1=b, initial=i, op0=op0, op1=op1),
                reads=_bufs(d0, d1, init), writes=_bufs(out))

    def recip(self, out, in_):
        o, i = _ap(out), _ap(in_)
        self.op("dve", lambda e: e.reciprocal(out=o, in_=i), reads=_bufs(in_), writes=_bufs(out))

    def memset(self, eng, out, val):
        o = _ap(out)
        self.op(eng, lambda e: e.memset(o, val), writes=_bufs(out))

    def load(self, queue, out, src_ap, src_bufs=()):
        o = _ap(out)
        self.dma(queue, lambda e: e.dma_start(out=o, in_=src_ap), reads=list(src_bufs), writes=_bufs(out))

    def store(self, queue, dst_ap, in_, dst_bufs=(), final=False):
        i = _ap(in_)
        self.dma(queue, lambda e: e.dma_start(out=dst_ap, in_=i), reads=_bufs(in_), writes=list(dst_bufs), final=final)

    def emit(self):
        nc = self.nc
        with contextlib.ExitStack() as st:
            sems = {}
            for e in ("pe", "dve", "act", "pool"):
                sems[e] = st.enter_context(nc.semaphore("s_" + e))
            for key in self.slot_val:
                sems[key] = st.enter_context(nc.semaphore("d_%s_%d" % (key[1], key[2])))
            block = st.enter_context(nc.Block())
            endw = [(e, self.cnt[e]) for e in ("pe", "dve", "act", "pool") if self.cnt[e] > 0]
            endw += list(self.slot_val.items())

            def mk(ename):
                def body(eng):
                    for waits, fn, tok in self.q[ename]:
                        for k, v in waits:
                            eng.wait_ge(sems[k], v)
                        if fn is None:
                            continue
                        ins = fn(eng)
                        k, v = tok
                        ins.then_inc(sems[k], 16 if isinstance(k, tuple) else 1)
                    if ename == "sp":
                        for k, v in endw:
                            eng.wait_ge(sems[k], v)
                return body

            block.tensor(mk("pe"))
            block.vector(mk("dve"))
            block.scalar(mk("act"))
            block.gpsimd(mk("pool"))
            block.sync(mk("sp"))


class Arena:
    def __init__(self, buf_ap, words):
        self.ap = buf_ap
        self.words = words
        self.off = 0

    def mark(self):
        return self.off

    def release(self, m):
        self.off = m

    def alloc(self, shape, dt=F32, name=""):
        n = 1
        for s in shape[1:]:
            n *= s
        w = n if dt != BF16 else (n + 1) // 2
        w = (w + 7) // 8 * 8
        assert self.off + w <= self.words, ("arena overflow", name, self.off, w, self.words)
        ap = self.ap[:, self.off:self.off + w]
        self.off += w
        if dt != F32:
            ap = ap.bitcast(dt)
        ap = ap[:, 0:n]
        if len(shape) == 3:
            ap = ap.rearrange("p (a b) -> p a b", a=shape[1])
        elif len(shape) == 4:
            ap = ap.rearrange("p (a b c) -> p a b c", a=shape[1], b=shape[2])
        if shape[0] < 128:
            ap = ap[0:shape[0]]
        return Buf(ap, name)


COLS = {}
_o = 0
for _n, _c in [("mixg", 8), ("mlpg", 8), ("mu", 7), ("w0", 2), ("a0", 2), ("kk", 2), ("ka", 2), ("omka", 2),
               ("lng", 2), ("lnb", 2), ("rk", 2), ("foxqg", 1), ("foxkg", 1), ("foxfb", 1), ("ssmd", 2),
               ("og_fox", 4), ("og_ssm", 2), ("og_mla", 4), ("mqlg", 2), ("mkvlg", 1), ("mqg", 1), ("mkg", 1),
               ("slre", 8), ("slim", 8), ("sldt", 8)]:
    COLS[_n] = _o
    _o += _c
NC_COLS = _o


def build(T, NL, dbg=False):
    NB = T // 512
    NCH = T // 128
    nc = bass.Bass("TRN2", target_bir_lowering=False)
    P = Prog(nc)

    def din(name, shape, dt=F32):
        return nc.dram_tensor(name, shape, dt, kind="ExternalInput").ap()

    xT_in = din("xT", [D, T])
    pos_in = din("pos", [1, T], I32)
    w_in_d = din("w_in_p", [NL, D, ZW])
    w_out_d = din("w_out", [NL, D, D])
    w_ff1_d = din("w_ff1", [NL, D, 4 * D])
    w_ff2_d = din("w_ff2", [NL, 4 * D, D])
    cols_d = din("cols", [NL, 128, NC_COLS])
    lr_d = din("lr_w", [NL, 128, 256])
    ssmB_d = din("ssmB", [NL, 128, 16, 128])
    ssmC_d = din("ssmC", [NL, 128, 16, 128])
    glu_d = din("glu_bd", [NL, 128, 2, 128])
    wq_d = din("mla_wq", [NL, 128, 2, 384])
    wkvK_d = din("mla_wkvK", [NL, 128, 4, 96])
    wkvV_d = din("mla_wkvV", [NL, 128, 256])
    cst_d = din("cst", [128, 2048])
    amask_d = din("amask", [128, 4, 512])
    place_d = din("place", [4, 4, 128])
    auxm_d = din("auxm", [128, 2, 8])
    outT = nc.dram_tensor("outT", [D, T], F32, kind="ExternalOutput").ap()

    def dscr(name, shape):
        return nc.dram_tensor(name, shape, F32, kind="Internal").ap()

    zT_d = dscr("zT", [ZW, T])
    mixT_d = dscr("mixT", [D, T])
    xmid_d = dscr("xmid", [D, T])
    xres_d = dscr("xres", [D, T])
    cos_d = dscr("cosT", [96, T])
    sin_d = dscr("sinT", [96, T])
    dbg_out = {}
    if dbg:
        dbg_out["zT_o"] = nc.dram_tensor("zT_o", [ZW, T], F32, kind="ExternalOutput").ap()
        dbg_out["mixT_o"] = nc.dram_tensor("mixT_o", [D, T], F32, kind="ExternalOutput").ap()
        dbg_out["xmid_o"] = nc.dram_tensor("xmid_o", [D, T], F32, kind="ExternalOutput").ap()

    def regions(nchunk):
        return [[Buf(None, "r") for _ in range(NB)] for _ in range(nchunk)]

    zR = regions(NZC)
    mixR = regions(8)
    xmidR = regions(8)
    xresR = regions(8)
    outR = regions(8)

    st = contextlib.ExitStack()
    with st:
        AW = 52000
        arena_t = st.enter_context(nc.sbuf_tensor("arena", [128, AW], F32))
        A = Arena(arena_t[:, :], AW)
        for i in range(8):
            P.psb.append(Buf(st.enter_context(nc.psum_tensor("ps%d" % i, [128, 512], F32))[:, :], "ps%d" % i))

        cst = A.alloc([128, 2048], F32, "cst")
        P.load("sp", cst.v(), cst_d[:, :])
        ident = cst[:, 0:128]
        onesblk = cst[:, 128:256]
        ones = cst[:, 256:384]
        tri = cst[:, 384:576]
        scanmask = cst[:, 576:1088]
        iota = cst[:, 1088:1600]
        rotP = cst[0:96, 1600:1696]
        selkr = cst[0:64, 1696:1792]
        invfreq = cst[0:96, 1792:1793]
        onesrow = cst[:, 1800:1928]
        ones_bf = A.alloc([128, 128], BF16, "ones_bf")
        ident_bf = A.alloc([128, 128], BF16, "ident_bf")
        rotP_bf = A.alloc([96, 96], BF16, "rotP_bf")
        selkr_bf = A.alloc([64, 96], BF16, "selkr_bf")
        P.copy("dve", ones_bf.v(), ones)
        P.copy("dve", ident_bf.v(), ident)
        P.copy("dve", rotP_bf.v(), rotP)
        P.copy("dve", selkr_bf.v(), selkr)
        amask = A.alloc([128, 4, 512], BF16, "amask")
        P.load("pool", amask.v(), amask_d[:, :, :])
        place = A.alloc([4, 4, 128], F32, "place")
        P.load("sp", place.v(), place_d[:, :, :])
        auxm = A.alloc([128, 2, 8], F32, "auxm")
        P.load("sp", auxm.v(), auxm_d[:, :, :])
        ones512 = A.alloc([128, 512], F32, "ones512")
        P.memset("pool", ones512.v(), 1.0)
        cols = A.alloc([128, NC_COLS], F32, "cols")
        m_layer = A.mark()
        ropeR = [[Buf(None, "rope") for _ in range(NB)] for _ in range(2)]
        posi = A.alloc([96, 512], I32, "posi")
        ang, red, kf_, COSb, SINb = [A.alloc([96, 512], F32, n_) for n_ in ("ang", "red", "kf", "COSb", "SINb")]
        ki_ = A.alloc([96, 512], I32, "ki")
        for blk in range(NB):
            ts_ = slice(blk * 512, (blk + 1) * 512)
            P.load("sp", posi.v(), pos_in[0:1, ts_].broadcast_to([96, 512]))
            P.copy("dve", ang.v(), posi.v())
            P.ts("dve", ang.v(), ang.v(), invfreq, ALU.mult)
            range_reduce(P, red.v(), ang.v(), ki_.v(), kf_.v())
            P.act(SINb.v(), red.v(), AF.Sin)
            P.ts("dve", ang.v(), ang.v(), math.pi / 2, ALU.add)
            range_reduce(P, red.v(), ang.v(), ki_.v(), kf_.v())
            P.act(COSb.v(), red.v(), AF.Sin)
            P.store("sp", cos_d[:, ts_], COSb.v(), [ropeR[0][blk]])
            P.store("sp", sin_d[:, ts_], SINb.v(), [ropeR[1][blk]])

        def col(name, i=0, p0=0, p1=128):
            c = COLS[name] + i
            return cols[p0:p1, c:c + 1]

        def rsqrt_into(out, ps_in, scale, bias):
            P.act(out, ps_in, AF.Sqrt, bias=bias, scale=scale)
            P.recip(out, out)

        evac_rr = [0]

        def evac(out, in_):
            evac_rr[0] ^= 1
            P.copy("act" if evac_rr[0] else "dve", out, in_)

        def rmsnorm_block(xsrcR, src_ap, gname, blk, xt, xn, sq, rstd):
            t0 = blk * 512
            P.load("sp", xt, src_ap[:, t0:t0 + 512].rearrange("(c p) n -> p c n", p=128),
                   [xsrcR[c][blk] for c in range(8)])
            P.act(sq, xt, AF.Square)
            ps = P.psum()
            for c in range(8):
                P.mm(ps.v(), ones_bf.v(), sq[:, c, :], start=(c == 0), stop=(c == 7))
            rsqrt_into(rstd.v(), ps.v(), 1.0 / D, EPS)
            for c in range(8):
                P.stt(xn[:, c, :], xt[:, c, :], col(gname, c), rstd.v(), ALU.mult, ALU.mult)

        for l in range(NL):
            A.release(m_layer)
            P.barrier()
            P.load("sp", cols.v(), cols_d[l, :, :])
            P.ts("dve", cols[:, COLS["omka"]:COLS["omka"] + 2], cols[:, COLS["ka"]:COLS["ka"] + 2], -1.0, ALU.mult,
                 1.0, ALU.add)
            xsrc_ap, xsrcR = (xT_in, [[Buf(None) for _ in range(NB)] for _ in range(8)]) if l == 0 else (xres_d, xresR)
            Vfox = A.alloc([128, NCH, 4, 65], BF16, "Vfox")
            m_phase = A.mark()

            Wb = A.alloc([128, 8, ZW], BF16, "Win")
            wst = [A.alloc([128, ZW], F32, "wst%d" % i) for i in range(2)]
            for c in range(8):
                P.load("sp", wst[c % 2].v(), w_in_d[l, c * 128:(c + 1) * 128, :])
                evac(Wb[:, c, :], wst[c % 2].v())
            P.memset("pool", Vfox[:, :, :, 64:65], 1.0)
            xts = [A.alloc([128, 8, 512], F32, "xt%d" % i) for i in range(2)]
            xns = [A.alloc([128, 8, 512], BF16, "xn%d" % i) for i in range(2)]
            sqs = [A.alloc([128, 8, 512], BF16, "sq%d" % i) for i in range(2)]
            rstds = [A.alloc([128, 512], F32, "rstd%d" % i) for i in range(2)]
            zst = [A.alloc([128, 512], F32, "zst%d" % i) for i in range(4)]
            zi = 0
            FOXV = 11 * 128
            def normA(b_):
                rmsnorm_block(xsrcR, xsrc_ap, "mixg", b_, xts[b_ % 2].v(), xns[b_ % 2], sqs[b_ % 2].v(), rstds[b_ % 2])

            normA(0)
            for blk in range(NB):
                t0 = blk * 512
                xt, xn, sq, rstd = xts[blk % 2], xns[blk % 2], sqs[blk % 2], rstds[blk % 2]
                if blk + 1 < NB:
                    normA(blk + 1)
                for oc in range(NZC):
                    ps = P.psum()
                    for c in range(8):
                        P.mm(ps.v(), Wb[:, c, oc * 128:(oc + 1) * 128], xn[:, c, :], start=(c == 0), stop=(c == 7))
                    zs_ = zst[zi % 4]
                    zi += 1
                    evac(zs_.v(), ps.v())
                    P.store("sp", zT_d[oc * 128:(oc + 1) * 128, t0:t0 + 512], zs_.v(), [zR[oc][blk]])
                for tt in range(4):
                    ps = P.psum()
                    for c in range(8):
                        P.mm(ps[:, 0:256], xn[:, c, tt * 128:(tt + 1) * 128], Wb[:, c, FOXV:FOXV + 256],
                             start=(c == 0), stop=(c == 7))
                    evac(Vfox[:, blk * 4 + tt, :, 0:64], ps[:, 0:256].rearrange("p (h d) -> p h d", h=4))
            if dbg and l == 0:
                P.barrier()
                tmp = A.alloc([128, 512], F32, "dbgtmp")
                for oc in range(NZC):
                    for blk in range(NB):
                        P.load("sp", tmp.v(), zT_d[oc * 128:(oc + 1) * 128, blk * 512:(blk + 1) * 512], [zR[oc][blk]])
                        P.store("sp", dbg_out["zT_o"][oc * 128:(oc + 1) * 128, blk * 512:(blk + 1) * 512], tmp.v(), final=True)
            P.barrier()
            A.release(m_phase)

            phase_rwkv(P, A, l, NB, cols, col, lr_d, zT_d, zR, mixT_d, mixR, ident, ident_bf, onesblk, tri, scanmask, rsqrt_into, evac)
            P.barrier()
            A.release(m_phase)

            phase_fox(P, A, l, NB, T, col, zT_d, zR, mixT_d, mixR, Vfox, onesblk, ones, ident_bf, amask, place, auxm,
                      ones512, rsqrt_into, evac)
            P.barrier()
            A.release(m_layer)
            m_phase = A.mark()

            phase_s5(P, A, l, NB, col, cols, zT_d, zR, mixT_d, mixR, ssmB_d, ssmC_d, glu_d, ones, iota, rsqrt_into, evac)
            P.barrier()
            A.release(m_phase)

            phase_mla(P, A, l, NB, T, col, zT_d, zR, mixT_d, mixR, wq_d, wkvK_d, wkvV_d, (cos_d, sin_d, ropeR), ones,
                      ident_bf, amask, rotP_bf, selkr_bf, invfreq, rsqrt_into, evac)
            P.barrier()
            A.release(m_phase)

            if dbg and l == 0:
                tmp = A.alloc([128, 512], F32, "dbgtmp")
                for oc in range(8):
                    for blk in range(NB):
                        P.load("sp", tmp.v(), mixT_d[oc * 128:(oc + 1) * 128, blk * 512:(blk + 1) * 512], [mixR[oc][blk]])
                        P.store("sp", dbg_out["mixT_o"][oc * 128:(oc + 1) * 128, blk * 512:(blk + 1) * 512], tmp.v(), final=True)
                P.barrier()
                A.release(m_phase)

            Wo = A.alloc([128, 8, D], BF16, "Wout")
            wst = [A.alloc([128, D], F32, "wsto%d" % i) for i in range(2)]
            for c in range(8):
                P.load("sp", wst[c % 2].v(), w_out_d[l, c * 128:(c + 1) * 128, :])
                evac(Wo[:, c, :], wst[c % 2].v())
            mixb = [A.alloc([128, 8, 512], BF16, "mixb%d" % i) for i in range(2)]
            xts = [A.alloc([128, 8, 512], F32, "xt%d" % i) for i in range(2)]
            zst = [A.alloc([128, 512], F32, "zst%d" % i) for i in range(4)]
            zi = 0
            def loadF(b_):
                tb_ = b_ * 512
                P.load("pool", mixb[b_ % 2].v(), mixT_d[:, tb_:tb_ + 512].rearrange("(c p) n -> p c n", p=128),
                       [mixR[c][b_] for c in range(8)])
                P.load("sp", xts[b_ % 2].v(), xsrc_ap[:, tb_:tb_ + 512].rearrange("(c p) n -> p c n", p=128),
                       [xsrcR[c][b_] for c in range(8)])

            loadF(0)
            for blk in range(NB):
                t0 = blk * 512
                mb, xt = mixb[blk % 2], xts[blk % 2]
                if blk + 1 < NB:
                    loadF(blk + 1)
                for oc in range(8):
                    ps = P.psum()
                    for c in range(8):
                        P.mm(ps.v(), Wo[:, c, oc * 128:(oc + 1) * 128], mb[:, c, :], start=(c == 0), stop=(c == 7))
                    zs_ = zst[zi % 4]
                    zi += 1
                    P.tt("dve", zs_.v(), ps.v(), xt[:, oc, :], ALU.add)
                    P.store("sp", xmid_d[oc * 128:(oc + 1) * 128, t0:t0 + 512], zs_.v(), [xmidR[oc][blk]])
                    if dbg and l == 0:
                        P.store("sp", dbg_out["xmid_o"][oc * 128:(oc + 1) * 128, t0:t0 + 512], zs_.v(), final=True)
            P.barrier()
            A.release(m_phase)

            W1 = A.alloc([128, 8, 4 * D], BF16, "W1")
            W2 = A.alloc([128, 32, D], BF16, "W2")
            m_st = A.mark()
            wst = [A.alloc([128, 4 * D], F32, "wstg%d" % i) for i in range(2)]
            for c in range(8):
                P.load("sp", wst[c % 2].v(), w_ff1_d[l, c * 128:(c + 1) * 128, :])
                evac(W1[:, c, :], wst[c % 2].v())
            for c4 in range(8):
                P.load("sp", wst[c4 % 2].v().rearrange("p (c n) -> p c n", c=4),
                       w_ff2_d[l, c4 * 512:(c4 + 1) * 512, :].rearrange("(c p) n -> p c n", p=128))
                evac(W2[:, c4 * 4:(c4 + 1) * 4, :], wst[c4 % 2].v().rearrange("p (c n) -> p c n", c=4))
            P.barrier()
            A.release(m_st)
            xn = A.alloc([128, 8, 512], BF16, "xn")
            rstd = A.alloc([128, 512], F32, "rstd")
            h1 = A.alloc([128, 32 * 512], BF16, "h1")
            xt = h1[:, 0:16 * 512].bitcast(F32).rearrange("p (a b) -> p a b", a=8)
            sq = h1[:, 16 * 512:24 * 512].rearrange("p (a b) -> p a b", a=8)
            h1 = h1.v().rearrange("p (a b) -> p a b", a=32)
            xr = [A.alloc([128, 512], F32, "xr%d" % i) for i in range(2)]
            rl = [A.alloc([128, 512], F32, "rl%d" % i) for i in range(2)]
            zst = [A.alloc([128, 512], F32, "zst%d" % i) for i in range(2)]
            last = (l == NL - 1)
            dst_ap, dstR = (outT, outR) if last else (xres_d, xresR)
            for blk in range(NB):
                t0 = blk * 512
                rmsnorm_block(xmidR, xmid_d, "mlpg", blk, xt, xn, sq, rstd)
                for f in range(32):
                    ps = P.psum()
                    for c in range(8):
                        P.mm(ps.v(), W1[:, c, f * 128:(f + 1) * 128], xn[:, c, :], start=(c == 0), stop=(c == 7))
                    r_ = rl[f % 2]
                    P.act(r_.v(), ps.v(), AF.Relu)
                    P.tt("pool" if f % 2 else "dve", h1[:, f, :], r_.v(), r_.v(), ALU.mult)
                for oc in range(8):
                    ps = P.psum()
                    for f in range(32):
                        P.mm(ps.v(), W2[:, f, oc * 128:(oc + 1) * 128], h1[:, f, :], start=(f == 0), stop=(f == 31))
                    zs_ = zst[oc % 2]
                    xr_ = xr[oc % 2]
                    P.load("sp", xr_.v(), xmid_d[oc * 128:(oc + 1) * 128, t0:t0 + 512], [xmidR[oc][blk]])
                    P.tt("dve", zs_.v(), ps.v(), xr_.v(), ALU.add)
                    P.store("sp", dst_ap[oc * 128:(oc + 1) * 128, t0:t0 + 512], zs_.v(), [dstR[oc][blk]], final=last)
            P.barrier()
        P.emit()
    return nc


def phase_rwkv(P, A, l, NB, cols, col, lr_d, zT_d, zR, mixT_d, mixR, ident, ident_bf, onesblk, tri, scanmask, rsqrt_into, evac):
    LR = A.alloc([128, 256], F32, "LR")
    P.load("sp", LR.v(), lr_d[l, :, :])
    zbs = [A.alloc([128, 7, 513], F32, "zb%d" % i) for i in range(2)]
    dtmp = A.alloc([128, 7, 512], F32, "dtmp")
    zs = A.alloc([128, 7, 512], F32, "zs")
    lowr = A.alloc([128, 512], F32, "lowr")

    def t2(name):
        return A.alloc([128, 2, 512], F32, name)

    sig, a_, g_, kk, sqk, rn, t1, kmod, Lc, Ep, Em, Eq, yT, yc, tmpb = [
        t2(n) for n in ("sig", "a", "g", "kk", "sqk", "rn", "t1", "kmod", "Lc", "Ep", "Em", "Eq",
                        "yT", "yc", "tmpb")]
    b_, Lp = yc, tmpb
    CD = RWKV_CD
    identx = ident_bf if CD == BF16 else ident
    bett = A.alloc([128, 2, 512], CD, "bett")
    ktil = A.alloc([128, 2, 512], CD, "ktil")
    vb = A.alloc([128, 2, 512], CD, "vb")
    KR = A.alloc([128, 2, 8, 128], CD, "KR")
    ST = [A.alloc([128, 64], F32, "ST%d" % i) for i in range(2)]
    STb = [A.alloc([128, 64], CD, "STb%d" % i) for i in range(2)]
    for s_ in ST + STb:
        P.memset("pool", s_.v(), 0.0)
    NR = 4
    ABR = [A.alloc([128, 64], CD, "ABR%d" % i) for i in range(NR)]
    AK = [A.alloc([128, 128], CD, "AK%d" % i) for i in range(NR)]
    TK = [A.alloc([128, 192], CD, "TK%d" % i) for i in range(NR)]
    DG = [A.alloc([128, 128], CD, "DG%d" % i) for i in range(NR)]
    nWT = [A.alloc([128, 64], RWKV_CHD, "nWT%d" % i) for i in range(NR)]
    UT = [A.alloc([128, 64], CD, "UT%d" % i) for i in range(NR)]
    Qm = [A.alloc([128, 128], RWKV_CHD, "Qm%d" % i) for i in range(NR)]
    PW = [[A.alloc([128, 128], RWKV_CHD, "PW%d_%d" % (i, j)) for j in range(4)] for i in range(NR)]
    for i in range(NR):
        for j in range(4):
            P.memset("pool", PW[i][j].v(), 0.0)
    stage = [A.alloc([128, 512], F32, "stg%d" % i) for i in range(2)]
    it = 0
    for blk in range(NB):
        t0 = blk * 512
        zb = zbs[blk % 2]
        P.load("sp", zb[:, :, 1:513], zT_d[0:896, t0:t0 + 512].rearrange("(c p) n -> p c n", p=128),
               [zR[c][blk] for c in range(7)])
        if blk == 0:
            P.memset("pool", zb[:, :, 0:1], 0.0)
        else:
            P.copy("pool", zb[:, :, 0:1], zbs[(blk - 1) % 2][:, :, 512:513])
        P.tt("dve", dtmp.v(), zb[:, :, 0:512], zb[:, :, 1:513], ALU.subtract)
        for c in range(7):
            P.stt(zs[:, c, :], dtmp[:, c, :], col("mu", c), zb[:, c, 1:513], ALU.mult, ALU.add)
        r_, k_, v_ = zs[:, 0:2, :], zs[:, 2:4, :], zs[:, 4:6, :]
        P.act(lowr[0:32, :], zs[0:32, 6, :], AF.Tanh)
        P.act(lowr[64:128, :], zs[64:128, 6, :], AF.Sigmoid)
        for h in range(2):
            ps = P.psum()
            P.mm(ps.v(), LR[0:32, h * 128:(h + 1) * 128], lowr[0:32, :])
            P.act(sig[:, h, :], ps.v(), AF.Sigmoid, bias=col("w0", h))
            ps = P.psum()
            P.mm(ps.v(), LR[32:64, h * 128:(h + 1) * 128], zs[32:64, 6, :])
            P.act(a_[:, h, :], ps.v(), AF.Sigmoid, bias=col("a0", h))
            ps = P.psum()
            P.mm(ps.v(), LR[64:128, h * 128:(h + 1) * 128], lowr[64:128, :])
            P.copy("act", g_[:, h, :], ps.v())
            P.ts("dve", kk[:, h, :], zs[:, 2 + h, :], col("kk", h), ALU.mult)
        P.act(sqk.v(), kk.v(), AF.Square)
        for h in range(2):
            ps = P.psum()
            P.mm(ps.v(), onesblk, sqk[:, h, :])
            P.act(rn[:, h, :], ps.v(), AF.Sqrt)
        P.ts("dve", rn.v(), rn.v(), 1e-12, ALU.max)
        P.recip(rn.v(), rn.v())
        P.tt("dve", kk.v(), kk.v(), rn.v(), ALU.mult)
        P.tt("pool", b_.v(), kk.v(), a_.v(), ALU.mult)
        for h in range(2):
            P.ts("dve", t1[:, h, :], a_[:, h, :], col("ka", h), ALU.mult, col("omka", h), ALU.add)
        P.tt("pool", kmod.v(), k_, t1.v(), ALU.mult)
        for h in range(2):
            P.scan(Lc[:, h, :], scanmask, sig[:, h, :], 0.0)
        P.tt("pool", Lp.v(), Lc.v(), sig.v(), ALU.subtract)
        P.act(Ep.v(), Lc.v(), AF.Exp, scale=-C0)
        P.act(Em.v(), Lc.v(), AF.Exp, scale=C0)
        P.act(Eq.v(), Lp.v(), AF.Exp, scale=-C0)
        ch = "p h (c s) -> p h c s"
        P.tt("dve", KR[:, :, :, 0:64], kk.v().rearrange(ch, s=64), Eq.v().rearrange(ch, s=64), ALU.mult)
        P.tt("dve", KR[:, :, :, 64:128], r_.rearrange(ch, s=64), Ep.v().rearrange(ch, s=64), ALU.mult)
        P.tt("pool", bett.v(), b_.v(), Em.v(), ALU.mult)
        P.tt("pool", ktil.v(), kmod.v(), Em.v(), ALU.mult)
        P.copy("pool", vb.v(), zs[:, 4:6, :])
        def make_chunk(c):
            cs = slice(c * 64, (c + 1) * 64)
            ctxs = []

            def front():
                nonlocal it
                for hp in range(2):
                    i = it % NR
                    it += 1
                    gam = Ep[:, hp, c * 64 + 63:c * 64 + 64]
                    G1, G2, G3, G4 = P.psum(), P.psum(), P.psum(), P.psum()
                    for h2 in range(2):
                        pb = slice(64 * h2, 64 * h2 + 64)
                        P.mm(G1[pb, 0:128], bett[pb, hp, cs], KR[pb, hp, c, :])
                        P.mm(G2[pb, 0:128], ktil[pb, hp, cs], KR[pb, hp, c, :])
                        P.mm(G3[pb, 0:64], KR[pb, hp, c, 0:64], bett[pb, hp, cs])
                    P.ts("dve", DG[i].v(), ident, gam, ALU.mult)
                    for h2 in range(2):
                        pb = slice(64 * h2, 64 * h2 + 64)
                        P.mm(G4[pb, 0:64], vb[pb, hp, cs], identx[pb, pb])
                        P.mm(G4[pb, 64:128], bett[pb, hp, cs], DG[i][pb, pb])
                        P.mm(G4[pb, 128:192], ktil[pb, hp, cs], DG[i][pb, pb])
                    cur, curT, nxt, nxtT = PW[i]
                    for h2 in range(2):
                        pb = slice(64 * h2, 64 * h2 + 64)
                        P.tt("dve", cur[pb, pb], G1[pb, 0:64], tri[pb, 0:64], ALU.mult)
                        P.tt("dve", curT[pb, pb], G3[pb, 0:64], tri[pb, 128:192], ALU.mult)
                    P.tt("dve", Qm[i].v(), ident, cur.v(), ALU.subtract)
                    P.tt("dve", ABR[i].v(), G1[:, 64:128], tri[:, 64:128], ALU.mult)
                    P.tt("dve", AK[i].v(), G2[:, 0:128], tri[:, 0:128], ALU.mult)
                    P.copy("act", TK[i].v(), G4[:, 0:192])
                    ctxs.append(dict(i=i, hp=hp, gam=gam, pw=[cur, curT, nxt, nxtT]))

            def chain(k):
                def f():
                    for cx in ctxs:
                        cur, curT, nxt, nxtT = cx["pw"]
                        if k < 4:
                            pA = P.psum()
                            P.mm(pA[:, 0:128], curT.v(), cur.v())
                            P.copy("act", nxt.v(), pA[:, 0:128])
                        pB = P.psum()
                        P.mm(pB[:, 0:128], cur.v(), curT.v())
                        P.copy("act", nxtT.v(), pB[:, 0:128])
                    for cx in ctxs:
                        cur, curT, nxt, nxtT = cx["pw"]
                        i = cx["i"]
                        pQ = P.psum()
                        P.mm(pQ[:, 0:128], nxtT.v(), Qm[i].v())
                        P.tt("dve", Qm[i].v(), Qm[i].v(), pQ[:, 0:128], ALU.add)
                        cx["pw"] = [nxt, nxtT, cur, curT]
                return f

            def s1():
                for cx in ctxs:
                    i, hp = cx["i"], cx["hp"]
                    WT = P.psum()
                    for h2 in range(2):
                        pb = slice(64 * h2, 64 * h2 + 64)
                        P.mm(WT[pb, 0:64], KR[pb, hp, c, 0:64], STb[hp][pb, :], start=True, stop=False)
                        P.mm(WT[pb, 0:64], AK[i][pb, 0:64], TK[i][pb, 0:64], start=False, stop=True)
                    P.ts("dve", nWT[i].v(), WT[:, 0:64], -1.0, ALU.mult)

            def s2():
                for cx in ctxs:
                    i, hp = cx["i"], cx["hp"]
                    UTp = P.psum()
                    P.mm(UTp[:, 0:64], Qm[i].v(), nWT[i].v())
                    P.copy("act", UT[i].v(), UTp[:, 0:64])

            def s3():
                for cx in ctxs:
                    i, hp, gam = cx["i"], cx["hp"], cx["gam"]
                    Yp, SN = P.psum(), P.psum()
                    for h2 in range(2):
                        pb = slice(64 * h2, 64 * h2 + 64)
                        P.mm(SN[pb, 0:64], TK[i][pb, 64:128], UT[i][pb, :], start=True, stop=False)
                        P.mm(SN[pb, 0:64], TK[i][pb, 128:192], TK[i][pb, 0:64], start=False, stop=True)
                        P.mm(Yp[pb, 0:64], STb[hp][pb, :], KR[pb, hp, c, 64:128], start=True, stop=False)
                        P.mm(Yp[pb, 0:64], UT[i][pb, :], ABR[i][pb, :], start=False, stop=False)
                        P.mm(Yp[pb, 0:64], TK[i][pb, 0:64], AK[i][pb, 64:128], start=False, stop=True)
                    P.stt(ST[hp].v(), ST[hp].v(), gam, SN[:, 0:64], ALU.mult, ALU.add)
                    P.copy("pool", STb[hp].v(), ST[hp].v())
                    P.copy("act", yT[:, hp, cs], Yp[:, 0:64])

            return [front] + [chain(k) for k in range(5)], [s1, s2, s3]

        chunks = [make_chunk(c) for c in range(8)]
        for f in chunks[0][0]:
            f()
        for c in range(8):
            Aq = list(chunks[c + 1][0]) if c + 1 < 8 else []
            Bq = list(chunks[c][1])
            order = ["A", "B", "A", "A", "B", "A", "A", "B", "A"]
            for o in order:
                if o == "A" and Aq:
                    Aq.pop(0)()
                elif o == "B" and Bq:
                    Bq.pop(0)()
            for f in Aq + Bq:
                f()
        for hp in range(2):
            ps = P.psum()
            P.mm(ps.v(), onesblk, yT[:, hp, :])
            P.stt(yc[:, hp, :], ps.v(), -1.0 / 64, yT[:, hp, :], ALU.mult, ALU.add)
        P.act(sqk.v(), yc.v(), AF.Square)
        for hp in range(2):
            ps = P.psum()
            P.mm(ps.v(), onesblk, sqk[:, hp, :])
            rsqrt_into(rn[:, hp, :], ps.v(), 1.0 / 64, 64e-5)
        P.tt("dve", yc.v(), yc.v(), rn.v(), ALU.mult)
        P.tt("pool", tmpb.v(), r_, kmod.v(), ALU.mult)
        for hp in range(2):
            P.ts("dve", yc[:, hp, :], yc[:, hp, :], col("lng", hp), ALU.mult, col("lnb", hp), ALU.add)
            P.ts("pool", tmpb[:, hp, :], tmpb[:, hp, :], col("rk", hp), ALU.mult)
            ps = P.psum()
            P.mm(ps.v(), onesblk, tmpb[:, hp, :])
            P.tt("dve", t1[:, hp, :], ps.v(), zs[:, 4 + hp, :], ALU.mult)
        P.tt("dve", yc.v(), yc.v(), t1.v(), ALU.add)
        for hp in range(2):
            sg_ = stage[hp]
            P.tt("dve", sg_.v(), yc[:, hp, :], g_[:, hp, :], ALU.mult)
            P.store("sp", mixT_d[hp * 128:(hp + 1) * 128, t0:t0 + 512], sg_.v(), [mixR[hp][blk]])


def attention(P, A, NB, Kt, Qt, krows, Vt, ident_bf, amask, ones, ycol, og_name, col, mix_row0, mixT_d, mixR,
              rsqrt_into, evac):
    PT = [A.alloc([128, 512], BF16, "PT%d" % i) for i in range(4)]
    Osb = [A.alloc([65, 512], F32, "Osb%d" % i) for i in range(2)]
    YF = A.alloc([64, 4, 512], F32, "YF")
    sqy = A.alloc([64, 4, 512], F32, "sqy")
    rst = A.alloc([64, 512], F32, "rst")
    stg = [A.alloc([64, 512], F32, "astg%d" % i) for i in range(2)]
    items = [(qb, h, kc) for qb in range(NB) for h in range(4) for kc in range(4 * (qb + 1))]
    LOOK = 2
    state = {"pi": 0, "O": None}

    def issue(item):
        qb, h, kc = item
        qs = slice(qb * 512, (qb + 1) * 512)
        S = P.psum()
        diag = kc >= 4 * qb
        P.mm(S.v(), Kt[0:krows, h, kc * 128:(kc + 1) * 128], Qt[0:krows, h, qs], start=True, stop=not diag)
        if diag:
            P.mm(S.v(), ident_bf.v(), amask[:, kc - 4 * qb, :], start=False, stop=True)
        return S

    def finish(item, S):
        qb, h, kc = item
        qs = slice(qb * 512, (qb + 1) * 512)
        nk = 4 * (qb + 1)
        if kc == 0:
            state["O"] = P.psum_acc()
        O = state["O"]
        pt = PT[state["pi"] % 4]
        state["pi"] += 1
        P.act(pt.v(), S.v(), AF.Exp)
        P.mm(O[0:65, :], Vt[:, kc, h, :], pt.v(), start=(kc == 0), stop=(kc == nk - 1))
        if kc != nk - 1:
            return
        ob = Osb[h % 2]
        P.copy("act", ob.v(), O[0:65, :])
        P.recip(ob[64:65, :], ob[64:65, :])
        bc = P.psum()
        P.mm(bc[0:64, :], ones[64:65, 0:64], ob[64:65, :])
        P.tt("dve", YF[:, h, :], ob[0:64, :], bc[0:64, :], ALU.mult)
        if h != 3:
            return
        P.act(sqy.v(), YF.v(), AF.Square)
        ps = P.psum()
        for hh in range(4):
            P.mm(ps[0:64, :], ones[0:64, 0:64], sqy[:, hh, :], start=(hh == 0), stop=(hh == 3))
        rsqrt_into(rst.v(), ps[0:64, :], 1.0 / 256, EPS)
        for hh in range(4):
            sg_ = stg[hh % 2]
            P.stt(sg_.v(), YF[:, hh, :], col(og_name, hh, 0, 64), rst.v(), ALU.mult, ALU.mult)
            r0 = mix_row0 + 64 * hh
            P.store("sp", mixT_d[r0:r0 + 64, qs], sg_.v(), [mixR[r0 // 128][qb]])

    pend = []
    for item in items:
        pend.append((item, issue(item)))
        if len(pend) > LOOK:
            finish(*pend.pop(0))
    while pend:
        finish(*pend.pop(0))


def phase_fox(P, A, l, NB, T, col, zT_d, zR, mixT_d, mixR, Vfox, onesblk, ones, ident_bf, amask, place, auxm,
              ones512, rsqrt_into, evac):
    Qp = A.alloc([128, 4, T], BF16, "Qp")
    Kp = A.alloc([128, 4, T], BF16, "Kp")
    P.memset("pool", Qp.v(), 0.0)
    P.memset("pool", Kp.v(), 0.0)
    m = A.mark()
    zq = [A.alloc([128, 4, 512], F32, "zq0")] * 2
    fz = [A.alloc([4, 512], F32, "fz%d" % i) for i in range(2)]
    sq = A.alloc([128, 4, 512], F32, "fsq")
    rb = A.alloc([128, 4, 512], F32, "frb")
    lf = A.alloc([4, 512], F32, "lf")
    C6 = [[A.alloc([128, 512], F32, "C6_%d_%d" % (h, i)) for i in range(2)] for h in range(4)]
    P6 = [A.alloc([128, 512], F32, "P6s%d" % h) for h in range(4)]
    Hb, Mb, Lb = [[A.alloc([128, 512], BF16, "%s%d" % (n, h)) for h in range(4)] for n in ("Hb", "Mb", "Lb")]
    r1, r2, tq = [[A.alloc([128, 512], F32, "%s%d" % (n, h)) for h in range(4)] for n in ("r1", "r2", "tq")]
    for blk in range(NB):
        t0 = blk * 512
        ts_ = slice(t0, t0 + 512)
        z = zq[blk % 2]
        f_ = fz[blk % 2]
        P.load("sp", z.v(), zT_d[7 * 128:11 * 128, ts_].rearrange("(c p) n -> p c n", p=128),
               [zR[c][blk] for c in range(7, 11)])
        P.load("sp", f_.v(), zT_d[13 * 128:13 * 128 + 4, ts_], [zR[13][blk]])
        P.act(sq.v(), z.v(), AF.Square)
        for c4 in range(4):
            ps = P.psum()
            P.mm(ps.v(), onesblk, sq[:, c4, :])
            if c4 < 2:
                rsqrt_into(rb[:, c4, :], ps.v(), 1.0, 64 * EPS)
            else:
                rsqrt_into(rb[:, c4, :], ps.v(), 1.0 / 64, EPS)
        for c4 in range(4):
            for h2 in range(2):
                pb = slice(64 * h2, 64 * h2 + 64)
                h = 2 * (c4 % 2) + h2
                dst = (Qp if c4 < 2 else Kp)[pb, h, ts_]
                g = col("foxqg" if c4 < 2 else "foxkg", 0, 64 * h2, 64 * h2 + 64)
                P.stt(dst, z[pb, c4, :], g, rb[pb, c4, :], ALU.mult, ALU.mult)
        P.act(lf.v(), f_.v(), AF.Sigmoid, bias=col("foxfb", 0, 0, 4))
        P.act(lf.v(), lf.v(), AF.Ln)
        ars = [slice(64 if h % 2 == 0 else 0, (64 if h % 2 == 0 else 0) + 32) for h in range(4)]
        for h in range(4):
            ar = ars[h]
            ps = P.psum()
            P.mm(ps.v(), place[0:4, h, :], lf.v())
            P.copy("act", P6[h][ar, :], ps[ar, :])
        for h in range(4):
            ar = ars[h]
            init = 0.0 if blk == 0 else C6[h][(blk - 1) % 2][ar, 511:512]
            P.scan(C6[h][blk % 2][ar, :], ones512[ar, :], P6[h][ar, :], init)
        for h in range(4):
            P.copy("act", Hb[h][ars[h], :], C6[h][blk % 2][ars[h], :])
        for h in range(4):
            P.tt("pool", r1[h][ars[h], :], C6[h][blk % 2][ars[h], :], Hb[h][ars[h], :], ALU.subtract)
        for h in range(4):
            P.copy("act", Mb[h][ars[h], :], r1[h][ars[h], :])
        for h in range(4):
            P.tt("pool", r2[h][ars[h], :], r1[h][ars[h], :], Mb[h][ars[h], :], ALU.subtract)
        for h in range(4):
            P.copy("act", Lb[h][ars[h], :], r2[h][ars[h], :])
        for h in range(4):
            ar, par = ars[h], h % 2
            P.ts("dve", tq[h][ar, :], Hb[h][ar, :], auxm[ar, par, 0:1], ALU.mult)
            P.stt(tq[h][ar, :], Mb[h][ar, :], auxm[ar, par, 1:2], tq[h][ar, :], ALU.mult, ALU.add)
        for h in range(4):
            ar, par = ars[h], h % 2
            P.stt(tq[h][ar, :], Lb[h][ar, :], auxm[ar, par, 2:3], tq[h][ar, :], ALU.mult, ALU.add)
        for h in range(4):
            ar, par = ars[h], h % 2
            P.ts("dve", Qp[ar, h, ts_], tq[h][ar, :], auxm[ar, par, 3:4], ALU.mult, auxm[ar, par, 4:5], ALU.add)
            P.ts("pool", Kp[ar, h, ts_], tq[h][ar, :], auxm[ar, par, 5:6], ALU.mult, auxm[ar, par, 6:7], ALU.add)
    A.release(m)
    attention(P, A, NB, Kp, Qp, 128, Vfox, ident_bf, amask, ones, None, "og_fox", col, 256, mixT_d, mixR,
              rsqrt_into, evac)


def range_reduce(P, out, ang, ki, kf):
    P.ts("dve", ki, ang, 1.0 / (2 * math.pi), ALU.mult)
    P.copy("dve", kf, ki)
    P.stt(out, kf, -6.28125, ang, ALU.mult, ALU.add)
    P.stt(out, kf, -(2 * math.pi - 6.28125), out, ALU.mult, ALU.add)
    P.ts("dve", kf, out, math.pi, ALU.is_gt)
    P.stt(out, kf, -2 * math.pi, out, ALU.mult, ALU.add)
    P.ts("dve", kf, out, -math.pi, ALU.is_lt)
    P.stt(out, kf, 2 * math.pi, out, ALU.mult, ALU.add)
    P.ts("dve", out, out, math.pi, ALU.min, -math.pi, ALU.max)


def phase_s5(P, A, l, NB, col, cols, zT_d, zR, mixT_d, mixR, ssmB_d, ssmC_d, glu_d, ones, iota, rsqrt_into, evac):
    SB = A.alloc([128, 16, 128], S5_CD, "ssmB")
    SC = A.alloc([128, 16, 128], S5_CD, "ssmC")
    GL = A.alloc([128, 2, 128], S5_CD, "glu")
    P.load("pool", SB.v(), ssmB_d[l, :, :, :])
    P.load("pool", SC.v(), ssmC_d[l, :, :, :])
    P.load("pool", GL.v(), glu_d[l, :, :, :])
    c_re, c_im, c_dt = COLS["slre"], COLS["slim"], COLS["sldt"]

    def s8(name):
        return A.alloc([128, 8], F32, name)

    dt, lr, lrdt, th, rho, sn, cs, abr1, abi, den, fr, fi, t8, u8 = [s8(n) for n in (
        "dt", "lr", "lrdt", "th", "rho", "sn", "cs", "abr1", "abi", "den", "fr", "fi", "t8", "u8")]
    ki8 = A.alloc([128, 8], I32, "ki8")
    kf8 = s8("kf8")
    li = cols[:, c_im:c_im + 8]
    P.act(dt.v(), cols[:, c_dt:c_dt + 8], AF.Exp)
    P.ts("dve", lr.v(), cols[:, c_re:c_re + 8], -1e-4, ALU.min)
    P.tt("dve", lrdt.v(), lr.v(), dt.v(), ALU.mult)
    P.tt("dve", th.v(), li, dt.v(), ALU.mult)
    P.act(rho.v(), lrdt.v(), AF.Exp)
    range_reduce(P, t8.v(), th.v(), ki8.v(), kf8.v())
    P.act(sn.v(), t8.v(), AF.Sin)
    P.ts("dve", u8.v(), th.v(), math.pi / 2, ALU.add)
    range_reduce(P, t8.v(), u8.v(), ki8.v(), kf8.v())
    P.act(cs.v(), t8.v(), AF.Sin)
    P.tt("dve", abr1.v(), rho.v(), cs.v(), ALU.mult)
    P.ts("dve", abr1.v(), abr1.v(), -1.0, ALU.add)
    P.tt("dve", abi.v(), rho.v(), sn.v(), ALU.mult)
    P.tt("dve", den.v(), lr.v(), lr.v(), ALU.mult)
    P.tt("dve", t8.v(), li, li, ALU.mult)
    P.tt("dve", den.v(), den.v(), t8.v(), ALU.add)
    P.recip(den.v(), den.v())
    P.tt("dve", fr.v(), abr1.v(), lr.v(), ALU.mult)
    P.tt("dve", t8.v(), abi.v(), li, ALU.mult)
    P.tt("dve", fr.v(), fr.v(), t8.v(), ALU.add)
    P.tt("dve", fr.v(), fr.v(), den.v(), ALU.mult)
    P.tt("dve", fi.v(), abi.v(), lr.v(), ALU.mult)
    P.tt("dve", t8.v(), abr1.v(), li, ALU.mult)
    P.tt("dve", fi.v(), fi.v(), t8.v(), ALU.subtract)
    P.tt("dve", fi.v(), fi.v(), den.v(), ALU.mult)
    ANG = A.alloc([128, 8, 512], F32, "ANG")
    RED = A.alloc([128, 8, 512], F32, "RED")
    KF = A.alloc([128, 8, 512], F32, "KF")
    SINT = A.alloc([128, 8, 512], F32, "SINT")
    COST = A.alloc([128, 8, 512], F32, "COST")
    m_ki = A.mark()
    KI = A.alloc([128, 8, 512], I32, "KI")
    for i in range(8):
        P.ts("dve", ANG[:, i, :], iota, th[:, i:i + 1], ALU.mult)
    range_reduce(P, RED.v(), ANG.v(), KI.v(), KF.v())
    P.act(SINT.v(), RED.v(), AF.Sin)
    P.ts("dve", ANG.v(), ANG.v(), math.pi / 2, ALU.add)
    range_reduce(P, RED.v(), ANG.v(), KI.v(), KF.v())
    P.act(COST.v(), RED.v(), AF.Sin)
    TinR, TinI, RHO = ANG, RED, KF
    for i in range(8):
        P.ts("dve", TinR[:, i, :], COST[:, i, :], fr[:, i:i + 1], ALU.mult)
        P.stt(TinR[:, i, :], SINT[:, i, :], fi[:, i:i + 1], TinR[:, i, :], ALU.mult, ALU.add)
        P.ts("dve", TinI[:, i, :], SINT[:, i, :], fr[:, i:i + 1], ALU.mult)
        P.stt(TinI[:, i, :], COST[:, i, :], fi[:, i:i + 1], TinI[:, i, :], ALU.mult, ALU.subtract)
        P.ts("dve", RHO[:, i, :], iota, 0.0, ALU.mult, rho[:, i:i + 1], ALU.add)
    P.barrier()
    A.release(m_ki)
    car_r = [s8("car_r0"), s8("car_r1")]
    car_i = [s8("car_i0"), s8("car_i1")]
    ub = [A.alloc([128, 2, 512], S5_CD, "ub%d" % i) for i in range(2)]

    def w5(name, dt=F32):
        return A.alloc([128, 512], dt, name)

    NS = 3
    ta, tb, tc, td, wr, wi, zr, zi_ = [[w5("%s%d" % (n, i)) for i in range(NS)] for n in (
        "ta", "tb", "tc", "td", "wr", "wi", "zr", "zi")]
    xr, xin = [[w5("%s%d" % (n, i), S5_CD) for i in range(NS)] for n in ("xr", "xin")]
    YS = A.alloc([128, 2, 512], F32, "YS")
    yp, sqs = w5("yp"), A.alloc([128, 2, 512], F32, "sqs")
    yg = w5("yg", S5_CD)
    rst = w5("rst")
    stg = [w5("sstg0"), w5("sstg1")]
    n = 0
    for blk in range(NB):
        t0 = blk * 512
        ts_ = slice(t0, t0 + 512)
        u = ub[blk % 2]
        P.load("pool", u.v(), zT_d[14 * 128:16 * 128, ts_].rearrange("(c p) n -> p c n", p=128),
               [zR[14][blk], zR[15][blk]])
        cr_o, ci_o = car_r[(blk + 1) % 2], car_i[(blk + 1) % 2]
        cr_n, ci_n = car_r[blk % 2], car_i[blk % 2]
        Ys = [P.psum_acc(), P.psum_acc()]

        def stA(i, j):
            chn = i // 4
            PR, PI = P.psum(), P.psum()
            P.mm(PR.v(), SB[:, i, :], u[:, chn, :])
            P.mm(PI.v(), SB[:, 8 + i, :], u[:, chn, :])
            P.tt("dve", ta[j].v(), PR.v(), TinR[:, i, :], ALU.mult)
            P.tt("dve", tb[j].v(), PI.v(), TinI[:, i, :], ALU.mult)
            P.tt("pool", wr[j].v(), ta[j].v(), tb[j].v(), ALU.subtract)
            P.tt("dve", tc[j].v(), PR.v(), TinI[:, i, :], ALU.mult)
            P.tt("dve", td[j].v(), PI.v(), TinR[:, i, :], ALU.mult)
            P.tt("pool", wi[j].v(), tc[j].v(), td[j].v(), ALU.add)

        def stB(i, j):
            ir = 0.0 if blk == 0 else cr_o[:, i:i + 1]
            ii = 0.0 if blk == 0 else ci_o[:, i:i + 1]
            P.scan(zr[j].v(), RHO[:, i, :], wr[j].v(), ir)
            P.scan(zi_[j].v(), RHO[:, i, :], wi[j].v(), ii)
            P.tt("pool", ta[j].v(), zr[j].v(), COST[:, i, :], ALU.mult)
            P.tt("pool", tb[j].v(), zi_[j].v(), SINT[:, i, :], ALU.mult)
            P.tt("pool", tc[j].v(), zr[j].v(), SINT[:, i, :], ALU.mult)
            P.tt("dve", td[j].v(), zi_[j].v(), COST[:, i, :], ALU.mult)

        def stC(i, j):
            chn, i4 = i // 4, i % 4
            Y = Ys[chn]
            P.tt("pool", xr[j].v(), ta[j].v(), tb[j].v(), ALU.subtract)
            P.tt("pool", cr_n[:, i:i + 1], ta[j][:, 511:512], tb[j][:, 511:512], ALU.subtract)
            P.stt(xin[j].v(), tc[j].v(), -1.0, td[j].v(), ALU.mult, ALU.subtract)
            P.tt("pool", ci_n[:, i:i + 1], tc[j][:, 511:512], td[j][:, 511:512], ALU.add)
            P.mm(Y.v(), SC[:, i, :], xr[j].v(), start=(i4 == 0), stop=False)
            P.mm(Y.v(), SC[:, 8 + i, :], xin[j].v(), start=False, stop=(i4 == 3))
            if i4 == 3:
                P.stt(yp.v(), u[:, chn, :], col("ssmd", chn), Y.v(), ALU.mult, ALU.add)
                P.act(yg.v(), yp.v(), AF.Gelu_apprx_tanh)
                ps = P.psum()
                P.mm(ps.v(), GL[:, chn, :], yg.v())
                P.act(yp.v(), ps.v(), AF.Sigmoid)
                P.tt("dve", YS[:, chn, :], yg.v(), yp.v(), ALU.mult)

        js = [(n + i) % NS for i in range(8)]
        n += 8
        for step in range(8 + 2):
            if step < 8:
                stA(step, js[step])
            if 0 <= step - 1 < 8:
                stB(step - 1, js[step - 1])
            if 0 <= step - 2 < 8:
                stC(step - 2, js[step - 2])
        P.act(sqs.v(), YS.v(), AF.Square)
        ps = P.psum()
        P.mm(ps.v(), ones, sqs[:, 0, :], start=True, stop=False)
        P.mm(ps.v(), ones, sqs[:, 1, :], start=False, stop=True)
        rsqrt_into(rst.v(), ps.v(), 1.0 / 256, EPS)
        for chn in range(2):
            P.stt(stg[chn].v(), YS[:, chn, :], col("og_ssm", chn), rst.v(), ALU.mult, ALU.mult)
            P.store("sp", mixT_d[(4 + chn) * 128:(5 + chn) * 128, ts_], stg[chn].v(), [mixR[4 + chn][blk]])


def phase_mla(P, A, l, NB, T, col, zT_d, zR, mixT_d, mixR, wq_d, wkvK_d, wkvV_d, pos_in, ones, ident_bf, amask,
              rotP_bf, selkr_bf, invfreq, rsqrt_into, evac):
    NCH = T // 128
    Qm = A.alloc([128, 4, T], BF16, "Qmla")
    Km = A.alloc([128, 4, T], BF16, "Kmla")
    Vm = A.alloc([128, NCH, 4, 65], BF16, "Vmla")
    P.memset("pool", Vm[:, :, :, 64:65], 1.0)
    m = A.mark()
    Wq = A.alloc([128, 2, 384], BF16, "Wq")
    WkK = A.alloc([128, 4, 96], BF16, "WkK")
    WkV = A.alloc([128, 256], BF16, "WkV")
    P.load("pool", Wq.v(), wq_d[l, :, :, :])
    P.load("pool", WkK.v(), wkvK_d[l, :, :, :])
    P.load("pool", WkV.v(), wkvV_d[l, :, :])

    def w5(name, dt=F32, p=128):
        return A.alloc([p, 512], dt, name)

    z16, z17, z18 = [[w5("%s_%d" % (n, i)) for i in range(2)] for n in ("z16", "z17", "z18")]
    z13b = [w5("z13b%d" % i, BF16, 64) for i in range(2)]
    cos_d, sin_d, ropeR = pos_in
    COSs = [w5("COS%d" % i, F32, 96) for i in range(2)]
    SINs = [w5("SIN%d" % i, F32, 96) for i in range(2)]
    sq16, sq17, sq18, rs192, rs128 = [w5(n) for n in ("sq16", "sq17", "sq18", "rs192", "rs128")]
    cqn = A.alloc([128, 2, 512], BF16, "cqn")
    ckvn = w5("ckvn", BF16)
    raw, sqr, rbq, qn, tc_ = [[w5("%s%d" % (n, i), F32, 96) for i in range(4)] for n in ("raw", "sqr", "rbq", "qn", "tc")]
    qnb = [w5("qnb%d" % i, BF16, 96) for i in range(4)]
    n = 0
    for blk in range(NB):
        t0 = blk * 512
        ts_ = slice(t0, t0 + 512)
        j2 = blk % 2
        P.load("sp", z16[j2].v(), zT_d[16 * 128:17 * 128, ts_], [zR[16][blk]])
        P.load("sp", z17[j2][0:64, :], zT_d[17 * 128:17 * 128 + 64, ts_], [zR[17][blk]])
        P.load("sp", z18[j2].v(), zT_d[18 * 128:19 * 128, ts_], [zR[18][blk]])
        P.load("pool", z13b[j2].v(), zT_d[13 * 128:13 * 128 + 64, ts_], [zR[13][blk]])
        COS, SIN = COSs[j2], SINs[j2]
        P.load("sp", COS.v(), cos_d[:, ts_], [ropeR[0][blk]])
        P.load("sp", SIN.v(), sin_d[:, ts_], [ropeR[1][blk]])
        P.act(sq16.v(), z16[j2].v(), AF.Square)
        P.act(sq17[0:64, :], z17[j2][0:64, :], AF.Square)
        P.act(sq18.v(), z18[j2].v(), AF.Square)
        ps = P.psum()
        P.mm(ps.v(), ones, sq16.v(), start=True, stop=False)
        P.mm(ps.v(), ones[0:64, :], sq17[0:64, :], start=False, stop=True)
        rsqrt_into(rs192.v(), ps.v(), 1.0 / 192, EPS)
        ps = P.psum()
        P.mm(ps.v(), ones, sq18.v())
        rsqrt_into(rs128.v(), ps.v(), 1.0 / 128, EPS)
        P.stt(cqn[:, 0, :], z16[j2].v(), col("mqlg", 0), rs192.v(), ALU.mult, ALU.mult)
        P.stt(cqn[0:64, 1, :], z17[j2][0:64, :], col("mqlg", 1, 0, 64), rs192[0:64, :], ALU.mult, ALU.mult)
        P.stt(ckvn.v(), z18[j2].v(), col("mkvlg", 0), rs128.v(), ALU.mult, ALU.mult)
        items = [(h, isk) for h in range(4) for isk in range(2)]
        for g0 in range(0, 8, 4):
            grp = items[g0:g0 + 4]
            for j, (h, isk) in enumerate(grp):
                ps = P.psum()
                if not isk:
                    P.mm(ps[0:96, :], Wq[:, 0, 96 * h:96 * h + 96], cqn[:, 0, :], start=True, stop=False)
                    P.mm(ps[0:96, :], Wq[0:64, 1, 96 * h:96 * h + 96], cqn[0:64, 1, :], start=False, stop=True)
                else:
                    P.mm(ps[0:96, :], WkK[:, h, :], ckvn.v(), start=True, stop=False)
                    P.mm(ps[0:96, :], selkr_bf.v(), z13b[j2].v(), start=False, stop=True)
                P.copy("act", raw[j].v(), ps[0:96, :])
                P.act(sqr[j].v(), ps[0:96, :], AF.Square)
            for j, (h, isk) in enumerate(grp):
                p2 = P.psum()
                P.mm(p2[0:96, :], ones[0:96, 0:96], sqr[j].v())
                if not isk:
                    P.act(rbq[j].v(), p2[0:96, :], AF.Sqrt, bias=96 * EPS, scale=1.0)
                else:
                    P.act(rbq[j].v(), p2[0:96, :], AF.Sqrt, bias=EPS, scale=1.0 / 96)
            for j, (h, isk) in enumerate(grp):
                P.recip(rbq[j].v(), rbq[j].v())
            for j, (h, isk) in enumerate(grp):
                P.stt(qn[j].v(), raw[j].v(), col("mkg" if isk else "mqg", 0, 0, 96), rbq[j].v(), ALU.mult, ALU.mult)
                P.copy("pool", qnb[j].v(), qn[j].v())
            for j, (h, isk) in enumerate(grp):
                p3 = P.psum()
                P.mm(p3[0:96, :], rotP_bf.v(), qnb[j].v())
                P.tt("pool", tc_[j].v(), qn[j].v(), COS.v(), ALU.mult)
                P.tt("dve", raw[j].v(), p3[0:96, :], SIN.v(), ALU.mult)
            for j, (h, isk) in enumerate(grp):
                P.tt("pool", (Km if isk else Qm)[0:96, h, ts_], tc_[j].v(), raw[j].v(), ALU.add)
        for tt in range(4):
            ps = P.psum()
            P.mm(ps[:, 0:256], ckvn[:, tt * 128:(tt + 1) * 128], WkV.v())
            evac(Vm[:, blk * 4 + tt, :, 0:64], ps[:, 0:256].rearrange("p (h d) -> p h d", h=4))
    A.release(m)
    attention(P, A, NB, Km, Qm, 96, Vm, ident_bf, amask, ones, None, "og_mla", col, 768, mixT_d, mixR,
              rsqrt_into, evac)


def host_prep(inputs, T):
    L = inputs["w_in"].shape[0]
    f = np.float32
    g = lambda k: np.asarray(inputs[k], dtype=f)
    w_in = g("w_in")
    wp = np.zeros((L, D, ZW), f)
    wp[:, :, 0:896] = w_in[:, :, 0:896]
    fo = 896
    wp[:, :, 7 * 128:13 * 128] = w_in[:, :, fo:fo + 768]
    wp[:, :, 13 * 128:13 * 128 + 4] = w_in[:, :, fo + 768:fo + 772]
    so = fo + 772
    wp[:, :, 14 * 128:16 * 128] = w_in[:, :, so:so + 256]
    mo = so + 256
    wp[:, :, 16 * 128:16 * 128 + 192] = w_in[:, :, mo:mo + 192]
    wp[:, :, 18 * 128:19 * 128] = w_in[:, :, mo + 192:mo + 320]
    wp[:, :, 13 * 128 + 32:13 * 128 + 64] = w_in[:, :, mo + 320:mo + 352]
    cols = np.zeros((L, 128, NC_COLS), f)

    def put(name, arr, width):
        a = arr.reshape(L, width, 128).transpose(0, 2, 1)
        cols[:, :, COLS[name]:COLS[name] + width] = a

    put("mixg", g("mix_norm_g"), 8)
    put("mlpg", g("mlp_norm_g"), 8)
    put("mu", g("rwkv_mu"), 7)
    put("w0", g("rwkv_w0"), 2)
    put("a0", g("rwkv_a0"), 2)
    put("kk", g("rwkv_k_k"), 2)
    put("ka", g("rwkv_k_a"), 2)
    put("lng", g("rwkv_ln_g"), 2)
    put("lnb", g("rwkv_ln_b"), 2)
    put("rk", g("rwkv_r_k").reshape(L, 256), 2)
    cols[:, :, COLS["foxqg"]] = np.tile(g("fox_q_g"), (1, 2))
    cols[:, :, COLS["foxkg"]] = np.tile(g("fox_k_g"), (1, 2))
    cols[:, 0:4, COLS["foxfb"]] = g("fox_f_b")
    put("ssmd", g("ssm_d"), 2)
    og = g("out_norm_g")
    cols[:, 0:64, COLS["og_fox"]:COLS["og_fox"] + 4] = og[:, 0].reshape(L, 4, 64).transpose(0, 2, 1)
    put("og_ssm", og[:, 1], 2)
    cols[:, 0:64, COLS["og_mla"]:COLS["og_mla"] + 4] = og[:, 2].reshape(L, 4, 64).transpose(0, 2, 1)
    ql = g("mla_q_latent_g")
    cols[:, :, COLS["mqlg"]] = ql[:, 0:128]
    cols[:, 0:64, COLS["mqlg"] + 1] = ql[:, 128:192]
    cols[:, :, COLS["mkvlg"]] = g("mla_kv_latent_g")
    cols[:, 0:96, COLS["mqg"]] = g("mla_q_g")
    cols[:, 0:96, COLS["mkg"]] = g("mla_k_g")
    for nm, arr in (("slre", g("ssm_lambda_re")), ("slim", g("ssm_lambda_im")),
                    ("sldt", np.repeat(g("ssm_log_dt")[:, :, None], 64, axis=2))):
        cols[:, :, COLS[nm]:COLS[nm] + 8] = arr.reshape(L, 8, 128).transpose(0, 2, 1)
    lr_w = np.concatenate([g("rwkv_w_up"), g("rwkv_a_up"), g("rwkv_g_up")], axis=1)
    bre, bim, cre, cim = g("ssm_b_re"), g("ssm_b_im"), g("ssm_c_re"), g("ssm_c_im")
    ssmB = np.zeros((L, 128, 16, 128), f)
    ssmC = np.zeros((L, 128, 16, 128), f)
    for gi in range(16):
        i, gg = gi // 2, gi % 2
        chn, i4 = i // 4, i % 4
        rows = slice(32 * i4 + 16 * gg, 32 * i4 + 16 * gg + 16)
        stc = slice(64 * gg, 64 * gg + 64)
        ssmB[:, rows, i, stc] = bre[:, gi].transpose(0, 2, 1)
        ssmB[:, rows, 8 + i, stc] = bim[:, gi].transpose(0, 2, 1)
        oc = slice((gi * 16) % 128, (gi * 16) % 128 + 16)
        ssmC[:, stc, i, oc] = cre[:, gi].transpose(0, 2, 1)
        ssmC[:, stc, 8 + i, oc] = cim[:, gi].transpose(0, 2, 1)
    glu = g("ssm_glu_w")
    glu_bd = np.zeros((L, 128, 2, 128), f)
    for gi in range(16):
        chn, o = gi // 8, (gi % 8) * 16
        glu_bd[:, o:o + 16, chn, o:o + 16] = glu[:, gi]
    wq = g("mla_w_q_up")
    mla_wq = np.zeros((L, 128, 2, 384), f)
    mla_wq[:, :, 0, :] = wq[:, 0:128]
    mla_wq[:, 0:64, 1, :] = wq[:, 128:192]
    wkv = g("mla_w_kv_up").reshape(L, 128, 4, 128)
    mla_wkvK = np.zeros((L, 128, 4, 96), f)
    mla_wkvK[:, :, :, 0:64] = wkv[:, :, :, 0:64]
    mla_wkvV = np.ascontiguousarray(wkv[:, :, :, 64:128]).reshape(L, 128, 256)
    cst = np.zeros((128, 2048), f)
    cst[:, 0:128] = np.eye(128)
    cst[0:64, 128:192] = 1
    cst[64:128, 192:256] = 1
    cst[:, 256:384] = 1
    s = (np.arange(128) % 64)[:, None]
    t = np.arange(64)[None, :]
    cst[:, 384:448] = (s < t)
    cst[:, 448:512] = (s <= t)
    cst[:, 512:576] = (s > t)
    sm = np.ones((128, 512), f)
    sm[:, 0::64] = 0
    cst[:, 576:1088] = sm
    cst[:, 1088:1600] = np.arange(1, 513)[None, :]
    rot = np.zeros((96, 96), f)
    for i in range(16):
        rot[80 + i, 64 + i] = -1
        rot[64 + i, 80 + i] = 1
    cst[0:96, 1600:1696] = rot
    sel = np.zeros((64, 96), f)
    for i in range(32):
        sel[32 + i, 64 + i] = 1
    cst[0:64, 1696:1792] = sel
    inv = (10000.0 ** (-np.arange(0, 32, 2, dtype=np.float32) / 32)).astype(f)
    cst[64:80, 1792] = inv
    cst[80:96, 1792] = inv
    cst[:, 1800:1928] = 1
    k = np.arange(128)[:, None]
    q = np.arange(512)[None, :]
    amask = np.zeros((128, 4, 512), f)
    for j in range(4):
        amask[:, j, :] = np.where(q >= k + 128 * j, 0.0, NEG)
    place = np.zeros((4, 4, 128), f)
    auxm = np.zeros((128, 2, 8), f)
    for h in range(4):
        a0 = 64 if h % 2 == 0 else 0
        place[h, h, a0:a0 + 6] = 1
    for par in range(2):
        a0 = 64 if par == 0 else 0
        for r in range(3):
            auxm[a0 + r, par, r] = 1
            auxm[a0 + 3 + r, par, r] = 1
            auxm[a0 + r, par, 3] = 1
            auxm[a0 + 3 + r, par, 4] = 1
            auxm[a0 + 3 + r, par, 5] = -1
            auxm[a0 + r, par, 6] = 1
    shared = dict(w_in_p=wp, w_out=g("w_out"), w_ff1=g("w_ff1"), w_ff2=g("w_ff2"), cols=cols, lr_w=lr_w, ssmB=ssmB,
                  ssmC=ssmC, glu_bd=glu_bd, mla_wq=mla_wq, mla_wkvK=mla_wkvK, mla_wkvV=mla_wkvV, cst=cst, amask=amask,
                  place=place, auxm=auxm)
    return shared


_NC_CACHE = {}


def kernel(**inputs):
    x = np.asarray(inputs["x"], dtype=np.float32)
    B, T, _ = x.shape
    L = inputs["w_in"].shape[0]
    shared = host_prep(inputs, T)
    key = (T, L)
    if key not in _NC_CACHE:
        _NC_CACHE[key] = build(T, L)
    nc = _NC_CACHE[key]
    pos = np.asarray(inputs["positions"], dtype=np.int32)
    in_maps = []
    for b in range(B):
        m = dict(shared)
        m["xT"] = np.ascontiguousarray(x[b].T)
        m["pos"] = np.ascontiguousarray(pos[b:b + 1])
        in_maps.append(m)
    res = run_bass_kernel_spmd(nc, in_maps, core_ids=list(range(B)))
    out = np.stack([np.asarray(r["outT"]).T for r in res.results], axis=0)
    return np.ascontiguousarray(out.astype(np.float32))
```
